# Optimizing a Trainium2 kernel written in Bass

```python
import math
import jax, jax.numpy as jnp
from jax import lax
import numpy as np

D_MODEL = 1024
BATCH = 4
SEQ = 8192
DEPTH = 2

N_MIXERS = 2
CONV_WIDTH = 31
N_HEADS = 8
HEAD_DIM = 64
V_DIM = 2 * HEAD_DIM
D_FF = 4 * D_MODEL
REL_BUCKETS = 32
REL_MAX_DIST = 128
Q_BLOCK = 128
ALPHA = (2 * DEPTH) ** 0.25
BETA = (8 * DEPTH) ** -0.25
LN_EPS = 1e-5
N_CONV = (DEPTH + 1) // 2
N_ATTN = DEPTH // 2

kernel_name = 'hybrid_conformer_conv_diff_attn_deepnorm_adaln'


def layer_norm(x, g, b):
    xf = x.astype(jnp.float32)
    mu = jnp.mean(xf, axis=-1, keepdims=True)
    var = jnp.mean(jnp.square(xf - mu), axis=-1, keepdims=True)
    return ((xf - mu) * lax.rsqrt(var + LN_EPS)).astype(x.dtype) * g + b


def rms_norm(x, g):
    xf = x.astype(jnp.float32)
    return (xf * lax.rsqrt(jnp.mean(jnp.square(xf), axis=-1, keepdims=True) + LN_EPS)).astype(x.dtype) * g


def ada_mod(c, w, b):
    m = (jax.nn.silu(c) @ w + b)[:, None, :]
    shift, scale, gate = jnp.split(m, 3, axis=-1)
    return shift, scale, gate


def conv_module(h, w_pw1, b_pw1, w_dw, b_dw, g_cn, b_cn, w_pw2, b_pw2):
    u = jax.nn.glu(h @ w_pw1 + b_pw1, axis=-1)
    u = jnp.pad(u, ((0, 0), (CONV_WIDTH - 1, 0), (0, 0)))
    u = lax.conv_general_dilated(
        u, w_dw[:, None, :], window_strides=(1,), padding='VALID',
        dimension_numbers=('NWC', 'WIO', 'NWC'),
        feature_group_count=D_MODEL) + b_dw
    u = jax.nn.silu(layer_norm(u, g_cn, b_cn))
    return u @ w_pw2 + b_pw2


def t5_bucket(rel):
    n = jnp.maximum(rel, 0)
    max_exact = REL_BUCKETS // 2
    nf = jnp.maximum(n, 1).astype(jnp.float32)
    large = max_exact + (jnp.log(nf / max_exact) / math.log(REL_MAX_DIST / max_exact)
                         * (REL_BUCKETS - max_exact)).astype(jnp.int32)
    large = jnp.minimum(large, REL_BUCKETS - 1)
    return jnp.where(n < max_exact, n, large)


def diff_attention(h, w_qkv, lam_q1, lam_k1, lam_q2, lam_k2, g_sub, w_o, rel_bias, lambda_init):
    B, S, _ = h.shape
    q, k, v = jnp.split(h @ w_qkv, 3, axis=-1)
    q = q.reshape(B, S, N_HEADS, 2, HEAD_DIM) * (HEAD_DIM ** -0.5)
    k = k.reshape(B, S, N_HEADS, 2, HEAD_DIM)
    v = v.reshape(B, S, N_HEADS, V_DIM)
    f32 = jnp.float32
    lam = (jnp.exp(jnp.sum(lam_q1.astype(f32) * lam_k1.astype(f32)))
           - jnp.exp(jnp.sum(lam_q2.astype(f32) * lam_k2.astype(f32))) + lambda_init)
    n_blocks = S // Q_BLOCK
    qb = jnp.moveaxis(q.reshape(B, n_blocks, Q_BLOCK, N_HEADS, 2, HEAD_DIM), 1, 0)
    k_pos = jnp.arange(S)

    def block(args):
        q_blk, blk_idx = args
        q_pos = blk_idx * Q_BLOCK + jnp.arange(Q_BLOCK)
        rel = q_pos[:, None] - k_pos[None, :]
        bias = jnp.moveaxis(rel_bias[t5_bucket(rel)], -1, 0).astype(f32)
        logits = jnp.einsum('bqhmd,bkhmd->bhmqk', q_blk, k).astype(f32) + bias[None, :, None]
        logits = jnp.where(rel[None, None, None] >= 0, logits, -jnp.inf)
        p = jax.nn.softmax(logits, axis=-1)
        attn = p[:, :, 0] - lam * p[:, :, 1]
        return jnp.einsum('bhqk,bkhe->bqhe', attn.astype(v.dtype), v)

    out = lax.map(block, (qb, jnp.arange(n_blocks)))
    out = jnp.moveaxis(out, 0, 1).reshape(B, S, N_HEADS, V_DIM)
    out = rms_norm(out, g_sub) * (1.0 - lambda_init)
    return out.reshape(B, S, N_HEADS * V_DIM) @ w_o


def sq_relu_mlp(h, w1, w2):
    return jnp.square(jax.nn.relu(h @ w1)) @ w2


def _normal(key, shape, std):
    return std * jax.random.normal(key, shape, dtype=jnp.float32)


def setup_inputs(seed: int = 0) -> dict:
    key = jax.random.key(seed)
    ks = jax.random.split(key, 40)
    D = D_MODEL
    s_in = D ** -0.5

    def mod_bias(k1, k2, n):
        return jnp.concatenate([_normal(k1, (n, 2 * D), 0.01),
                                1.0 + _normal(k2, (n, D), 0.01)], axis=-1)

    qk_w = _normal(ks[14], (N_ATTN, D, 2 * N_HEADS * 2 * HEAD_DIM), s_in)
    v_w = _normal(ks[15], (N_ATTN, D, N_HEADS * V_DIM), BETA * s_in)
    return {
        'x': _normal(ks[0], (BATCH, SEQ, D), 1.0),
        'c': _normal(ks[1], (BATCH, D), 1.0),
        'conv_mod_w': _normal(ks[2], (N_CONV, D, 3 * D), 0.2 * s_in),
        'conv_mod_b': mod_bias(ks[3], ks[4], N_CONV),
        'conv_pw1_w': _normal(ks[5], (N_CONV, D, 2 * D), BETA * s_in),
        'conv_pw1_b': _normal(ks[6], (N_CONV, 2 * D), 0.01),
        'conv_dw_w': _normal(ks[7], (N_CONV, CONV_WIDTH, D), CONV_WIDTH ** -0.5),
        'conv_dw_b': _normal(ks[8], (N_CONV, D), 0.01),
        'conv_norm_g': 1.0 + _normal(ks[9], (N_CONV, D), 0.01),
        'conv_norm_b': _normal(ks[10], (N_CONV, D), 0.01),
        'conv_pw2_w': _normal(ks[11], (N_CONV, D, D), BETA * s_in),
        'conv_pw2_b': _normal(ks[12], (N_CONV, D), 0.01),
        'attn_mod_w': _normal(ks[13], (N_ATTN, D, 3 * D), 0.2 * s_in),
        'attn_mod_b': mod_bias(ks[16], ks[17], N_ATTN),
        'attn_qkv_w': jnp.concatenate([qk_w, v_w], axis=-1),
        'attn_lam_q1': _normal(ks[18], (N_ATTN, HEAD_DIM), 0.1),
        'attn_lam_k1': _normal(ks[19], (N_ATTN, HEAD_DIM), 0.1),
        'attn_lam_q2': _normal(ks[20], (N_ATTN, HEAD_DIM), 0.1),
        'attn_lam_k2': _normal(ks[21], (N_ATTN, HEAD_DIM), 0.1),
        'attn_subln_g': 1.0 + _normal(ks[22], (N_ATTN, V_DIM), 0.01),
        'attn_out_w': _normal(ks[23], (N_ATTN, N_HEADS * V_DIM, D), BETA * s_in),
        'rel_bias': _normal(ks[24], (REL_BUCKETS, N_HEADS), 0.5),
        'mlp_mod_w': _normal(ks[25], (DEPTH, D, 3 * D), 0.2 * s_in),
        'mlp_mod_b': mod_bias(ks[26], ks[27], DEPTH),
        'mlp_w1': _normal(ks[28], (DEPTH, D, D_FF), BETA * s_in),
        'mlp_w2': _normal(ks[29], (DEPTH, D_FF, D), BETA * D_FF ** -0.5),
        'post_mix_g': 1.0 + _normal(ks[30], (DEPTH, D), 0.01),
        'post_mix_b': _normal(ks[31], (DEPTH, D), 0.01),
        'post_mlp_g': 1.0 + _normal(ks[32], (DEPTH, D), 0.01),
        'post_mlp_b': _normal(ks[33], (DEPTH, D), 0.01),
    }


def reference(x, c, conv_mod_w, conv_mod_b, conv_pw1_w, conv_pw1_b, conv_dw_w, conv_dw_b,
              conv_norm_g, conv_norm_b, conv_pw2_w, conv_pw2_b, attn_mod_w, attn_mod_b,
              attn_qkv_w, attn_lam_q1, attn_lam_k1, attn_lam_q2, attn_lam_k2, attn_subln_g,
              attn_out_w, rel_bias, mlp_mod_w, mlp_mod_b, mlp_w1, mlp_w2,
              post_mix_g, post_mix_b, post_mlp_g, post_mlp_b):
    for i in range(DEPTH):
        j = i // N_MIXERS
        if i % N_MIXERS == 0:
            shift, scale, gate = ada_mod(c, conv_mod_w[j], conv_mod_b[j])
            y = conv_module(x * (1 + scale) + shift, conv_pw1_w[j], conv_pw1_b[j],
                            conv_dw_w[j], conv_dw_b[j], conv_norm_g[j], conv_norm_b[j],
                            conv_pw2_w[j], conv_pw2_b[j])
        else:
            lambda_init = 0.8 - 0.6 * math.exp(-0.3 * i)
            shift, scale, gate = ada_mod(c, attn_mod_w[j], attn_mod_b[j])
            y = diff_attention(x * (1 + scale) + shift, attn_qkv_w[j], attn_lam_q1[j],
                               attn_lam_k1[j], attn_lam_q2[j], attn_lam_k2[j],
                               attn_subln_g[j], attn_out_w[j], rel_bias, lambda_init)
        x = layer_norm(ALPHA * x + gate * y, post_mix_g[i], post_mix_b[i])
        shift, scale, gate = ada_mod(c, mlp_mod_w[i], mlp_mod_b[i])
        y = sq_relu_mlp(x * (1 + scale) + shift, mlp_w1[i], mlp_w2[i])
        x = layer_norm(ALPHA * x + gate * y, post_mlp_g[i], post_mlp_b[i])
    return x
```

```python
import math
from contextlib import ExitStack

import numpy as np
import concourse.bass as bass
import concourse.mybir as mybir
from concourse.bass_utils import run_bass_kernel_spmd

F32 = mybir.dt.float32
BF16 = mybir.dt.bfloat16
AF = mybir.ActivationFunctionType
ALU = mybir.AluOpType

D = 1024
SEQ = 8192
NB = 4
DFF = 4096
GR = 512
HALO = 32
TT = GR + HALO
NG = 8
NH = 8
ALPHA = 4.0 ** 0.25
LN_EPS = 1e-5
LAMBDA_INIT = 0.8 - 0.6 * math.exp(-0.3 * 1)
CONVW = 31
NEG = -30000.0

ENGS = ("pe", "act", "dve", "pool", "sp")


class Sched:
    def __init__(self, nc, stack, strict_same=True):
        self.nc = nc
        self.stack = stack
        self.sems = {}
        self.cnt = {}
        self.prog = {e: [] for e in ENGS}
        self.seen = {e: {} for e in ENGS}
        self.last_w = {}
        self.readers = {}
        self.strict_same = strict_same
        for e in ENGS:
            self._mk(e)

    def _mk(self, name):
        self.sems[name] = self.stack.enter_context(self.nc.semaphore("s_" + name))
        self.cnt[name] = 0

    def _deps(self, reads, writes):
        d = {}
        for k in reads:
            w = self.last_w.get(k)
            if w:
                d[w[0]] = max(d.get(w[0], 0), w[1])
        for k in writes:
            w = self.last_w.get(k)
            if w:
                d[w[0]] = max(d.get(w[0], 0), w[1])
            for e2, c in self.readers.get(k, {}).items():
                d[e2] = max(d.get(e2, 0), c)
        return d

    def _wait(self, eng, d):
        for src, c in d.items():
            if src == eng and (eng == "pe" or not self.strict_same):
                continue
            if self.seen[eng].get(src, 0) >= c:
                continue
            self.seen[eng][src] = c
            if c > self.cnt[src]:
                print("SCHED WARNING: %s waits on future signal of %s (%d > %d)" % (eng, src, c, self.cnt[src]))
            unit = 1 if src in ENGS else 16
            self.prog[eng].append(("wait", src, c * unit))

    def _record(self, src, n, reads, writes):
        for k in reads:
            self.readers.setdefault(k, {})[src] = n
        for k in writes:
            self.last_w[k] = (src, n)
            self.readers[k] = {}

    def op(self, eng, fn, reads=(), writes=(), signal=True):
        self._wait(eng, self._deps(reads, writes))
        n = self.cnt[eng] + 1
        if signal:
            self.cnt[eng] = n
        self.prog[eng].append(("op", fn, eng if signal else None, 1))
        self._record(eng, n, reads, writes)

    def dma(self, issuer, lane, fn, reads=(), writes=()):
        if lane not in self.sems:
            self._mk(lane)
        d = self._deps(reads, writes)
        if self.cnt[lane] > 0:
            d[lane] = max(d.get(lane, 0), self.cnt[lane])
        self._wait(issuer, d)
        self.cnt[lane] += 1
        self.prog[issuer].append(("op", fn, lane, 16))
        self._record(lane, self.cnt[lane], reads, writes)

    def wait_all(self, eng, lanes):
        d = {l: self.cnt[l] for l in lanes if self.cnt.get(l, 0) > 0}
        self._wait(eng, d)

    def flush(self):
        nc = self.nc
        prog = self.prog
        sems = self.sems
        self.prog = {e: [] for e in ENGS}

        def run(e, lst):
            for it in lst:
                if it[0] == "wait":
                    e.wait_ge(sems[it[1]], it[2])
                else:
                    ins = it[1](e)
                    if it[2] is not None:
                        ins.then_inc(sems[it[2]], it[3])

        with nc.Block() as block:
            @block.tensor
            def _(e):
                run(e, prog["pe"])

            @block.scalar
            def _(e):
                run(e, prog["act"])

            @block.vector
            def _(e):
                run(e, prog["dve"])

            @block.gpsimd
            def _(e):
                run(e, prog["pool"])

            @block.sync
            def _(e):
                run(e, prog["sp"])


W_INPUTS = [
    ("conv_mod_w", [D, 3 * D]), ("conv_mod_b", [3 * D]),
    ("conv_pw1_w", [D, 2 * D]), ("conv_pw1_b", [2 * D]),
    ("conv_dw_w", [CONVW, D]), ("conv_dw_b", [D]),
    ("conv_norm_g", [D]), ("conv_norm_b", [D]),
    ("conv_pw2_w", [D, D]), ("conv_pw2_b", [D]),
    ("attn_mod_w", [D, 3 * D]), ("attn_mod_b", [3 * D]),
    ("attn_qkv_w", [D, 3 * D]),
    ("attn_lam_q1", [64]), ("attn_lam_k1", [64]), ("attn_lam_q2", [64]), ("attn_lam_k2", [64]),
    ("attn_subln_g", [128]),
    ("attn_out_w", [D, D]),
    ("mlp_mod_w", [2, D, 3 * D]), ("mlp_mod_b", [2, 3 * D]),
    ("mlp_w1", [2, D, DFF]), ("mlp_w2", [2, DFF, D]),
    ("post_mix_g", [2, D]), ("post_mix_b", [2, D]),
    ("post_mlp_g", [2, D]), ("post_mlp_b", [2, D]),
]


def build(mode, ng=NG, strict_same=True, debug=False):
    doA = mode in ("A", "F")
    doB = mode in ("B", "F")
    nc = bass.Bass("TRN2", target_bir_lowering=False)
    I = {}

    def din(name, shape, dt=F32):
        I[name] = nc.dram_tensor(name, list(shape), dt, kind="ExternalInput").ap()
        return I[name]

    for name, shape in W_INPUTS:
        din(name, shape)
    din("cvec", [D])
    din("ident", [128, 128])
    if doA:
        din("xs", [2 * ng, TT, D])
        din("hv", [128, 2 * NG])
    if doB:
        din("biasarr", [NH, 128, 2304])
        din("ch", [128, NH])

    inter = "Internal" if mode == "F" else None
    x1_scr = nc.dram_tensor("x1_scr", [ng, GR, D], F32,
                            kind=inter or ("ExternalOutput" if mode == "A" else "ExternalInput")).ap()
    qt_scr = nc.dram_tensor("qt_scr", [16 * 64, NG * GR], BF16,
                            kind=inter or ("ExternalOutput" if mode == "A" else "ExternalInput")).ap()
    kv_all = nc.dram_tensor("kv_all", [4096, NG * GR], BF16,
                            kind=inter or ("ExternalOutput" if mode == "A" else "ExternalInput")).ap()
    if doB:
        out = nc.dram_tensor("out", [ng, GR, D], F32, kind="ExternalOutput").ap()
    NCH = 46
    if debug and doB:
        dbg_at = nc.dram_tensor("dbg_at", [128, 4, D], BF16, kind="ExternalOutput").ap()
        dbg_xq = nc.dram_tensor("dbg_xq", [128, 4, D], F32, kind="ExternalOutput").ap()
    if debug and doA:
        dbg_ht = nc.dram_tensor("dbg_ht", [128, 8, TT], BF16, kind="ExternalOutput").ap()
        dbg_acc = nc.dram_tensor("dbg_acc", [128, 8, GR], F32, kind="ExternalOutput").ap()
        dbg_vt = nc.dram_tensor("dbg_vt", [128, 8, GR], BF16, kind="ExternalOutput").ap()
        dbg_x0 = nc.dram_tensor("dbg_x0", [128, 4, D], F32, kind="ExternalOutput").ap()
        dbg_cv = nc.dram_tensor("dbg_cv", [128, 64], F32, kind="ExternalOutput").ap()
    wscr = nc.dram_tensor("wscr", [NCH, 128, 4096], BF16, kind="Internal").ap()
    modscr = nc.dram_tensor("modscr", [4, 3 * D], F32, kind="Internal").ap()

    st = ExitStack()
    with st:
        sch = Sched(nc, st, strict_same=strict_same)

        def sb(name, shape, dt, stack=st):
            return stack.enter_context(nc.sbuf_tensor(name, list(shape), dt))

        NWR = 3
        WR = sb("WR", [128, NWR, 4096], BF16)
        X = sb("X", [128, 5, D], F32)
        XB = sb("XB", [128, 5, D], BF16)
        HT = sb("HT", [128, 8, TT], BF16)
        HID = sb("HID", [128, 32, GR], BF16)
        Z = sb("Z", [128, 2, D], F32)
        ZN = sb("ZN", [128, 2, D], F32)
        XO = sb("XO", [128, 2, D], F32)
        BC = sb("BC", [128, 6, D], F32)
        RT = sb("RT", [128, 2, GR], F32)
        CV = sb("CV", [128, 8 * 8], F32)
        IDB = sb("IDB", [128, 128], BF16)
        STt = sb("STt", [128, 2, 12], F32)
        MV = sb("MV", [128, 2, 4], F32)
        NHALF = sb("NHALF", [128, GR], F32)
        PS = [st.enter_context(nc.psum_tensor("PS%d" % i, [128, 1024], F32)) for i in range(4)]

        def bank(b):
            return PS[b // 2][:, (b % 2) * 512:(b % 2) * 512 + 512]

        bptr = [0]
        nbmod = [8]

        def nb():
            b = bptr[0]
            bptr[0] = (b + 1) % nbmod[0]
            return b

        def nb2():
            if bptr[0] % 2:
                bptr[0] = (bptr[0] + 1) % 8
            b = bptr[0]
            bptr[0] = (b + 2) % 8
            return b

        def mm(out_ap, lhsT, rhs, start, stop, reads, writes, signal=None):
            sig = stop if signal is None else signal
            sch.op("pe", lambda e: e.matmul(out_ap, lhsT=lhsT, rhs=rhs, start=start, stop=stop),
                   reads, writes, signal=sig)

        def act(out_ap, in_ap, func, reads, writes, bias=None, scale=None):
            kw = {}
            if bias is not None:
                kw["bias"] = bias
            if scale is not None:
                kw["scale"] = scale
            sch.op("act", lambda e: e.activation(out=out_ap, in_=in_ap, func=func, **kw), reads, writes)

        def tt_(eng, out_ap, a, b, op, reads, writes):
            sch.op(eng, lambda e: e.tensor_tensor(out=out_ap, in0=a, in1=b, op=op), reads, writes)

        def ts_(eng, out_ap, a, s1, s2, op0, op1, reads, writes):
            if s2 is None:
                sch.op(eng, lambda e: e.tensor_scalar(out=out_ap, in0=a, scalar1=s1, scalar2=None, op0=op0),
                       reads, writes)
            else:
                sch.op(eng, lambda e: e.tensor_scalar(out=out_ap, in0=a, scalar1=s1, scalar2=s2, op0=op0, op1=op1),
                       reads, writes)

        def stt(out_ap, a, s, b, op0, op1, reads, writes, accum=None):
            if accum is None:
                sch.op("dve", lambda e: e.scalar_tensor_tensor(out=out_ap, in0=a, scalar=s, in1=b, op0=op0, op1=op1),
                       reads, writes)
            else:
                sch.op("dve", lambda e: e.scalar_tensor_tensor(out=out_ap, in0=a, scalar=s, in1=b, op0=op0, op1=op1,
                                                               accum_out=accum), reads, writes)

        def cp(eng, out_ap, in_ap, reads, writes):
            if eng == "act":
                sch.op("act", lambda e: e.copy(out=out_ap, in_=in_ap), reads, writes)
            else:
                sch.op(eng, lambda e: e.tensor_copy(out=out_ap, in_=in_ap), reads, writes)

        def dma(issuer, lane, out_ap, in_ap, reads, writes, nonc=False):
            if nonc:
                sch.dma(issuer, lane, lambda e: e.dma_start(out=out_ap, in_=in_ap, allow_slow_non_contiguous=True),
                        reads, writes)
            else:
                sch.dma(issuer, lane, lambda e: e.dma_start(out=out_ap, in_=in_ap), reads, writes)

        chunk_id = {}
        cvl = [0]

        cvt_jobs = []

        def cvt(out_ap, in_ap, ci):
            cvt_jobs.append((out_ap, in_ap, ci))

        def emit_cvts(nmax=None):
            k = 0
            while cvt_jobs and (nmax is None or k < nmax):
                (out_ap, in_ap, ci) = cvt_jobs.pop(0)
                lane = "cv%d" % (cvl[0] % 4)
                first = cvl[0] == 0
                cvl[0] += 1
                k += 1
                dma("pool", lane, out_ap, in_ap, [("modscr", s_, n_) for s_ in range(4) for n_ in range(6)] if first else [], [("wscr", ci)])

        def add_kmajor(name, W, ncols):
            Wv = W.rearrange("(k p) n -> p k n", p=128)
            for n in range(ncols // 512):
                ci = len(chunk_id)
                chunk_id[(name, n)] = ci
                cvt(wscr[ci].rearrange("p (k n) -> p k n", k=8), Wv[:, :, n * 512:(n + 1) * 512], ci)

        def add_w2(name, W):
            Wv = W.rearrange("(f p) n -> p f n", p=128)
            for c in range(8):
                ci = len(chunk_id)
                chunk_id[(name, c)] = ci
                cvt(wscr[ci].rearrange("p (f n) -> p f n", f=4), Wv[:, 4 * c:4 * c + 4, :], ci)

        if doA:
            Wv = I["conv_pw1_w"].rearrange("(k p) n -> p k n", p=128)
            for j in range(4):
                ci = len(chunk_id)
                chunk_id[("pw1", j)] = ci
                dst = wscr[ci].rearrange("p (k n) -> p k n", k=8)
                cvt(dst[:, :, 0:256], Wv[:, :, 256 * j:256 * j + 256], ci)
                cvt(dst[:, :, 256:512], Wv[:, :, 1024 + 256 * j:1024 + 256 * j + 256], ci)
            add_kmajor("pw2", I["conv_pw2_w"], D)
            add_kmajor("w1_0", I["mlp_w1"][0], DFF)
            add_w2("w2_0", I["mlp_w2"][0])
            add_kmajor("qkv", I["attn_qkv_w"], 3 * D)
        if doB:
            add_kmajor("wo", I["attn_out_w"], D)
            add_kmajor("w1_1", I["mlp_w1"][1], DFF)
            add_w2("w2_1", I["mlp_w2"][1])

        class WStream:
            def __init__(self, seq):
                self.seq = seq
                self.issued = 0
                self.pos = 0
                self.released = 0

            def _issue(self):
                k = self.issued
                slot = k % NWR
                ci = chunk_id[self.seq[k]]
                dma("sp", "wr%d" % slot, WR[:, slot, :], wscr[ci], [("wscr", ci)], [("WR", slot)])
                self.issued += 1

            def prefetch(self):
                while self.issued < len(self.seq) and self.issued - NWR < self.released:
                    self._issue()

            def get(self, name):
                assert self.seq[self.pos] == name, (self.seq[self.pos], name)
                self.prefetch()
                assert self.issued > self.pos
                slot = self.pos % NWR
                self.pos += 1
                return slot

            def release(self, n=1):
                self.released += n
                self.prefetch()

        if doA:
            CA = sb("CA", [128, 8 * 5 + CONVW * 8], F32)
            HVt = sb("HVt", [128, 2 * NG], F32)
            ONES = sb("ONES", [128, 128], F32)
            ONEB = sb("ONEB", [1, 128], BF16)
            PB2 = sb("PB2", [1, D], BF16)
            cal = [0]

            def colA(dst, vec, key):
                cal[0] += 1
                dma("sp", "pl5", dst, vec.rearrange("(k p) -> p k", p=128), [], [key], nonc=True)

            dma("sp", "pl5", CA[:, 40:40 + CONVW * 8].rearrange("p (j c) -> p j c", c=8),
                I["conv_dw_w"].rearrange("j (c p) -> p j c", p=128), [], [("CAt",)], nonc=True)
            colA(CA[:, 0:8], I["conv_pw1_b"][0:D], ("CA", 0))
            colA(CA[:, 8:16], I["conv_pw1_b"][D:2 * D], ("CA", 1))
            colA(CA[:, 16:24], I["conv_dw_b"], ("CA", 2))
            colA(CA[:, 24:32], I["conv_norm_g"], ("CA", 3))
            colA(CA[:, 32:40], I["conv_norm_b"], ("CA", 4))
            dma("sp", "pl6", HVt[:, :], I["hv"], [], [("HV",)])
            dma("pool", "pl0", PB2[:, :], I["conv_pw2_b"].rearrange("(o n) -> o n", o=1), [], [("PB2",)])
            sch.op("dve", lambda e: e.memset(ONES[:, :], 1.0 / D), [], [("ONES",)])
            sch.op("dve", lambda e: e.memset(ONEB[:, :], 1.0), [], [("ONEB",)])
        pst = ExitStack()
        with pst:
            MW = sb("MW", [128, 2, 8, 512], F32, pst)
            SREP = sb("SREP", [128, 8, 128], F32, pst)
            CCOL = sb("CCOL", [128, 8], F32, pst)
            SCOL = sb("SCOL", [128, 8], F32, pst)
            MB = sb("MB", [128, 2, 512], F32, pst)
            MROW = sb("MROW", [128, 2, 512], F32, pst)
            mwc = [0]
            TMPC4 = sb("TMPC", [128, 4, 48], F32, pst)

            dma("pool", "pl0", IDB[:, :], I["ident"], [], [("IDB",)])
            sch.op("dve", lambda e: e.memset(NHALF[:, :], -0.5), [], [("NHALF",)])
            dma("sp", "pl1", CCOL[:, :], I["cvec"].rearrange("(k p) -> p k", p=128), [], [("CCOL",)], nonc=True)
            act(SCOL[:, :], CCOL[:, :], AF.Sigmoid, [("CCOL",)], [("SCOL",)])
            tt_("dve", SCOL[:, :], SCOL[:, :], CCOL[:, :], ALU.mult, [("SCOL",), ("CCOL",)], [("SCOL",)])
            for k in range(8):
                cp("dve", SREP[:, k, :], SCOL[:, k:k + 1].to_broadcast([128, 128]), [("SCOL",)], [("SREP",)])
            mods = []
            if doA:
                mods += [(0, I["conv_mod_w"], I["conv_mod_b"]), (1, I["mlp_mod_w"][0], I["mlp_mod_b"][0])]
            mods += [(2, I["attn_mod_w"], I["attn_mod_b"])]
            if doB:
                mods += [(3, I["mlp_mod_w"][1], I["mlp_mod_b"][1])]
            for (s, mw, mb) in mods:
                mwv = mw.rearrange("(k p) n -> p k n", p=128)
                for n in range(6):
                    q = mwc[0] % 2
                    mwc[0] += 1
                    dma("sp", "pl2%d" % q, MW[:, q, :, :], mwv[:, :, n * 512:(n + 1) * 512], [], [("MW", q)])
                    dma("sp", "pl3%d" % q, MB[:, q, :], mb[n * 512:(n + 1) * 512].partition_broadcast(128), [], [("MB", q)])
                    b = nb()
                    for k in range(8):
                        mm(bank(b), SREP[:, k, :], MW[:, q, k, :], k == 0, k == 7,
                           [("SREP",), ("MW", q)], [("ps", b)])
                    tt_("dve", MROW[:, q, :], bank(b), MB[:, q, :], ALU.add, [("ps", b), ("MB", q)], [("MROW", q)])
                    dma("sp", "pl4%d" % q, modscr[s:s + 1, n * 512:(n + 1) * 512], MROW[0:1, q, :], [("MROW", q)], [("modscr", s, n)])

            emit_cvts(36 if mode == "F" else None)
            cll = [0]

            def colload(dst_ap, vec, key):
                cll[0] += 1
                dma("sp", "pl5", dst_ap, vec.rearrange("(k p) -> p k", p=128), [("modscr", s_, n_) for s_ in range(4) for n_ in range(6)], [key], nonc=True)

            lnp = {1: (I["post_mix_g"][0], I["post_mix_b"][0]), 2: (I["post_mlp_g"][0], I["post_mlp_b"][0]),
                   3: (I["post_mix_g"][1], I["post_mix_b"][1])}
            for (s, _, _) in mods:
                G2 = CV[:, s * 16:s * 16 + 8]
                B2 = CV[:, s * 16 + 8:s * 16 + 16]
                TMPC = TMPC4[:, s, :]
                colload(TMPC[:, 0:8], modscr[s, 0:D], ("TMPC", s, 0))
                colload(TMPC[:, 8:16], modscr[s, D:2 * D], ("TMPC", s, 1))
                ts_("dve", TMPC[:, 8:16], TMPC[:, 8:16], 1.0, None, ALU.add, None, [("TMPC", s, 1)], [("TMPC", s, 1)])
                if s == 0:
                    cp("dve", G2, TMPC[:, 8:16], [("TMPC", s, 1)], [("CV", s)])
                    cp("dve", B2, TMPC[:, 0:8], [("TMPC", s, 0)], [("CV", s)])
                else:
                    colload(TMPC[:, 16:24], lnp[s][0], ("TMPC", s, 2))
                    colload(TMPC[:, 24:32], lnp[s][1], ("TMPC", s, 3))
                    tt_("dve", G2, TMPC[:, 16:24], TMPC[:, 8:16], ALU.mult, [("TMPC", s, 2), ("TMPC", s, 1)], [("CV", s)])
                    tt_("dve", TMPC[:, 32:40], TMPC[:, 24:32], TMPC[:, 8:16], ALU.mult, [("TMPC", s, 3), ("TMPC", s, 1)], [("TMPC", s, 4)])
                    tt_("dve", B2, TMPC[:, 32:40], TMPC[:, 0:8], ALU.add, [("TMPC", s, 4), ("TMPC", s, 0)], [("CV", s)])
            sch.flush()

        def load_bc(slot, vec, lane="bc"):
            dma("sp", "bc%d" % slot, BC[:, slot, :], vec.partition_broadcast(128),
                [("modscr", s_, n_) for s_ in range(4) for n_ in range(6)], [("BC", slot)])

        def to_hT(s, halo):
            for c in range(8):
                b = nb()
                for t in range(4):
                    mm(bank(b)[:, t * 128:(t + 1) * 128], XB[:, 1 + t, c * 128:(c + 1) * 128], IDB[:, :], True, True,
                       [("XB", 1 + t), ("IDB",)], [("ps", b)], signal=(t == 3))
                act(HT[:, c, HALO:TT], bank(b), AF.Identity, [("ps", b), ("CV", s)], [("HT", c)],
                    bias=CV[:, s * 16 + 8 + c:s * 16 + 9 + c], scale=CV[:, s * 16 + c:s * 16 + c + 1])
            if halo:
                b = nb()
                for c in range(8):
                    mm(bank(b)[:, c * 32:(c + 1) * 32], XB[0:32, 0, c * 128:(c + 1) * 128], IDB[0:32, 0:32], True, True,
                       [("XB", 0), ("IDB",)], [("ps", b)], signal=(c == 7))
                for c in range(8):
                    act(HT[:, c, 0:HALO], bank(b)[:, c * 32:(c + 1) * 32], AF.Identity, [("ps", b), ("CV", s)], [("HT", c)],
                        bias=CV[:, s * 16 + 8 + c:s * 16 + 9 + c], scale=CV[:, s * 16 + c:s * 16 + c + 1])

        def epilogue(t, pb, res_ap, res_key, bcs, out_ap, out_key, znb):
            zb = t % 2
            P = PS[pb // 2][:, :]
            kz, kn = ("Z", zb), ("ZN", zb)
            tt_("dve", Z[:, zb, :], P, BC[:, bcs, :], ALU.mult, [("ps", pb), ("ps", pb + 1), ("BC", bcs)], [kz])
            stt(Z[:, zb, :], res_ap, ALPHA, Z[:, zb, :], ALU.mult, ALU.add, [res_key, kz], [kz])
            for hh in range(2):
                sch.op("dve", lambda e, hh=hh: e.bn_stats(out=STt[:, zb, hh * 6:hh * 6 + 6], in_=Z[:, zb, hh * 512:(hh + 1) * 512]),
                       [kz], [("ST", zb)])
            sch.op("dve", lambda e: e.bn_aggr(out=MV[:, zb, 0:2], in_=STt[:, zb, :]), [("ST", zb)], [("MV", zb)])
            ts_("dve", MV[:, zb, 1:2], MV[:, zb, 1:2], LN_EPS, None, ALU.add, None, [("MV", zb)], [("MV", zb)])
            tt_("pool", MV[:, zb, 2:3], MV[:, zb, 1:2], NHALF[:, 0:1], ALU.pow, [("MV", zb), ("NHALF",)], [("MV", zb)])
            stt(MV[:, zb, 3:4], MV[:, zb, 0:1], -1.0, MV[:, zb, 2:3], ALU.mult, ALU.mult, [("MV", zb)], [("MV", zb)])
            if znb:
                act(XB[:, 1 + t, :], Z[:, zb, :], AF.Identity, [kz, ("MV", zb)], [("XB", 1 + t)],
                    bias=MV[:, zb, 3:4], scale=MV[:, zb, 2:3])
            if out_ap is None:
                return lambda: None
            act(ZN[:, zb, :], Z[:, zb, :], AF.Identity, [kz, ("MV", zb)], [kn],
                bias=MV[:, zb, 3:4], scale=MV[:, zb, 2:3])

            def tail():
                tt_("pool", out_ap, ZN[:, zb, :], BC[:, bcs + 1, :], ALU.mult, [kn, ("BC", bcs + 1)], [out_key])
                tt_("pool", out_ap, out_ap, BC[:, bcs + 2, :], ALU.add, [out_key, ("BC", bcs + 2)], [out_key])
            return tail

        def mlp(ws, s, w1n, w2n):
            to_hT(s, False)
            for f in range(8):
                slot = ws.get((w1n, f))
                for fl in range(4):
                    fc = 4 * f + fl
                    b = nb()
                    for k in range(8):
                        mm(bank(b), WR[:, slot, k * 512 + fl * 128:k * 512 + fl * 128 + 128], HT[:, k, HALO:TT], k == 0, k == 7,
                           [("WR", slot), ("HT", k)], [("ps", b)])
                    rb = fc % 2
                    act(RT[:, rb, :], bank(b), AF.Relu, [("ps", b)], [("RT", rb)])
                    tt_("pool", HID[:, fc, :], RT[:, rb, :], RT[:, rb, :], ALU.mult, [("RT", rb)], [("HID", fc)])
                ws.release()
            for c in range(8):
                slot = ws.get((w2n, c))
                for t in range(4):
                    for half in range(2):
                        b = 2 * t + half
                        for fl in range(4):
                            mm(bank(b), HID[:, 4 * c + fl, t * 128:(t + 1) * 128],
                               WR[:, slot, fl * 1024 + half * 512:fl * 1024 + half * 512 + 512],
                               c == 0 and fl == 0, c == 7 and fl == 3,
                               [("HID", 4 * c + fl), ("WR", slot)], [("ps", b)], signal=(fl == 3))
                ws.release()
            bptr[0] = 0

        if doA:
            ast = ExitStack()
            with ast:
                U = sb("U", [128, 2, TT], BF16, ast)
                NDG = 8
                DG = sb("DG", [128, NDG, 128], BF16, ast)
                dgc = [0]
                SG = sb("SG", [128, 2, TT], F32, ast)
                ACC = sb("ACC", [128, 8, GR], F32, ast)
                SQ = sb("SQ", [128, 2, GR], F32, ast)
                UN = sb("UN", [128, 2, GR], F32, ast)
                VT = sb("VT", [128, 8, GR], BF16, ast)
                STB = sb("STB", [128, 4, GR], F32, ast)
                QS = sb("QS", [128, 2, GR], BF16, ast)
                VS = sb("VS", [128, 2, D], BF16, ast)
                load_bc(0, modscr[0, 2 * D:3 * D])
                load_bc(1, I["post_mix_g"][0])
                load_bc(2, I["post_mix_b"][0])
                load_bc(3, modscr[1, 2 * D:3 * D])
                load_bc(4, I["post_mlp_g"][0])
                load_bc(5, I["post_mlp_b"][0])

                seqA = []
                for pp in range(2 * ng):
                    seqA += [("pw1", j) for j in range(4)] + [("pw2", n) for n in range(2)]
                    seqA += [("w1_0", f) for f in range(8)] + [("w2_0", c) for c in range(8)]
                    seqA += [("qkv", n) for n in range(0 if pp % 2 == 0 else 2, 6)]
                ws = WStream(seqA)

                def load_x(g):
                    dma("sp", "xl0", X[0:HALO, 0, :], I["xs"][g, 0:HALO, :], [], [("X", 0)])
                    dma("sp", "xl1", X[:, 1:5, :], I["xs"][g, HALO:TT, :].rearrange("(t p) d -> p t d", p=128),
                        [], [("X", 1), ("X", 2), ("X", 3), ("X", 4)])

                load_x(0)
                ws.prefetch()
                for pp in range(2 * ng):
                    g = pp // 2
                    own = (pp % 2 == 0)
                    kvl = 0 if own else 1
                    cp("dve", XB[0:HALO, 0, :], X[0:HALO, 0, :], [("X", 0)], [("XB", 0)])
                    for t in range(4):
                        cp("dve" if t % 2 else "act", XB[:, 1 + t, :], X[:, 1 + t, :], [("X", 1 + t)], [("XB", 1 + t)])
                    to_hT(0, True)
                    if debug and pp == 0:
                        dma("sp", "dbg", dbg_ht[:, :, :], HT[:, :, :], [("HT", c) for c in range(8)], [("dbg", 0)])
                        dma("sp", "dbg", dbg_cv[:, :], CV[:, :], [("CV", 0)], [("dbg", 4)])
                    nbmod[0] = 6
                    bptr[0] = 0
                    pw1_slot = {}
                    pw1_banks = {}

                    def pw1_glu(i):
                        j, s2 = i // 2, i % 2
                        if s2 == 0:
                            pw1_slot[j] = ws.get(("pw1", j))
                        slot = pw1_slot[j]
                        ub = i % 2
                        ba, bg, bh = nb(), nb(), nb()
                        for (bk, col0) in ((ba, 128 * s2), (bg, 256 + 128 * s2)):
                            for k in range(8):
                                mm(bank(bk), WR[:, slot, k * 512 + col0:k * 512 + col0 + 128], HT[:, k, HALO:TT], k == 0, k == 7,
                                   [("WR", slot), ("HT", k)], [("ps", bk)])
                        for hi, col0 in enumerate((128 * s2, 256 + 128 * s2)):
                            for k in range(8):
                                mm(bank(bh)[:, hi * 32:hi * 32 + 32], WR[:, slot, k * 512 + col0:k * 512 + col0 + 128], HT[:, k, 0:HALO],
                                   k == 0, k == 7, [("WR", slot), ("HT", k)], [("ps", bh)], signal=(k == 7 and hi == 1))
                        if s2 == 1:
                            ws.release()
                        pw1_banks[i] = (ba, bg, bh)

                    def glu_elem(i):
                        ub = i % 2
                        ba, bg, bh = pw1_banks[i]
                        act(SG[:, ub, HALO:TT], bank(bg), AF.Sigmoid, [("ps", bg), ("CA", 0), ("CA", 1), ("CA", 2), ("CA", 3), ("CA", 4), ("CAt",)], [("SG", ub)], bias=CA[:, 8 + i:9 + i])
                        act(SG[:, ub, 0:HALO], bank(bh)[:, 32:64], AF.Sigmoid, [("ps", bh), ("CA", 0), ("CA", 1), ("CA", 2), ("CA", 3), ("CA", 4), ("CAt",)], [("SG", ub)], bias=CA[:, 8 + i:9 + i])
                        stt(U[:, ub, HALO:TT], bank(ba), CA[:, i:i + 1], SG[:, ub, HALO:TT], ALU.add, ALU.mult,
                            [("ps", ba), ("SG", ub), ("CA", 0), ("CA", 1), ("CA", 2), ("CA", 3), ("CA", 4), ("CAt",)], [("U", ub)])
                        stt(U[:, ub, 0:HALO], bank(bh)[:, 0:32], CA[:, i:i + 1], SG[:, ub, 0:HALO], ALU.add, ALU.mult,
                            [("ps", bh), ("SG", ub), ("CA", 0), ("CA", 1), ("CA", 2), ("CA", 3), ("CA", 4), ("CAt",)], [("U", ub)])
                        ts_("dve", U[:, ub, 0:HALO], U[:, ub, 0:HALO], HVt[:, pp:pp + 1], None, ALU.mult, None,
                            [("U", ub), ("HV",)], [("U", ub)])

                    def dwconv(i):
                        ub = i % 2
                        bc_ = nb()
                        for jt in range(CONVW):
                            dgs = dgc[0] % NDG
                            dgc[0] += 1
                            if jt % 2 == 0:
                                ts_("dve", DG[:, dgs, :], IDB[:, :], CA[:, 40 + jt * 8 + i:41 + jt * 8 + i], None, ALU.mult, None,
                                    [("IDB",), ("CA", 0), ("CA", 1), ("CA", 2), ("CA", 3), ("CA", 4), ("CAt",)], [("DG", dgs)])
                            else:
                                act(DG[:, dgs, :], IDB[:, :], AF.Copy, [("IDB",), ("CA", 0), ("CA", 1), ("CA", 2), ("CA", 3), ("CA", 4), ("CAt",)], [("DG", dgs)],
                                    scale=CA[:, 40 + jt * 8 + i:41 + jt * 8 + i])
                            mm(bank(bc_), DG[:, dgs, :], U[:, ub, 2 + jt:2 + jt + GR], jt == 0, jt == CONVW - 1,
                               [("DG", dgs), ("U", ub)], [("ps", bc_)], signal=True)
                        act(ACC[:, i, :], bank(bc_), AF.Identity, [("ps", bc_), ("CA", 0), ("CA", 1), ("CA", 2), ("CA", 3), ("CA", 4), ("CAt",)], [("ACC", i)], bias=CA[:, 16 + i:17 + i])
                        act(SQ[:, ub, :], ACC[:, i, :], AF.Square, [("ACC", i)], [("SQ", ub)])
                        mm(bank(6), ONES[:, :], ACC[:, i, :], i == 0, i == 7, [("ONES",), ("ACC", i)], [("ps", 6)], signal=True)
                        mm(bank(7), ONES[:, :], SQ[:, ub, :], i == 0, i == 7, [("ONES",), ("SQ", ub)], [("ps", 7)], signal=True)

                    pw1_glu(0)
                    glu_elem(0)
                    for i in range(8):
                        if i + 1 < 8:
                            pw1_glu(i + 1)
                        dwconv(i)
                        if i + 1 < 8:
                            glu_elem(i + 1)
                    bm, bq = 6, 7
                    nbmod[0] = 8
                    bptr[0] = 0
                    cp("dve", STB[:, 0, :], bank(bm), [("ps", bm)], [("STB", 0)])
                    tt_("dve", STB[:, 3, :], STB[:, 0, :], STB[:, 0, :], ALU.mult, [("STB", 0)], [("STB", 3)])
                    tt_("dve", STB[:, 1, :], bank(bq), STB[:, 3, :], ALU.subtract, [("ps", bq), ("STB", 3)], [("STB", 1)])
                    act(STB[:, 1, :], STB[:, 1, :], AF.Ln, [("STB", 1)], [("STB", 1)], bias=LN_EPS)
                    act(STB[:, 1, :], STB[:, 1, :], AF.Exp, [("STB", 1)], [("STB", 1)], scale=-0.5)
                    stt(STB[:, 2, :], STB[:, 0, :], -1.0, STB[:, 1, :], ALU.mult, ALU.mult, [("STB", 0), ("STB", 1)], [("STB", 2)])
                    for i in range(8):
                        ub = i % 2
                        tt_("dve", UN[:, ub, :], ACC[:, i, :], STB[:, 1, :], ALU.mult, [("ACC", i), ("STB", 1)], [("UN", ub)])
                        tt_("dve", UN[:, ub, :], UN[:, ub, :], STB[:, 2, :], ALU.add, [("UN", ub), ("STB", 2)], [("UN", ub)])
                        act(SQ[:, ub, :], UN[:, ub, :], AF.Identity, [("UN", ub), ("CA", 0), ("CA", 1), ("CA", 2), ("CA", 3), ("CA", 4), ("CAt",)], [("SQ", ub)],
                            bias=CA[:, 32 + i:33 + i], scale=CA[:, 24 + i:25 + i])
                        act(RT[:, ub, :], UN[:, ub, :], AF.Sigmoid, [("UN", ub), ("CA", 0), ("CA", 1), ("CA", 2), ("CA", 3), ("CA", 4), ("CAt",)], [("RT", ub)],
                            bias=CA[:, 32 + i:33 + i], scale=CA[:, 24 + i:25 + i])
                        tt_("pool", VT[:, i, :], SQ[:, ub, :], RT[:, ub, :], ALU.mult, [("SQ", ub), ("RT", ub)], [("VT", i)])
                    if debug and pp == 0:
                        dma("sp", "dbg", dbg_acc[:, :, :], ACC[:, :, :], [("ACC", c) for c in range(8)], [("dbg", 1)])
                        dma("sp", "dbg", dbg_vt[:, :, :], VT[:, :, :], [("VT", c) for c in range(8)], [("dbg", 2)])
                    s0 = ws.get(("pw2", 0))
                    s1 = ws.get(("pw2", 1))
                    for t in range(4):
                        pb = nb2()
                        for half, slot in ((0, s0), (1, s1)):
                            b = pb + half
                            mm(bank(b), ONEB[0:1, :], PB2[0:1, half * 512:(half + 1) * 512], True, False,
                               [("ONEB",), ("PB2",)], [("ps", b)])
                            for k in range(8):
                                mm(bank(b), VT[:, k, t * 128:(t + 1) * 128], WR[:, slot, k * 512:(k + 1) * 512], False, k == 7,
                                   [("VT", k), ("WR", slot)], [("ps", b)])
                        if t == 3:
                            ws.release(2)
                        tl = epilogue(t, pb, X[:, 1 + t, :], ("X", 1 + t), 0, X[:, 1 + t, :], ("X", 1 + t), True)
                        if t > 0:
                            prev_tail()
                        prev_tail = tl
                    prev_tail()
                    if debug and pp == 0:
                        dma("sp", "dbg", dbg_x0[:, :, :], X[:, 1:5, :], [("X", 1 + t) for t in range(4)], [("dbg", 3)])
                    mlp(ws, 1, "w1_0", "w2_0")
                    tails = []
                    for t in range(4):
                        if own:
                            tl = epilogue(t, 2 * t, X[:, 1 + t, :], ("X", 1 + t), 3, XO[:, t % 2, :], ("XO", t % 2), True)

                            def fin(t=t, tl=tl):
                                tl()
                                dma("sp", "xo%d" % (t % 2), x1_scr[g, t * 128:(t + 1) * 128, :], XO[:, t % 2, :],
                                    [("XO", t % 2)], [("x1", g, t)])
                            tails.append(fin)
                            if t > 0:
                                tails[t - 1]()
                        else:
                            epilogue(t, 2 * t, X[:, 1 + t, :], ("X", 1 + t), 3, None, None, True)
                    if own:
                        tails[3]()
                    if pp + 1 < 2 * ng:
                        load_x(pp + 1)
                    emit_cvts(2)
                    to_hT(2, False)
                    for n in range(0 if own else 2, 4):
                        slot = ws.get(("qkv", n))
                        for cg in range(4):
                            b = nb()
                            for k in range(8):
                                mm(bank(b), WR[:, slot, k * 512 + cg * 128:k * 512 + cg * 128 + 128], HT[:, k, HALO:TT], k == 0, k == 7,
                                   [("WR", slot), ("HT", k)], [("ps", b)])
                            qb = (n * 4 + cg) % 2
                            if n < 2:
                                act(QS[:, qb, :], bank(b), AF.Copy, [("ps", b)], [("QS", qb)], scale=0.125)
                            else:
                                cp("dve", QS[:, qb, :], bank(b), [("ps", b)], [("QS", qb)])
                            r0 = ((n % 2) * 8 + cg * 2) * 64
                            if n < 2:
                                dst = qt_scr[r0:r0 + 128, g * GR:(g + 1) * GR]
                            else:
                                dst = kv_all[kvl * 2048 + r0:kvl * 2048 + r0 + 128, g * GR:(g + 1) * GR]
                            dma("sp", "qs%d" % qb, dst, QS[:, qb, :], [("QS", qb)],
                                [("qk", n, cg, g)] if n < 2 else [("kw", kvl, n, cg, g)])
                        ws.release()
                    sv0 = ws.get(("qkv", 4))
                    sv1 = ws.get(("qkv", 5))
                    Vv = kv_all[kvl * 2048 + 1024:kvl * 2048 + 2048, :].rearrange("r (a c) -> (r a) c", a=4)
                    for t in range(4):
                        pb = nb2()
                        for half, slot in ((0, sv0), (1, sv1)):
                            b = pb + half
                            for k in range(8):
                                mm(bank(b), HT[:, k, HALO + t * 128:HALO + (t + 1) * 128], WR[:, slot, k * 512:(k + 1) * 512], k == 0, k == 7,
                                   [("HT", k), ("WR", slot)], [("ps", b)])
                        if t == 3:
                            ws.release(2)
                        vb = t % 2
                        cp("act" if t % 2 else "dve", VS[:, vb, :], PS[pb // 2][:, :], [("ps", pb), ("ps", pb + 1)], [("VS", vb)])
                        dma("sp", "vs%d" % vb, Vv[g * GR + t * 128:g * GR + (t + 1) * 128, :], VS[:, vb, :], [("VS", vb)], [("vw", kvl, g, t)])
                emit_cvts()
                lanesA = ["xo0", "xo1", "qs0", "qs1", "vs0", "vs1"]
                sch.wait_all("sp", lanesA)
                sch.flush()

        if doB:
            bst = ExitStack()
            with bst:
                QT = sb("QT", [128, 2, 2, GR], BF16, bst)
                KT = sb("KT", [128, 3, 2, GR], BF16, bst)
                VA = sb("VA", [128, 3, 4, 132], BF16, bst)
                PT = sb("PT", [128, 2, 1024], BF16, bst)
                BA = sb("BA", [128, 2, 2304], BF16, bst)
                CH = sb("CH", [128, NH], F32, bst)
                LM = sb("LM", [128, 4, 64], F32, bst)
                LS = sb("LS", [128, 8], F32, bst)
                GSUB = sb("GSUB", [128, 128], F32, bst)
                OS = sb("OS", [128, 2, 128], F32, bst)
                SM = sb("SM", [128, 2, 8], F32, bst)
                OJ = sb("OJ", [128, 128], F32, bst)
                OAC = sb("OAC", [128, 1032], F32, bst)

                dma("sp", "pl6", CH[:, :], I["ch"], [], [("CH",)])
                for q, nm in enumerate(("attn_lam_q1", "attn_lam_k1", "attn_lam_q2", "attn_lam_k2")):
                    dma("sp", "pl7", LM[:, q, :], I[nm].partition_broadcast(128), [], [("LM", q)])
                dma("sp", "pl7", GSUB[:, :], I["attn_subln_g"].partition_broadcast(128), [], [("GSUB",)])
                ts_("dve", GSUB[:, :], GSUB[:, :], 1.0 - LAMBDA_INIT, None, ALU.mult, None, [("GSUB",)], [("GSUB",)])
                for q in range(2):
                    stt(LM[:, 2 * q, :], LM[:, 2 * q, :], 1.0, LM[:, 2 * q + 1, :], ALU.mult, ALU.mult,
                        [("LM", 2 * q), ("LM", 2 * q + 1)], [("LM", 2 * q)], accum=LS[:, q:q + 1])
                act(LS[:, 2:4], LS[:, 0:2], AF.Exp, [("LM", 0), ("LM", 2)], [("LS",)])
                tt_("dve", LS[:, 4:5], LS[:, 2:3], LS[:, 3:4], ALU.subtract, [("LS",)], [("LS",)])
                ts_("dve", LS[:, 5:6], LS[:, 4:5], LAMBDA_INIT, -1.0, ALU.add, ALU.mult, [("LS",)], [("LS",)])
                sch.op("dve", lambda e: e.memset(VA[:, :, :, 128:129], 1.0), [], [("VA", 0), ("VA", 1), ("VA", 2)])
                sch.op("dve", lambda e: e.memset(QT[64:128, :, :, :], 0.0), [], [("QT", 0), ("QT", 1)])
                sch.op("dve", lambda e: e.memset(KT[64:128, :, :, :], 0.0), [], [("KT", 0), ("KT", 1), ("KT", 2)])
                load_bc(0, modscr[2, 2 * D:3 * D])
                load_bc(1, I["post_mix_g"][1])
                load_bc(2, I["post_mix_b"][1])
                load_bc(3, modscr[3, 2 * D:3 * D])
                load_bc(4, I["post_mlp_g"][1])
                load_bc(5, I["post_mlp_b"][1])

                seqB = []
                for g in range(ng):
                    seqB += [("wo", n) for n in range(2)] + [("w1_1", f) for f in range(8)] + [("w2_1", c) for c in range(8)]
                ws = WStream(seqB)

                Vall = [kv_all[r * 2048 + 1024:r * 2048 + 2048, :].rearrange("r (a c) -> (r a) c", a=4) for r in range(2)]

                def acc_ap(m, qs):
                    a = m * 4 + qs
                    bk, sl = a // 3, a % 3
                    if bk < 2:
                        return PS[2][:, bk * 512 + sl * 129:bk * 512 + sl * 129 + 129], ("ps", 4 + bk)
                    return PS[3][:, sl * 129:sl * 129 + 129], ("ps", 6)

                kvc = [0]
                hcount = [0]
                for j in range(ng):
                    dma("sp", "xl1", X[:, 1:5, :], x1_scr[j].rearrange("(t p) d -> p t d", p=128),
                        [("x1", j, t) for t in range(4)], [("X", 1), ("X", 2), ("X", 3), ("X", 4)])
                    for h in range(NH):
                        hb = hcount[0] % 2
                        hcount[0] += 1

                        def load_head(jj, hh, hbb):
                            dma("sp", "qt%d" % hbb, QT[0:64, hbb, :, :],
                                qt_scr[2 * hh * 64:2 * hh * 64 + 128, jj * GR:(jj + 1) * GR].rearrange("(m d) t -> d m t", m=2),
                                [("qk", n, cg, jj) for n in range(2) for cg in range(4)], [("QT", hbb)])
                            dma("pool", "ba%d" % hbb, BA[:, hbb, :], I["biasarr"][hh], [], [("BA", hbb)])

                        if j == 0 and h == 0:
                            load_head(0, 0, hb)
                        if h + 1 < NH:
                            load_head(j, h + 1, 1 - hb)
                        elif j + 1 < ng:
                            load_head(j + 1, 0, 1 - hb)
                        blocks = []
                        for i in range(j + 1):
                            for r in range(2):
                                for kb in range(4):
                                    sp_off = None
                                    if i == j:
                                        sp_off = (0 if r == 0 else 896) + 384 - kb * 128
                                    elif i == j - 1 and r == 1 and kb == 3:
                                        sp_off = 1792
                                    blocks.append((r, i, kb, sp_off))
                        kvslot = {}

                        def load_kv(r, i):
                            sl = kvc[0] % 3
                            kvc[0] += 1
                            kvslot[(r, i)] = sl
                            dma("sp", "kt%d" % sl, KT[0:64, sl, :, :],
                                kv_all[r * 2048 + 2 * h * 64:r * 2048 + 2 * h * 64 + 128, i * GR:(i + 1) * GR].rearrange("(m d) t -> d m t", m=2),
                                [], [("KT", sl)])
                            dma("sp", "va%d" % sl, VA[:, sl, :, 0:128],
                                Vall[r][i * GR:(i + 1) * GR, h * 128:(h + 1) * 128].rearrange("(kb p) e -> p kb e", p=128),
                                [], [("VA", sl)], nonc=True)

                        def qk(n):
                            r, i, kb, _ = blocks[n]
                            if (r, i) not in kvslot:
                                load_kv(r, i)
                            sl = kvslot[(r, i)]
                            sbuf_ = n % 2
                            spo = blocks[n][3]
                            for m in range(2):
                                b = 2 * sbuf_ + m
                                mm(bank(b), KT[:, sl, m, kb * 128:(kb + 1) * 128], QT[:, hb, m, :], True, spo is None,
                                   [("KT", sl), ("QT", hb)], [("ps", b)], signal=(m == 1 and spo is None))
                                if spo is not None:
                                    mm(bank(b), IDB[:, :], BA[:, hb, spo:spo + 512], False, True,
                                       [("IDB",), ("BA", hb)], [("ps", b)], signal=(m == 1))

                        qk(0)
                        nblk = len(blocks)
                        for n in range(nblk):
                            r, i, kb, sp_off = blocks[n]
                            if n + 1 < nblk:
                                qk(n + 1)
                            sbuf_ = n % 2
                            sl = kvslot[(r, i)]
                            b0 = 2 * sbuf_
                            if sp_off is not None:
                                act(PT[:, sbuf_, :], PS[sbuf_][:, :], AF.Exp, [("ps", b0), ("ps", b0 + 1)], [("PT", sbuf_)])
                            else:
                                act(PT[:, sbuf_, :], PS[sbuf_][:, :], AF.Exp, [("ps", b0), ("ps", b0 + 1), ("CH",)], [("PT", sbuf_)],
                                    bias=CH[:, h:h + 1])
                            for m in range(2):
                                for qs in range(4):
                                    ap_, key_ = acc_ap(m, qs)
                                    mm(ap_, PT[:, sbuf_, m * 512 + qs * 128:m * 512 + qs * 128 + 128], VA[:, sl, kb, 0:129],
                                       n == 0 and (m * 4 + qs) % 3 == 0, n == nblk - 1, [("PT", sbuf_), ("VA", sl)], [key_],
                                       signal=(m == 1 and qs == 3))
                        cp("dve", OAC[:, 0:387], PS[2][:, 0:387], [("ps", 4)], [("OAC", 0)])
                        cp("dve", OAC[:, 387:774], PS[2][:, 512:899], [("ps", 5)], [("OAC", 1)])
                        cp("dve", OAC[:, 774:1032], PS[3][:, 0:258], [("ps", 6)], [("OAC", 2)])
                        for qs in range(4):
                            ob = qs % 2
                            i1, i2 = qs, 4 + qs
                            a1, k1 = OAC[:, i1 * 129:(i1 + 1) * 129], ("OAC", i1 // 3)
                            a2, k2 = OAC[:, i2 * 129:(i2 + 1) * 129], ("OAC", i2 // 3)
                            ko, ks = ("OS", ob), ("SM", ob)
                            sch.op("dve", lambda e, a1=a1, ob=ob: e.reciprocal(out=SM[:, ob, 0:1], in_=a1[:, 128:129]), [k1], [ks])
                            sch.op("dve", lambda e, a2=a2, ob=ob: e.reciprocal(out=SM[:, ob, 1:2], in_=a2[:, 128:129]), [k2], [ks])
                            ts_("dve", SM[:, ob, 2:3], SM[:, ob, 1:2], LS[:, 5:6], None, ALU.mult, None, [ks, ("LS",)], [ks])
                            ts_("dve", OS[:, ob, :], a1[:, 0:128], SM[:, ob, 0:1], None, ALU.mult, None, [k1, ks], [ko])
                            stt(OS[:, ob, :], a2[:, 0:128], SM[:, ob, 2:3], OS[:, ob, :], ALU.mult, ALU.add, [k2, ks, ko], [ko])
                            stt(OJ[:, :], OS[:, ob, :], 1.0, OS[:, ob, :], ALU.mult, ALU.mult, [ko], [("OJ",)], accum=SM[:, ob, 3:4])
                            ts_("dve", SM[:, ob, 4:5], SM[:, ob, 3:4], 1.0 / 128.0, LN_EPS, ALU.mult, ALU.add, [("OJ",), ks], [ks])
                            tt_("pool", SM[:, ob, 5:6], SM[:, ob, 4:5], NHALF[:, 0:1], ALU.pow, [ks, ("NHALF",)], [ks])
                            stt(XB[:, 1 + qs, h * 128:(h + 1) * 128], OS[:, ob, :], SM[:, ob, 5:6], GSUB[:, :], ALU.mult, ALU.mult,
                                [ko, ks, ("GSUB",)], [("XB", 1 + qs)])
                    if debug and j == 0:
                        dma("sp", "dbg", dbg_at[:, :, :], XB[:, 1:5, :], [("XB", 1 + t) for t in range(4)], [("dbg", 5)])
                    bptr[0] = 7
                    for c in range(8):
                        b = 7
                        for t in range(4):
                            mm(bank(b)[:, t * 128:(t + 1) * 128], XB[:, 1 + t, c * 128:(c + 1) * 128], IDB[:, :], True, True,
                               [("XB", 1 + t), ("IDB",)], [("ps", b)], signal=(t == 3))
                        cp("act" if c % 2 else "dve", HT[:, c, HALO:TT], bank(b), [("ps", b)], [("HT", c)])
                    bptr[0] = 0
                    s0 = ws.get(("wo", 0))
                    s1 = ws.get(("wo", 1))
                    for t in range(4):
                        pb = nb2()
                        for half, slot in ((0, s0), (1, s1)):
                            b = pb + half
                            for k in range(8):
                                mm(bank(b), HT[:, k, HALO + t * 128:HALO + (t + 1) * 128], WR[:, slot, k * 512:(k + 1) * 512], k == 0, k == 7,
                                   [("HT", k), ("WR", slot)], [("ps", b)])
                        if t == 3:
                            ws.release(2)
                        tl = epilogue(t, pb, X[:, 1 + t, :], ("X", 1 + t), 0, X[:, 1 + t, :], ("X", 1 + t), True)
                        if t > 0:
                            prev_tail()
                        prev_tail = tl
                    prev_tail()
                    if debug and j == 0:
                        dma("sp", "dbg", dbg_xq[:, :, :], X[:, 1:5, :], [("X", 1 + t) for t in range(4)], [("dbg", 6)])
                    mlp(ws, 3, "w1_1", "w2_1")
                    tails = []
                    for t in range(4):
                        tl = epilogue(t, 2 * t, X[:, 1 + t, :], ("X", 1 + t), 3, XO[:, t % 2, :], ("XO", t % 2), False)

                        def fin(t=t, tl=tl):
                            tl()
                            dma("sp", "xo%d" % (t % 2), out[j, t * 128:(t + 1) * 128, :], XO[:, t % 2, :],
                                [("XO", t % 2)], [("out", j, t)])
                        tails.append(fin)
                        if t > 0:
                            tails[t - 1]()
                    tails[3]()
                sch.wait_all("sp", ["xo0", "xo1"])
                sch.flush()
    return nc


def _t5_bucket(rel):
    n = np.maximum(rel, 0)
    nf = np.maximum(n, 1).astype(np.float32)
    large = 16 + (np.log(nf / np.float32(16)) / np.float32(math.log(128 / 16)) * np.float32(16)).astype(np.int32)
    large = np.minimum(large, 31)
    return np.where(n < 16, n, large)


def _bias_arrays(rel_bias, role):
    p = np.arange(128)[:, None]
    out = np.empty((NH, 128, 2304), np.float32)

    def fill(width, base_rel):
        j = np.arange(width)[None, :]
        rel = j - p + base_rel
        bk = _t5_bucket(rel)
        v = rel_bias[bk]
        v = np.where((rel >= 0)[:, :, None], v, np.float32(NEG))
        return np.transpose(v, (2, 0, 1))

    if role == 0:
        dA, dB, relC = 0, -512, 128
    else:
        dA, dB, relC = 0, 512, 1152
    out[:, :, 0:896] = fill(896, -384 + dA)
    out[:, :, 896:1792] = fill(896, -384 + dB)
    out[:, :, 1792:2304] = fill(512, relC)
    return out


_CACHE = {}


def _get(mode):
    if mode not in _CACHE:
        _CACHE[mode] = build(mode)
    return _CACHE[mode]


def _core_inputs(inputs, c):
    b, r = c // 2, c % 2
    x = inputs["x"]
    d = {}
    for name, shape in W_INPUTS:
        a = np.ascontiguousarray(inputs[name], dtype=np.float32).reshape(shape)
        d[name] = a
    d["cvec"] = np.ascontiguousarray(inputs["c"][b])
    d["ident"] = np.eye(128, dtype=np.float32)
    xs = np.zeros((2 * NG, TT, D), np.float32)
    hv = np.ones((128, 2 * NG), np.float32)
    for pp in range(2 * NG):
        i = pp // 2
        G = 2 * i + (r if pp % 2 == 0 else 1 - r)
        lo = G * GR - HALO
        if lo < 0:
            xs[pp, HALO:] = x[b, 0:GR]
            hv[:, pp] = 0.0
        else:
            xs[pp] = x[b, lo:lo + TT]
    d["xs"] = xs
    d["hv"] = hv
    d["biasarr"] = _bias_arrays(np.asarray(inputs["rel_bias"], np.float32), r)
    d["ch"] = np.ascontiguousarray(np.broadcast_to(np.asarray(inputs["rel_bias"], np.float32)[31][None, :], (128, NH)))
    return d


FUSED = True


def kernel(**inputs):
    inputs = {k: np.asarray(v) for k, v in inputs.items()}
    cores = list(range(8))
    per = [_core_inputs(inputs, c) for c in cores]
    if FUSED:
        nc = _get("F")
        keys = [t for t in per[0].keys()]
        res = run_bass_kernel_spmd(nc, per, core_ids=cores)
        outs = [r["out"] for r in res.results]
    else:
        ncA = _get("A")
        inA = [{k: v for k, v in p.items() if k not in ("biasarr", "ch")} for p in per]
        resA = run_bass_kernel_spmd(ncA, inA, core_ids=cores).results
        ncB = _get("B")
        inB = []
        for c in cores:
            p = {k: v for k, v in per[c].items() if k not in ("xs", "hv")}
            p["kv_all"] = resA[c]["kv_all"]
            p["x1_scr"] = resA[c]["x1_scr"]
            p["qt_scr"] = resA[c]["qt_scr"]
            inB.append(p)
        resB = run_bass_kernel_spmd(ncB, inB, core_ids=cores).results
        outs = [r["out"] for r in resB]
    y = np.empty((NB, SEQ, D), np.float32)
    for c in cores:
        b, r = c // 2, c % 2
        o = np.asarray(outs[c]).reshape(NG, GR, D)
        for i in range(NG):
            G = 2 * i + r
            y[b, G * GR:(G + 1) * GR] = o[i]
    return y
```

```python
import math
from contextlib import ExitStack

import numpy as np
import concourse.bass as bass
import concourse.mybir as mybir
from concourse.bass_utils import run_bass_kernel_spmd

F32 = mybir.dt.float32
BF16 = mybir.dt.bfloat16
AF = mybir.ActivationFunctionType
ALU = mybir.AluOpType

D = 1024
SEQ = 8192
NB = 4
DFF = 4096
GR = 512
HALO = 32
TT = GR + HALO
NG = 8
NH = 8
ALPHA = 4.0 ** 0.25
LN_EPS = 1e-5
LAMBDA_INIT = 0.8 - 0.6 * math.exp(-0.3 * 1)
CONVW = 31
NEG = -30000.0

ENGS = ("pe", "act", "dve", "pool", "sp")


class Sched:
    def __init__(self, nc, stack, strict_same=True):
        self.nc = nc
        self.stack = stack
        self.sems = {}
        self.cnt = {}
        self.prog = {e: [] for e in ENGS}
        self.seen = {e: {} for e in ENGS}
        self.last_w = {}
        self.readers = {}
        self.strict_same = strict_same
        for e in ENGS:
            self._mk(e)

    def _mk(self, name):
        self.sems[name] = self.stack.enter_context(self.nc.semaphore("s_" + name))
        self.cnt[name] = 0

    def _deps(self, reads, writes):
        d = {}
        for k in reads:
            w = self.last_w.get(k)
            if w:
                d[w[0]] = max(d.get(w[0], 0), w[1])
        for k in writes:
            w = self.last_w.get(k)
            if w:
                d[w[0]] = max(d.get(w[0], 0), w[1])
            for e2, c in self.readers.get(k, {}).items():
                d[e2] = max(d.get(e2, 0), c)
        return d

    def _wait(self, eng, d):
        for src, c in d.items():
            if src == eng and (eng == "pe" or not self.strict_same):
                continue
            if self.seen[eng].get(src, 0) >= c:
                continue
            self.seen[eng][src] = c
            if c > self.cnt[src]:
                print("SCHED WARNING: %s waits on future signal of %s (%d > %d)" % (eng, src, c, self.cnt[src]))
            unit = 1 if src in ENGS else 16
            self.prog[eng].append(("wait", src, c * unit))

    def _record(self, src, n, reads, writes):
        for k in reads:
            self.readers.setdefault(k, {})[src] = n
        for k in writes:
            self.last_w[k] = (src, n)
            self.readers[k] = {}

    def op(self, eng, fn, reads=(), writes=(), signal=True):
        self._wait(eng, self._deps(reads, writes))
        n = self.cnt[eng] + 1
        if signal:
            self.cnt[eng] = n
        self.prog[eng].append(("op", fn, eng if signal else None, 1))
        self._record(eng, n, reads, writes)

    def dma(self, issuer, lane, fn, reads=(), writes=()):
        if lane not in self.sems:
            self._mk(lane)
        d = self._deps(reads, writes)
        if self.cnt[lane] > 0:
            d[lane] = max(d.get(lane, 0), self.cnt[lane])
        self._wait(issuer, d)
        self.cnt[lane] += 1
        self.prog[issuer].append(("op", fn, lane, 16))
        self._record(lane, self.cnt[lane], reads, writes)

    def wait_all(self, eng, lanes):
        d = {l: self.cnt[l] for l in lanes if self.cnt.get(l, 0) > 0}
        self._wait(eng, d)

    def flush(self):
        nc = self.nc
        prog = self.prog
        sems = self.sems
        self.prog = {e: [] for e in ENGS}

        def run(e, lst):
            for it in lst:
                if it[0] == "wait":
                    e.wait_ge(sems[it[1]], it[2])
                else:
                    ins = it[1](e)
                    if it[2] is not None:
                        ins.then_inc(sems[it[2]], it[3])

        with nc.Block() as block:
            @block.tensor
            def _(e):
                run(e, prog["pe"])

            @block.scalar
            def _(e):
                run(e, prog["act"])

            @block.vector
            def _(e):
                run(e, prog["dve"])

            @block.gpsimd
            def _(e):
                run(e, prog["pool"])

            @block.sync
            def _(e):
                run(e, prog["sp"])


W_INPUTS = [
    ("conv_mod_w", [D, 3 * D]), ("conv_mod_b", [3 * D]),
    ("conv_pw1_w", [D, 2 * D]), ("conv_pw1_b", [2 * D]),
    ("conv_dw_w", [CONVW, D]), ("conv_dw_b", [D]),
    ("conv_norm_g", [D]), ("conv_norm_b", [D]),
    ("conv_pw2_w", [D, D]), ("conv_pw2_b", [D]),
    ("attn_mod_w", [D, 3 * D]), ("attn_mod_b", [3 * D]),
    ("attn_qkv_w", [D, 3 * D]),
    ("attn_lam_q1", [64]), ("attn_lam_k1", [64]), ("attn_lam_q2", [64]), ("attn_lam_k2", [64]),
    ("attn_subln_g", [128]),
    ("attn_out_w", [D, D]),
    ("mlp_mod_w", [2, D, 3 * D]), ("mlp_mod_b", [2, 3 * D]),
    ("mlp_w1", [2, D, DFF]), ("mlp_w2", [2, DFF, D]),
    ("post_mix_g", [2, D]), ("post_mix_b", [2, D]),
    ("post_mlp_g", [2, D]), ("post_mlp_b", [2, D]),
]


def build(mode, ng=NG, strict_same=True, debug=False):
    doA = mode in ("A", "F")
    doB = mode in ("B", "F")
    nc = bass.Bass("TRN2", target_bir_lowering=False)
    I = {}

    def din(name, shape, dt=F32):
        I[name] = nc.dram_tensor(name, list(shape), dt, kind="ExternalInput").ap()
        return I[name]

    for name, shape in W_INPUTS:
        din(name, shape)
    din("cvec", [D])
    din("ident", [128, 128])
    if doA:
        din("xs", [2 * ng, TT, D])
        din("hv", [128, 2 * NG])
    if doB:
        din("biasarr", [NH, 128, 2304])
        din("ch", [128, NH])

    inter = "Internal" if mode == "F" else None
    x1_scr = nc.dram_tensor("x1_scr", [ng, GR, D], F32,
                            kind=inter or ("ExternalOutput" if mode == "A" else "ExternalInput")).ap()
    qt_scr = nc.dram_tensor("qt_scr", [16 * 64, NG * GR], BF16,
                            kind=inter or ("ExternalOutput" if mode == "A" else "ExternalInput")).ap()
    kv_all = nc.dram_tensor("kv_all", [4096, NG * GR], BF16,
                            kind=inter or ("ExternalOutput" if mode == "A" else "ExternalInput")).ap()
    if doB:
        out = nc.dram_tensor("out", [ng, GR, D], F32, kind="ExternalOutput").ap()
    NCH = 46
    if debug and doB:
        dbg_at = nc.dram_tensor("dbg_at", [128, 4, D], BF16, kind="ExternalOutput").ap()
        dbg_xq = nc.dram_tensor("dbg_xq", [128, 4, D], F32, kind="ExternalOutput").ap()
    if debug and doA:
        dbg_ht = nc.dram_tensor("dbg_ht", [128, 8, TT], BF16, kind="ExternalOutput").ap()
        dbg_acc = nc.dram_tensor("dbg_acc", [128, 8, GR], F32, kind="ExternalOutput").ap()
        dbg_vt = nc.dram_tensor("dbg_vt", [128, 8, GR], BF16, kind="ExternalOutput").ap()
        dbg_x0 = nc.dram_tensor("dbg_x0", [128, 4, D], F32, kind="ExternalOutput").ap()
        dbg_cv = nc.dram_tensor("dbg_cv", [128, 64], F32, kind="ExternalOutput").ap()
    wscr = nc.dram_tensor("wscr", [NCH, 128, 4096], BF16, kind="Internal").ap()
    modscr = nc.dram_tensor("modscr", [4, 3 * D], F32, kind="Internal").ap()

    st = ExitStack()
    with st:
        sch = Sched(nc, st, strict_same=strict_same)

        def sb(name, shape, dt, stack=st):
            return stack.enter_context(nc.sbuf_tensor(name, list(shape), dt))

        NWR = 3
        WR = sb("WR", [128, NWR, 4096], BF16)
        X = sb("X", [128, 5, D], F32)
        XB = sb("XB", [128, 5, D], BF16)
        HT = sb("HT", [128, 8, TT], BF16)
        HID = sb("HID", [128, 32, GR], BF16)
        Z = sb("Z", [128, 2, D], F32)
        ZN = sb("ZN", [128, 2, D], F32)
        XO = sb("XO", [128, 2, D], F32)
        BC = sb("BC", [128, 6, D], F32)
        RT = sb("RT", [128, 2, GR], F32)
        CV = sb("CV", [128, 8 * 8], F32)
        IDB = sb("IDB", [128, 128], BF16)
        STt = sb("STt", [128, 2, 12], F32)
        MV = sb("MV", [128, 2, 4], F32)
        NHALF = sb("NHALF", [128, GR], F32)
        PS = [st.enter_context(nc.psum_tensor("PS%d" % i, [128, 1024], F32)) for i in range(4)]

        def bank(b):
            return PS[b // 2][:, (b % 2) * 512:(b % 2) * 512 + 512]

        bptr = [0]
        nbmod = [8]

        def nb():
            b = bptr[0]
            bptr[0] = (b + 1) % nbmod[0]
            return b

        def nb2():
            if bptr[0] % 2:
                bptr[0] = (bptr[0] + 1) % 8
            b = bptr[0]
            bptr[0] = (b + 2) % 8
            return b

        def mm(out_ap, lhsT, rhs, start, stop, reads, writes, signal=None):
            sig = stop if signal is None else signal
            sch.op("pe", lambda e: e.matmul(out_ap, lhsT=lhsT, rhs=rhs, start=start, stop=stop),
                   reads, writes, signal=sig)

        def act(out_ap, in_ap, func, reads, writes, bias=None, scale=None):
            kw = {}
            if bias is not None:
                kw["bias"] = bias
            if scale is not None:
                kw["scale"] = scale
            sch.op("act", lambda e: e.activation(out=out_ap, in_=in_ap, func=func, **kw), reads, writes)

        def tt_(eng, out_ap, a, b, op, reads, writes):
            sch.op(eng, lambda e: e.tensor_tensor(out=out_ap, in0=a, in1=b, op=op), reads, writes)

        def ts_(eng, out_ap, a, s1, s2, op0, op1, reads, writes):
            if s2 is None:
                sch.op(eng, lambda e: e.tensor_scalar(out=out_ap, in0=a, scalar1=s1, scalar2=None, op0=op0),
                       reads, writes)
            else:
                sch.op(eng, lambda e: e.tensor_scalar(out=out_ap, in0=a, scalar1=s1, scalar2=s2, op0=op0, op1=op1),
                       reads, writes)

        def stt(out_ap, a, s, b, op0, op1, reads, writes, accum=None):
            if accum is None:
                sch.op("dve", lambda e: e.scalar_tensor_tensor(out=out_ap, in0=a, scalar=s, in1=b, op0=op0, op1=op1),
                       reads, writes)
            else:
                sch.op("dve", lambda e: e.scalar_tensor_tensor(out=out_ap, in0=a, scalar=s, in1=b, op0=op0, op1=op1,
                                                               accum_out=accum), reads, writes)

        def cp(eng, out_ap, in_ap, reads, writes):
            if eng == "act":
                sch.op("act", lambda e: e.copy(out=out_ap, in_=in_ap), reads, writes)
            else:
                sch.op(eng, lambda e: e.tensor_copy(out=out_ap, in_=in_ap), reads, writes)

        def dma(issuer, lane, out_ap, in_ap, reads, writes, nonc=False):
            if nonc:
                sch.dma(issuer, lane, lambda e: e.dma_start(out=out_ap, in_=in_ap, allow_slow_non_contiguous=True),
                        reads, writes)
            else:
                sch.dma(issuer, lane, lambda e: e.dma_start(out=out_ap, in_=in_ap), reads, writes)

        chunk_id = {}
        cvl = [0]

        cvt_jobs = []

        def cvt(out_ap, in_ap, ci):
            cvt_jobs.append((out_ap, in_ap, ci))

        def emit_cvts(nmax=None, nodep=False):
            k = 0
            while cvt_jobs and (nmax is None or k < nmax):
                (out_ap, in_ap, ci) = cvt_jobs.pop(0)
                lane = "cv%d" % (cvl[0] % 4)
                first = (k == 0) and not nodep
                cvl[0] += 1
                k += 1
                dma("pool", lane, out_ap, in_ap, [("modscr", s_, n_) for s_ in range(4) for n_ in range(6)] if first else [], [("wscr", ci)])

        def add_kmajor(name, W, ncols):
            Wv = W.rearrange("(k p) n -> p k n", p=128)
            for n in range(ncols // 512):
                ci = len(chunk_id)
                chunk_id[(name, n)] = ci
                cvt(wscr[ci].rearrange("p (k n) -> p k n", k=8), Wv[:, :, n * 512:(n + 1) * 512], ci)

        def add_w2(name, W):
            Wv = W.rearrange("(f p) n -> p f n", p=128)
            for c in range(8):
                ci = len(chunk_id)
                chunk_id[(name, c)] = ci
                cvt(wscr[ci].rearrange("p (f n) -> p f n", f=4), Wv[:, 4 * c:4 * c + 4, :], ci)

        if doA:
            Wv = I["conv_pw1_w"].rearrange("(k p) n -> p k n", p=128)
            for j in range(4):
                ci = len(chunk_id)
                chunk_id[("pw1", j)] = ci
                dst = wscr[ci].rearrange("p (k n) -> p k n", k=8)
                cvt(dst[:, :, 0:256], Wv[:, :, 256 * j:256 * j + 256], ci)
                cvt(dst[:, :, 256:512], Wv[:, :, 1024 + 256 * j:1024 + 256 * j + 256], ci)
            add_kmajor("pw2", I["conv_pw2_w"], D)
            add_kmajor("w1_0", I["mlp_w1"][0], DFF)
            add_w2("w2_0", I["mlp_w2"][0])
            add_kmajor("qkv", I["attn_qkv_w"], 3 * D)
        if doB:
            add_kmajor("wo", I["attn_out_w"], D)
            add_kmajor("w1_1", I["mlp_w1"][1], DFF)
            add_w2("w2_1", I["mlp_w2"][1])

        class WStream:
            def __init__(self, seq):
                self.seq = seq
                self.issued = 0
                self.pos = 0
                self.released = 0

            def _issue(self):
                k = self.issued
                slot = k % NWR
                ci = chunk_id[self.seq[k]]
                dma("sp", "wr%d" % slot, WR[:, slot, :], wscr[ci], [("wscr", ci)], [("WR", slot)])
                self.issued += 1

            def prefetch(self):
                while self.issued < len(self.seq) and self.issued - NWR < self.released:
                    self._issue()

            def get(self, name):
                assert self.seq[self.pos] == name, (self.seq[self.pos], name)
                self.prefetch()
                assert self.issued > self.pos
                slot = self.pos % NWR
                self.pos += 1
                return slot

            def release(self, n=1):
                self.released += n
                self.prefetch()

        if doA:
            CA = sb("CA", [128, 8 * 5 + CONVW * 8], F32)
            HVt = sb("HVt", [128, 2 * NG], F32)
            ONES = sb("ONES", [128, 128], F32)
            ONEB = sb("ONEB", [1, 128], BF16)
            PB2 = sb("PB2", [1, D], BF16)
            cal = [0]

            def colA(dst, vec, key):
                cal[0] += 1
                dma("sp", "pl5", dst, vec.rearrange("(k p) -> p k", p=128), [], [key], nonc=True)

            dma("sp", "pl5", CA[:, 40:40 + CONVW * 8].rearrange("p (j c) -> p j c", c=8),
                I["conv_dw_w"].rearrange("j (c p) -> p j c", p=128), [], [("CAt",)], nonc=True)
            colA(CA[:, 0:8], I["conv_pw1_b"][0:D], ("CA", 0))
            colA(CA[:, 8:16], I["conv_pw1_b"][D:2 * D], ("CA", 1))
            colA(CA[:, 16:24], I["conv_dw_b"], ("CA", 2))
            colA(CA[:, 24:32], I["conv_norm_g"], ("CA", 3))
            colA(CA[:, 32:40], I["conv_norm_b"], ("CA", 4))
            dma("sp", "pl6", HVt[:, :], I["hv"], [], [("HV",)])
            dma("pool", "pl0", PB2[:, :], I["conv_pw2_b"].rearrange("(o n) -> o n", o=1), [], [("PB2",)])
            sch.op("dve", lambda e: e.memset(ONES[:, :], 1.0 / D), [], [("ONES",)])
            sch.op("dve", lambda e: e.memset(ONEB[:, :], 1.0), [], [("ONEB",)])
        pst = ExitStack()
        with pst:
            MW = sb("MW", [128, 2, 8, 512], F32, pst)
            SREP = sb("SREP", [128, 8, 128], F32, pst)
            CCOL = sb("CCOL", [128, 8], F32, pst)
            SCOL = sb("SCOL", [128, 8], F32, pst)
            MB = sb("MB", [128, 2, 512], F32, pst)
            MROW = sb("MROW", [128, 2, 512], F32, pst)
            mwc = [0]
            TMPC4 = sb("TMPC", [128, 4, 48], F32, pst)

            dma("pool", "pl0", IDB[:, :], I["ident"], [], [("IDB",)])
            sch.op("dve", lambda e: e.memset(NHALF[:, :], -0.5), [], [("NHALF",)])
            dma("sp", "pl1", CCOL[:, :], I["cvec"].rearrange("(k p) -> p k", p=128), [], [("CCOL",)], nonc=True)
            act(SCOL[:, :], CCOL[:, :], AF.Sigmoid, [("CCOL",)], [("SCOL",)])
            tt_("dve", SCOL[:, :], SCOL[:, :], CCOL[:, :], ALU.mult, [("SCOL",), ("CCOL",)], [("SCOL",)])
            for k in range(8):
                cp("dve", SREP[:, k, :], SCOL[:, k:k + 1].to_broadcast([128, 128]), [("SCOL",)], [("SREP",)])
            if doA:
                emit_cvts(8, nodep=True)
            mods = []
            if doA:
                mods += [(0, I["conv_mod_w"], I["conv_mod_b"]), (1, I["mlp_mod_w"][0], I["mlp_mod_b"][0])]
            mods += [(2, I["attn_mod_w"], I["attn_mod_b"])]
            if doB:
                mods += [(3, I["mlp_mod_w"][1], I["mlp_mod_b"][1])]
            for (s, mw, mb) in mods:
                mwv = mw.rearrange("(k p) n -> p k n", p=128)
                for n in range(6):
                    q = mwc[0] % 2
                    mwc[0] += 1
                    dma("sp", "pl2%d" % q, MW[:, q, :, :], mwv[:, :, n * 512:(n + 1) * 512], [], [("MW", q)])
                    dma("sp", "pl3%d" % q, MB[:, q, :], mb[n * 512:(n + 1) * 512].partition_broadcast(128), [], [("MB", q)])
                    b = nb()
                    for k in range(8):
                        mm(bank(b), SREP[:, k, :], MW[:, q, k, :], k == 0, k == 7,
                           [("SREP",), ("MW", q)], [("ps", b)])
                    tt_("dve", MROW[:, q, :], bank(b), MB[:, q, :], ALU.add, [("ps", b), ("MB", q)], [("MROW", q)])
                    dma("act", "pl4%d" % q, modscr[s:s + 1, n * 512:(n + 1) * 512], MROW[0:1, q, :], [("MROW", q)], [("modscr", s, n)])

            emit_cvts(28 if mode == "F" else None)
            cll = [0]

            def colload(dst_ap, vec, key):
                cll[0] += 1
                dma("sp", "pl5", dst_ap, vec.rearrange("(k p) -> p k", p=128), [("modscr", key[1], n_) for n_ in range(6)], [key], nonc=True)

            lnp = {1: (I["post_mix_g"][0], I["post_mix_b"][0]), 2: (I["post_mlp_g"][0], I["post_mlp_b"][0]),
                   3: (I["post_mix_g"][1], I["post_mix_b"][1])}
            for (s, _, _) in mods:
                G2 = CV[:, s * 16:s * 16 + 8]
                B2 = CV[:, s * 16 + 8:s * 16 + 16]
                TMPC = TMPC4[:, s, :]
                colload(TMPC[:, 0:8], modscr[s, 0:D], ("TMPC", s, 0))
                colload(TMPC[:, 8:16], modscr[s, D:2 * D], ("TMPC", s, 1))
                ts_("dve", TMPC[:, 8:16], TMPC[:, 8:16], 1.0, None, ALU.add, None, [("TMPC", s, 1)], [("TMPC", s, 1)])
                if s == 0:
                    cp("dve", G2, TMPC[:, 8:16], [("TMPC", s, 1)], [("CV", s)])
                    cp("dve", B2, TMPC[:, 0:8], [("TMPC", s, 0)], [("CV", s)])
                else:
                    colload(TMPC[:, 16:24], lnp[s][0], ("TMPC", s, 2))
                    colload(TMPC[:, 24:32], lnp[s][1], ("TMPC", s, 3))
                    tt_("dve", G2, TMPC[:, 16:24], TMPC[:, 8:16], ALU.mult, [("TMPC", s, 2), ("TMPC", s, 1)], [("CV", s)])
                    tt_("dve", TMPC[:, 32:40], TMPC[:, 24:32], TMPC[:, 8:16], ALU.mult, [("TMPC", s, 3), ("TMPC", s, 1)], [("TMPC", s, 4)])
                    tt_("dve", B2, TMPC[:, 32:40], TMPC[:, 0:8], ALU.add, [("TMPC", s, 4), ("TMPC", s, 0)], [("CV", s)])
            sch.flush()

        def load_bc(slot, vec, lane="bc"):
            dma("sp", "bc%d" % slot, BC[:, slot, :], vec.partition_broadcast(128),
                [("modscr", s_, n_) for s_ in range(4) for n_ in range(6)], [("BC", slot)])

        def to_hT(s, halo):
            for c in range(8):
                b = nb()
                for t in range(4):
                    mm(bank(b)[:, t * 128:(t + 1) * 128], XB[:, 1 + t, c * 128:(c + 1) * 128], IDB[:, :], True, True,
                       [("XB", 1 + t), ("IDB",)], [("ps", b)], signal=(t == 3))
                act(HT[:, c, HALO:TT], bank(b), AF.Identity, [("ps", b), ("CV", s)], [("HT", c)],
                    bias=CV[:, s * 16 + 8 + c:s * 16 + 9 + c], scale=CV[:, s * 16 + c:s * 16 + c + 1])
            if halo:
                b = nb()
                for c in range(8):
                    mm(bank(b)[:, c * 32:(c + 1) * 32], XB[0:32, 0, c * 128:(c + 1) * 128], IDB[0:32, 0:32], True, True,
                       [("XB", 0), ("IDB",)], [("ps", b)], signal=(c == 7))
                for c in range(8):
                    act(HT[:, c, 0:HALO], bank(b)[:, c * 32:(c + 1) * 32], AF.Identity, [("ps", b), ("CV", s)], [("HT", c)],
                        bias=CV[:, s * 16 + 8 + c:s * 16 + 9 + c], scale=CV[:, s * 16 + c:s * 16 + c + 1])

        def epilogue(t, pb, res_ap, res_key, bcs, out_ap, out_key, znb):
            zb = t % 2
            P = PS[pb // 2][:, :]
            kz, kn = ("Z", zb), ("ZN", zb)
            tt_("dve", Z[:, zb, :], P, BC[:, bcs, :], ALU.mult, [("ps", pb), ("ps", pb + 1), ("BC", bcs)], [kz])
            stt(Z[:, zb, :], res_ap, ALPHA, Z[:, zb, :], ALU.mult, ALU.add, [res_key, kz], [kz])
            for hh in range(2):
                sch.op("dve", lambda e, hh=hh: e.bn_stats(out=STt[:, zb, hh * 6:hh * 6 + 6], in_=Z[:, zb, hh * 512:(hh + 1) * 512]),
                       [kz], [("ST", zb)])
            sch.op("dve", lambda e: e.bn_aggr(out=MV[:, zb, 0:2], in_=STt[:, zb, :]), [("ST", zb)], [("MV", zb)])
            ts_("dve", MV[:, zb, 1:2], MV[:, zb, 1:2], LN_EPS, None, ALU.add, None, [("MV", zb)], [("MV", zb)])
            tt_("pool", MV[:, zb, 2:3], MV[:, zb, 1:2], NHALF[:, 0:1], ALU.pow, [("MV", zb), ("NHALF",)], [("MV", zb)])
            stt(MV[:, zb, 3:4], MV[:, zb, 0:1], -1.0, MV[:, zb, 2:3], ALU.mult, ALU.mult, [("MV", zb)], [("MV", zb)])
            if znb:
                act(XB[:, 1 + t, :], Z[:, zb, :], AF.Identity, [kz, ("MV", zb)], [("XB", 1 + t)],
                    bias=MV[:, zb, 3:4], scale=MV[:, zb, 2:3])
            if out_ap is None:
                return lambda: None
            act(ZN[:, zb, :], Z[:, zb, :], AF.Identity, [kz, ("MV", zb)], [kn],
                bias=MV[:, zb, 3:4], scale=MV[:, zb, 2:3])

            def tail():
                tt_("pool", out_ap, ZN[:, zb, :], BC[:, bcs + 1, :], ALU.mult, [kn, ("BC", bcs + 1)], [out_key])
                tt_("pool", out_ap, out_ap, BC[:, bcs + 2, :], ALU.add, [out_key, ("BC", bcs + 2)], [out_key])
            return tail

        def mlp(ws, s, w1n, w2n):
            to_hT(s, False)
            for f in range(8):
                slot = ws.get((w1n, f))
                for fl in range(4):
                    fc = 4 * f + fl
                    b = nb()
                    for k in range(8):
                        mm(bank(b), WR[:, slot, k * 512 + fl * 128:k * 512 + fl * 128 + 128], HT[:, k, HALO:TT], k == 0, k == 7,
                           [("WR", slot), ("HT", k)], [("ps", b)])
                    rb = fc % 2
                    act(RT[:, rb, :], bank(b), AF.Relu, [("ps", b)], [("RT", rb)])
                    tt_("pool", HID[:, fc, :], RT[:, rb, :], RT[:, rb, :], ALU.mult, [("RT", rb)], [("HID", fc)])
                ws.release()
            for c in range(8):
                slot = ws.get((w2n, c))
                for t in range(4):
                    for half in range(2):
                        b = 2 * t + half
                        for fl in range(4):
                            mm(bank(b), HID[:, 4 * c + fl, t * 128:(t + 1) * 128],
                               WR[:, slot, fl * 1024 + half * 512:fl * 1024 + half * 512 + 512],
                               c == 0 and fl == 0, c == 7 and fl == 3,
                               [("HID", 4 * c + fl), ("WR", slot)], [("ps", b)], signal=(fl == 3))
                ws.release()
            bptr[0] = 0

        if doA:
            ast = ExitStack()
            with ast:
                U = sb("U", [128, 2, TT], BF16, ast)
                NDG = 8
                DG = sb("DG", [128, NDG, 128], BF16, ast)
                dgc = [0]
                SG = sb("SG", [128, 2, TT], F32, ast)
                ACC = sb("ACC", [128, 8, GR], F32, ast)
                SQ = sb("SQ", [128, 2, GR], F32, ast)
                UN = sb("UN", [128, 2, GR], F32, ast)
                VT = sb("VT", [128, 8, GR], BF16, ast)
                STB = sb("STB", [128, 4, GR], F32, ast)
                QS = sb("QS", [128, 2, GR], BF16, ast)
                VS = sb("VS", [128, 2, D], BF16, ast)
                load_bc(0, modscr[0, 2 * D:3 * D])
                load_bc(1, I["post_mix_g"][0])
                load_bc(2, I["post_mix_b"][0])
                load_bc(3, modscr[1, 2 * D:3 * D])
                load_bc(4, I["post_mlp_g"][0])
                load_bc(5, I["post_mlp_b"][0])

                seqA = []
                for pp in range(2 * ng):
                    seqA += [("pw1", j) for j in range(4)] + [("pw2", n) for n in range(2)]
                    seqA += [("w1_0", f) for f in range(8)] + [("w2_0", c) for c in range(8)]
                    seqA += [("qkv", n) for n in range(0 if pp % 2 == 0 else 2, 6)]
                ws = WStream(seqA)

                def load_x(g):
                    dma("sp", "xl0", X[0:HALO, 0, :], I["xs"][g, 0:HALO, :], [], [("X", 0)])
                    dma("sp", "xl1", X[:, 1:5, :], I["xs"][g, HALO:TT, :].rearrange("(t p) d -> p t d", p=128),
                        [], [("X", 1), ("X", 2), ("X", 3), ("X", 4)])

                load_x(0)
                ws.prefetch()
                for pp in range(2 * ng):
                    g = pp // 2
                    own = (pp % 2 == 0)
                    kvl = 0 if own else 1
                    cp("dve", XB[0:HALO, 0, :], X[0:HALO, 0, :], [("X", 0)], [("XB", 0)])
                    for t in range(4):
                        cp("dve" if t % 2 else "act", XB[:, 1 + t, :], X[:, 1 + t, :], [("X", 1 + t)], [("XB", 1 + t)])
                    to_hT(0, True)
                    if debug and pp == 0:
                        dma("sp", "dbg", dbg_ht[:, :, :], HT[:, :, :], [("HT", c) for c in range(8)], [("dbg", 0)])
                        dma("sp", "dbg", dbg_cv[:, :], CV[:, :], [("CV", 0)], [("dbg", 4)])
                    nbmod[0] = 6
                    bptr[0] = 0
                    pw1_slot = {}
                    pw1_banks = {}

                    def pw1_glu(i):
                        j, s2 = i // 2, i % 2
                        if s2 == 0:
                            pw1_slot[j] = ws.get(("pw1", j))
                        slot = pw1_slot[j]
                        ub = i % 2
                        ba, bg, bh = nb(), nb(), nb()
                        for (bk, col0) in ((ba, 128 * s2), (bg, 256 + 128 * s2)):
                            for k in range(8):
                                mm(bank(bk), WR[:, slot, k * 512 + col0:k * 512 + col0 + 128], HT[:, k, HALO:TT], k == 0, k == 7,
                                   [("WR", slot), ("HT", k)], [("ps", bk)])
                        for hi, col0 in enumerate((128 * s2, 256 + 128 * s2)):
                            for k in range(8):
                                mm(bank(bh)[:, hi * 32:hi * 32 + 32], WR[:, slot, k * 512 + col0:k * 512 + col0 + 128], HT[:, k, 0:HALO],
                                   k == 0, k == 7, [("WR", slot), ("HT", k)], [("ps", bh)], signal=(k == 7 and hi == 1))
                        if s2 == 1:
                            ws.release()
                        pw1_banks[i] = (ba, bg, bh)

                    def glu_elem(i):
                        ub = i % 2
                        ba, bg, bh = pw1_banks[i]
                        act(SG[:, ub, HALO:TT], bank(bg), AF.Sigmoid, [("ps", bg), ("CA", 0), ("CA", 1), ("CA", 2), ("CA", 3), ("CA", 4), ("CAt",)], [("SG", ub)], bias=CA[:, 8 + i:9 + i])
                        act(SG[:, ub, 0:HALO], bank(bh)[:, 32:64], AF.Sigmoid, [("ps", bh), ("CA", 0), ("CA", 1), ("CA", 2), ("CA", 3), ("CA", 4), ("CAt",)], [("SG", ub)], bias=CA[:, 8 + i:9 + i])
                        stt(U[:, ub, HALO:TT], bank(ba), CA[:, i:i + 1], SG[:, ub, HALO:TT], ALU.add, ALU.mult,
                            [("ps", ba), ("SG", ub), ("CA", 0), ("CA", 1), ("CA", 2), ("CA", 3), ("CA", 4), ("CAt",)], [("U", ub)])
                        stt(U[:, ub, 0:HALO], bank(bh)[:, 0:32], CA[:, i:i + 1], SG[:, ub, 0:HALO], ALU.add, ALU.mult,
                            [("ps", bh), ("SG", ub), ("CA", 0), ("CA", 1), ("CA", 2), ("CA", 3), ("CA", 4), ("CAt",)], [("U", ub)])
                        ts_("dve", U[:, ub, 0:HALO], U[:, ub, 0:HALO], HVt[:, pp:pp + 1], None, ALU.mult, None,
                            [("U", ub), ("HV",)], [("U", ub)])

                    def dwconv(i):
                        ub = i % 2
                        bc_ = nb()
                        for jt in range(CONVW):
                            dgs = dgc[0] % NDG
                            dgc[0] += 1
                            if jt % 2 == 0:
                                ts_("dve", DG[:, dgs, :], IDB[:, :], CA[:, 40 + jt * 8 + i:41 + jt * 8 + i], None, ALU.mult, None,
                                    [("IDB",), ("CA", 0), ("CA", 1), ("CA", 2), ("CA", 3), ("CA", 4), ("CAt",)], [("DG", dgs)])
                            else:
                                act(DG[:, dgs, :], IDB[:, :], AF.Copy, [("IDB",), ("CA", 0), ("CA", 1), ("CA", 2), ("CA", 3), ("CA", 4), ("CAt",)], [("DG", dgs)],
                                    scale=CA[:, 40 + jt * 8 + i:41 + jt * 8 + i])
                            mm(bank(bc_), DG[:, dgs, :], U[:, ub, 2 + jt:2 + jt + GR], jt == 0, jt == CONVW - 1,
                               [("DG", dgs), ("U", ub)], [("ps", bc_)], signal=True)
                        act(ACC[:, i, :], bank(bc_), AF.Identity, [("ps", bc_), ("CA", 0), ("CA", 1), ("CA", 2), ("CA", 3), ("CA", 4), ("CAt",)], [("ACC", i)], bias=CA[:, 16 + i:17 + i])
                        act(SQ[:, ub, :], ACC[:, i, :], AF.Square, [("ACC", i)], [("SQ", ub)])
                        mm(bank(6), ONES[:, :], ACC[:, i, :], i == 0, i == 7, [("ONES",), ("ACC", i)], [("ps", 6)], signal=True)
                        mm(bank(7), ONES[:, :], SQ[:, ub, :], i == 0, i == 7, [("ONES",), ("SQ", ub)], [("ps", 7)], signal=True)

                    pw1_glu(0)
                    glu_elem(0)
                    for i in range(8):
                        if i + 1 < 8:
                            pw1_glu(i + 1)
                        dwconv(i)
                        if i + 1 < 8:
                            glu_elem(i + 1)
                    bm, bq = 6, 7
                    nbmod[0] = 8
                    bptr[0] = 0
                    cp("dve", STB[:, 0, :], bank(bm), [("ps", bm)], [("STB", 0)])
                    tt_("dve", STB[:, 3, :], STB[:, 0, :], STB[:, 0, :], ALU.mult, [("STB", 0)], [("STB", 3)])
                    tt_("dve", STB[:, 1, :], bank(bq), STB[:, 3, :], ALU.subtract, [("ps", bq), ("STB", 3)], [("STB", 1)])
                    act(STB[:, 1, :], STB[:, 1, :], AF.Ln, [("STB", 1)], [("STB", 1)], bias=LN_EPS)
                    act(STB[:, 1, :], STB[:, 1, :], AF.Exp, [("STB", 1)], [("STB", 1)], scale=-0.5)
                    stt(STB[:, 2, :], STB[:, 0, :], -1.0, STB[:, 1, :], ALU.mult, ALU.mult, [("STB", 0), ("STB", 1)], [("STB", 2)])
                    for i in range(8):
                        ub = i % 2
                        tt_("dve", UN[:, ub, :], ACC[:, i, :], STB[:, 1, :], ALU.mult, [("ACC", i), ("STB", 1)], [("UN", ub)])
                        tt_("dve", UN[:, ub, :], UN[:, ub, :], STB[:, 2, :], ALU.add, [("UN", ub), ("STB", 2)], [("UN", ub)])
                        act(SQ[:, ub, :], UN[:, ub, :], AF.Identity, [("UN", ub), ("CA", 0), ("CA", 1), ("CA", 2), ("CA", 3), ("CA", 4), ("CAt",)], [("SQ", ub)],
                            bias=CA[:, 32 + i:33 + i], scale=CA[:, 24 + i:25 + i])
                        act(RT[:, ub, :], UN[:, ub, :], AF.Sigmoid, [("UN", ub), ("CA", 0), ("CA", 1), ("CA", 2), ("CA", 3), ("CA", 4), ("CAt",)], [("RT", ub)],
                            bias=CA[:, 32 + i:33 + i], scale=CA[:, 24 + i:25 + i])
                        tt_("pool", VT[:, i, :], SQ[:, ub, :], RT[:, ub, :], ALU.mult, [("SQ", ub), ("RT", ub)], [("VT", i)])
                    if debug and pp == 0:
                        dma("sp", "dbg", dbg_acc[:, :, :], ACC[:, :, :], [("ACC", c) for c in range(8)], [("dbg", 1)])
                        dma("sp", "dbg", dbg_vt[:, :, :], VT[:, :, :], [("VT", c) for c in range(8)], [("dbg", 2)])
                    s0 = ws.get(("pw2", 0))
                    s1 = ws.get(("pw2", 1))
                    for t in range(4):
                        pb = nb2()
                        for half, slot in ((0, s0), (1, s1)):
                            b = pb + half
                            mm(bank(b), ONEB[0:1, :], PB2[0:1, half * 512:(half + 1) * 512], True, False,
                               [("ONEB",), ("PB2",)], [("ps", b)])
                            for k in range(8):
                                mm(bank(b), VT[:, k, t * 128:(t + 1) * 128], WR[:, slot, k * 512:(k + 1) * 512], False, k == 7,
                                   [("VT", k), ("WR", slot)], [("ps", b)])
                        if t == 3:
                            ws.release(2)
                        tl = epilogue(t, pb, X[:, 1 + t, :], ("X", 1 + t), 0, X[:, 1 + t, :], ("X", 1 + t), True)
                        if t > 0:
                            prev_tail()
                        prev_tail = tl
                    prev_tail()
                    if debug and pp == 0:
                        dma("sp", "dbg", dbg_x0[:, :, :], X[:, 1:5, :], [("X", 1 + t) for t in range(4)], [("dbg", 3)])
                    mlp(ws, 1, "w1_0", "w2_0")
                    tails = []
                    for t in range(4):
                        if own:
                            tl = epilogue(t, 2 * t, X[:, 1 + t, :], ("X", 1 + t), 3, XO[:, t % 2, :], ("XO", t % 2), True)

                            def fin(t=t, tl=tl):
                                tl()
                                dma("sp", "xo%d" % (t % 2), x1_scr[g, t * 128:(t + 1) * 128, :], XO[:, t % 2, :],
                                    [("XO", t % 2)], [("x1", g, t)])
                            tails.append(fin)
                            if t > 0:
                                tails[t - 1]()
                        else:
                            epilogue(t, 2 * t, X[:, 1 + t, :], ("X", 1 + t), 3, None, None, True)
                    if own:
                        tails[3]()
                    if pp + 1 < 2 * ng:
                        load_x(pp + 1)
                    emit_cvts(2)
                    to_hT(2, False)
                    for n in range(0 if own else 2, 4):
                        slot = ws.get(("qkv", n))
                        for cg in range(4):
                            b = nb()
                            for k in range(8):
                                mm(bank(b), WR[:, slot, k * 512 + cg * 128:k * 512 + cg * 128 + 128], HT[:, k, HALO:TT], k == 0, k == 7,
                                   [("WR", slot), ("HT", k)], [("ps", b)])
                            qb = (n * 4 + cg) % 2
                            if n < 2:
                                act(QS[:, qb, :], bank(b), AF.Copy, [("ps", b)], [("QS", qb)], scale=0.125)
                            else:
                                cp("dve", QS[:, qb, :], bank(b), [("ps", b)], [("QS", qb)])
                            r0 = ((n % 2) * 8 + cg * 2) * 64
                            if n < 2:
                                dst = qt_scr[r0:r0 + 128, g * GR:(g + 1) * GR]
                            else:
                                dst = kv_all[kvl * 2048 + r0:kvl * 2048 + r0 + 128, g * GR:(g + 1) * GR]
                            dma("sp", "qs%d" % qb, dst, QS[:, qb, :], [("QS", qb)],
                                [("qk", n, cg, g)] if n < 2 else [("kw", kvl, n, cg, g)])
                        ws.release()
                    sv0 = ws.get(("qkv", 4))
                    sv1 = ws.get(("qkv", 5))
                    Vv = kv_all[kvl * 2048 + 1024:kvl * 2048 + 2048, :].rearrange("r (a c) -> (r a) c", a=4)
                    for t in range(4):
                        pb = nb2()
                        for half, slot in ((0, sv0), (1, sv1)):
                            b = pb + half
                            for k in range(8):
                                mm(bank(b), HT[:, k, HALO + t * 128:HALO + (t + 1) * 128], WR[:, slot, k * 512:(k + 1) * 512], k == 0, k == 7,
                                   [("HT", k), ("WR", slot)], [("ps", b)])
                        if t == 3:
                            ws.release(2)
                        vb = t % 2
                        cp("act" if t % 2 else "dve", VS[:, vb, :], PS[pb // 2][:, :], [("ps", pb), ("ps", pb + 1)], [("VS", vb)])
                        dma("sp", "vs%d" % vb, Vv[g * GR + t * 128:g * GR + (t + 1) * 128, :], VS[:, vb, :], [("VS", vb)], [("vw", kvl, g, t)])
                emit_cvts()
                lanesA = ["xo0", "xo1", "qs0", "qs1", "vs0", "vs1"]
                sch.wait_all("sp", lanesA)
                sch.flush()

        if doB:
            bst = ExitStack()
            with bst:
                QT = sb("QT", [128, 2, 2, GR], BF16, bst)
                KT = sb("KT", [128, 3, 2, GR], BF16, bst)
                VA = sb("VA", [128, 3, 4, 132], BF16, bst)
                PT = sb("PT", [128, 2, 1024], BF16, bst)
                BA = sb("BA", [128, 2, 2304], BF16, bst)
                CH = sb("CH", [128, NH], F32, bst)
                LM = sb("LM", [128, 4, 64], F32, bst)
                LS = sb("LS", [128, 8], F32, bst)
                GSUB = sb("GSUB", [128, 128], F32, bst)
                OS = sb("OS", [128, 2, 128], F32, bst)
                SM = sb("SM", [128, 2, 8], F32, bst)
                OJ = sb("OJ", [128, 128], F32, bst)
                OAC = sb("OAC", [128, 1032], F32, bst)

                dma("sp", "pl6", CH[:, :], I["ch"], [], [("CH",)])
                for q, nm in enumerate(("attn_lam_q1", "attn_lam_k1", "attn_lam_q2", "attn_lam_k2")):
                    dma("sp", "pl7", LM[:, q, :], I[nm].partition_broadcast(128), [], [("LM", q)])
                dma("sp", "pl7", GSUB[:, :], I["attn_subln_g"].partition_broadcast(128), [], [("GSUB",)])
                ts_("dve", GSUB[:, :], GSUB[:, :], 1.0 - LAMBDA_INIT, None, ALU.mult, None, [("GSUB",)], [("GSUB",)])
                for q in range(2):
                    stt(LM[:, 2 * q, :], LM[:, 2 * q, :], 1.0, LM[:, 2 * q + 1, :], ALU.mult, ALU.mult,
                        [("LM", 2 * q), ("LM", 2 * q + 1)], [("LM", 2 * q)], accum=LS[:, q:q + 1])
                act(LS[:, 2:4], LS[:, 0:2], AF.Exp, [("LM", 0), ("LM", 2)], [("LS",)])
                tt_("dve", LS[:, 4:5], LS[:, 2:3], LS[:, 3:4], ALU.subtract, [("LS",)], [("LS",)])
                ts_("dve", LS[:, 5:6], LS[:, 4:5], LAMBDA_INIT, -1.0, ALU.add, ALU.mult, [("LS",)], [("LS",)])
                sch.op("dve", lambda e: e.memset(VA[:, :, :, 128:129], 1.0), [], [("VA", 0), ("VA", 1), ("VA", 2)])
                sch.op("dve", lambda e: e.memset(QT[64:128, :, :, :], 0.0), [], [("QT", 0), ("QT", 1)])
                sch.op("dve", lambda e: e.memset(KT[64:128, :, :, :], 0.0), [], [("KT", 0), ("KT", 1), ("KT", 2)])
                load_bc(0, modscr[2, 2 * D:3 * D])
                load_bc(1, I["post_mix_g"][1])
                load_bc(2, I["post_mix_b"][1])
                load_bc(3, modscr[3, 2 * D:3 * D])
                load_bc(4, I["post_mlp_g"][1])
                load_bc(5, I["post_mlp_b"][1])

                seqB = []
                for g in range(ng):
                    seqB += [("wo", n) for n in range(2)] + [("w1_1", f) for f in range(8)] + [("w2_1", c) for c in range(8)]
                ws = WStream(seqB)

                Vall = [kv_all[r * 2048 + 1024:r * 2048 + 2048, :].rearrange("r (a c) -> (r a) c", a=4) for r in range(2)]

                def acc_ap(m, qs):
                    a = m * 4 + qs
                    bk, sl = a // 3, a % 3
                    if bk < 2:
                        return PS[2][:, bk * 512 + sl * 129:bk * 512 + sl * 129 + 129], ("ps", 4 + bk)
                    return PS[3][:, sl * 129:sl * 129 + 129], ("ps", 6)

                kvc = [0]
                hcount = [0]
                for j in range(ng):
                    dma("sp", "xl1", X[:, 1:5, :], x1_scr[j].rearrange("(t p) d -> p t d", p=128),
                        [("x1", j, t) for t in range(4)], [("X", 1), ("X", 2), ("X", 3), ("X", 4)])
                    for h in range(NH):
                        hb = hcount[0] % 2
                        hcount[0] += 1

                        def load_head(jj, hh, hbb):
                            dma("sp", "qt%d" % hbb, QT[0:64, hbb, :, :],
                                qt_scr[2 * hh * 64:2 * hh * 64 + 128, jj * GR:(jj + 1) * GR].rearrange("(m d) t -> d m t", m=2),
                                [("qk", n, cg, jj) for n in range(2) for cg in range(4)], [("QT", hbb)])
                            dma("pool", "ba%d" % hbb, BA[:, hbb, :], I["biasarr"][hh], [], [("BA", hbb)])

                        if j == 0 and h == 0:
                            load_head(0, 0, hb)
                        if h + 1 < NH:
                            load_head(j, h + 1, 1 - hb)
                        elif j + 1 < ng:
                            load_head(j + 1, 0, 1 - hb)
                        blocks = []
                        for i in range(j + 1):
                            for r in range(2):
                                for kb in range(4):
                                    sp_off = None
                                    if i == j:
                                        sp_off = (0 if r == 0 else 896) + 384 - kb * 128
                                    elif i == j - 1 and r == 1 and kb == 3:
                                        sp_off = 1792
                                    blocks.append((r, i, kb, sp_off))
                        kvslot = {}

                        def load_kv(r, i):
                            sl = kvc[0] % 3
                            kvc[0] += 1
                            kvslot[(r, i)] = sl
                            dma("sp", "kt%d" % sl, KT[0:64, sl, :, :],
                                kv_all[r * 2048 + 2 * h * 64:r * 2048 + 2 * h * 64 + 128, i * GR:(i + 1) * GR].rearrange("(m d) t -> d m t", m=2),
                                [], [("KT", sl)])
                            dma("sp", "va%d" % sl, VA[:, sl, :, 0:128],
                                Vall[r][i * GR:(i + 1) * GR, h * 128:(h + 1) * 128].rearrange("(kb p) e -> p kb e", p=128),
                                [], [("VA", sl)], nonc=True)

                        def qk(n):
                            r, i, kb, _ = blocks[n]
                            if (r, i) not in kvslot:
                                load_kv(r, i)
                            sl = kvslot[(r, i)]
                            sbuf_ = n % 2
                            spo = blocks[n][3]
                            for m in range(2):
                                b = 2 * sbuf_ + m
                                mm(bank(b), KT[:, sl, m, kb * 128:(kb + 1) * 128], QT[:, hb, m, :], True, spo is None,
                                   [("KT", sl), ("QT", hb)], [("ps", b)], signal=(m == 1 and spo is None))
                                if spo is not None:
                                    mm(bank(b), IDB[:, :], BA[:, hb, spo:spo + 512], False, True,
                                       [("IDB",), ("BA", hb)], [("ps", b)], signal=(m == 1))

                        qk(0)
                        nblk = len(blocks)
                        for n in range(nblk):
                            r, i, kb, sp_off = blocks[n]
                            if n + 1 < nblk:
                                qk(n + 1)
                            sbuf_ = n % 2
                            sl = kvslot[(r, i)]
                            b0 = 2 * sbuf_
                            if sp_off is not None:
                                act(PT[:, sbuf_, :], PS[sbuf_][:, :], AF.Exp, [("ps", b0), ("ps", b0 + 1)], [("PT", sbuf_)])
                            else:
                                act(PT[:, sbuf_, :], PS[sbuf_][:, :], AF.Exp, [("ps", b0), ("ps", b0 + 1), ("CH",)], [("PT", sbuf_)],
                                    bias=CH[:, h:h + 1])
                            for m in range(2):
                                for qs in range(4):
                                    ap_, key_ = acc_ap(m, qs)
                                    mm(ap_, PT[:, sbuf_, m * 512 + qs * 128:m * 512 + qs * 128 + 128], VA[:, sl, kb, 0:129],
                                       n == 0 and (m * 4 + qs) % 3 == 0, n == nblk - 1, [("PT", sbuf_), ("VA", sl)], [key_],
                                       signal=(m == 1 and qs == 3))
                        cp("dve", OAC[:, 0:387], PS[2][:, 0:387], [("ps", 4)], [("OAC", 0)])
                        cp("dve", OAC[:, 387:774], PS[2][:, 512:899], [("ps", 5)], [("OAC", 1)])
                        cp("dve", OAC[:, 774:1032], PS[3][:, 0:258], [("ps", 6)], [("OAC", 2)])
                        for qs in range(4):
                            ob = qs % 2
                            i1, i2 = qs, 4 + qs
                            a1, k1 = OAC[:, i1 * 129:(i1 + 1) * 129], ("OAC", i1 // 3)
                            a2, k2 = OAC[:, i2 * 129:(i2 + 1) * 129], ("OAC", i2 // 3)
                            ko, ks = ("OS", ob), ("SM", ob)
                            sch.op("dve", lambda e, a1=a1, ob=ob: e.reciprocal(out=SM[:, ob, 0:1], in_=a1[:, 128:129]), [k1], [ks])
                            sch.op("dve", lambda e, a2=a2, ob=ob: e.reciprocal(out=SM[:, ob, 1:2], in_=a2[:, 128:129]), [k2], [ks])
                            ts_("dve", SM[:, ob, 2:3], SM[:, ob, 1:2], LS[:, 5:6], None, ALU.mult, None, [ks, ("LS",)], [ks])
                            ts_("dve", OS[:, ob, :], a1[:, 0:128], SM[:, ob, 0:1], None, ALU.mult, None, [k1, ks], [ko])
                            stt(OS[:, ob, :], a2[:, 0:128], SM[:, ob, 2:3], OS[:, ob, :], ALU.mult, ALU.add, [k2, ks, ko], [ko])
                            stt(OJ[:, :], OS[:, ob, :], 1.0, OS[:, ob, :], ALU.mult, ALU.mult, [ko], [("OJ",)], accum=SM[:, ob, 3:4])
                            ts_("dve", SM[:, ob, 4:5], SM[:, ob, 3:4], 1.0 / 128.0, LN_EPS, ALU.mult, ALU.add, [("OJ",), ks], [ks])
                            tt_("pool", SM[:, ob, 5:6], SM[:, ob, 4:5], NHALF[:, 0:1], ALU.pow, [ks, ("NHALF",)], [ks])
                            stt(XB[:, 1 + qs, h * 128:(h + 1) * 128], OS[:, ob, :], SM[:, ob, 5:6], GSUB[:, :], ALU.mult, ALU.mult,
                                [ko, ks, ("GSUB",)], [("XB", 1 + qs)])
                    if debug and j == 0:
                        dma("sp", "dbg", dbg_at[:, :, :], XB[:, 1:5, :], [("XB", 1 + t) for t in range(4)], [("dbg", 5)])
                    bptr[0] = 7
                    for c in range(8):
                        b = 7
                        for t in range(4):
                            mm(bank(b)[:, t * 128:(t + 1) * 128], XB[:, 1 + t, c * 128:(c + 1) * 128], IDB[:, :], True, True,
                               [("XB", 1 + t), ("IDB",)], [("ps", b)], signal=(t == 3))
                        cp("act" if c % 2 else "dve", HT[:, c, HALO:TT], bank(b), [("ps", b)], [("HT", c)])
                    bptr[0] = 0
                    s0 = ws.get(("wo", 0))
                    s1 = ws.get(("wo", 1))
                    for t in range(4):
                        pb = nb2()
                        for half, slot in ((0, s0), (1, s1)):
                            b = pb + half
                            for k in range(8):
                                mm(bank(b), HT[:, k, HALO + t * 128:HALO + (t + 1) * 128], WR[:, slot, k * 512:(k + 1) * 512], k == 0, k == 7,
                                   [("HT", k), ("WR", slot)], [("ps", b)])
                        if t == 3:
                            ws.release(2)
                        tl = epilogue(t, pb, X[:, 1 + t, :], ("X", 1 + t), 0, X[:, 1 + t, :], ("X", 1 + t), True)
                        if t > 0:
                            prev_tail()
                        prev_tail = tl
                    prev_tail()
                    if debug and j == 0:
                        dma("sp", "dbg", dbg_xq[:, :, :], X[:, 1:5, :], [("X", 1 + t) for t in range(4)], [("dbg", 6)])
                    mlp(ws, 3, "w1_1", "w2_1")
                    tails = []
                    for t in range(4):
                        tl = epilogue(t, 2 * t, X[:, 1 + t, :], ("X", 1 + t), 3, XO[:, t % 2, :], ("XO", t % 2), False)

                        def fin(t=t, tl=tl):
                            tl()
                            dma("sp", "xo%d" % (t % 2), out[j, t * 128:(t + 1) * 128, :], XO[:, t % 2, :],
                                [("XO", t % 2)], [("out", j, t)])
                        tails.append(fin)
                        if t > 0:
                            tails[t - 1]()
                    tails[3]()
                sch.wait_all("sp", ["xo0", "xo1"])
                sch.flush()
    return nc


def _t5_bucket(rel):
    n = np.maximum(rel, 0)
    nf = np.maximum(n, 1).astype(np.float32)
    large = 16 + (np.log(nf / np.float32(16)) / np.float32(math.log(128 / 16)) * np.float32(16)).astype(np.int32)
    large = np.minimum(large, 31)
    return np.where(n < 16, n, large)


def _bias_arrays(rel_bias, role):
    p = np.arange(128)[:, None]
    out = np.empty((NH, 128, 2304), np.float32)

    def fill(width, base_rel):
        j = np.arange(width)[None, :]
        rel = j - p + base_rel
        bk = _t5_bucket(rel)
        v = rel_bias[bk]
        v = np.where((rel >= 0)[:, :, None], v, np.float32(NEG))
        return np.transpose(v, (2, 0, 1))

    if role == 0:
        dA, dB, relC = 0, -512, 128
    else:
        dA, dB, relC = 0, 512, 1152
    out[:, :, 0:896] = fill(896, -384 + dA)
    out[:, :, 896:1792] = fill(896, -384 + dB)
    out[:, :, 1792:2304] = fill(512, relC)
    return out


_CACHE = {}


def _get(mode):
    if mode not in _CACHE:
        _CACHE[mode] = build(mode)
    return _CACHE[mode]


def _core_inputs(inputs, c):
    b, r = c // 2, c % 2
    x = inputs["x"]
    d = {}
    for name, shape in W_INPUTS:
        a = np.ascontiguousarray(inputs[name], dtype=np.float32).reshape(shape)
        d[name] = a
    d["cvec"] = np.ascontiguousarray(inputs["c"][b])
    d["ident"] = np.eye(128, dtype=np.float32)
    xs = np.zeros((2 * NG, TT, D), np.float32)
    hv = np.ones((128, 2 * NG), np.float32)
    for pp in range(2 * NG):
        i = pp // 2
        G = 2 * i + (r if pp % 2 == 0 else 1 - r)
        lo = G * GR - HALO
        if lo < 0:
            xs[pp, HALO:] = x[b, 0:GR]
            hv[:, pp] = 0.0
        else:
            xs[pp] = x[b, lo:lo + TT]
    d["xs"] = xs
    d["hv"] = hv
    d["biasarr"] = _bias_arrays(np.asarray(inputs["rel_bias"], np.float32), r)
    d["ch"] = np.ascontiguousarray(np.broadcast_to(np.asarray(inputs["rel_bias"], np.float32)[31][None, :], (128, NH)))
    return d


FUSED = True


def kernel(**inputs):
    inputs = {k: np.asarray(v) for k, v in inputs.items()}
    cores = list(range(8))
    per = [_core_inputs(inputs, c) for c in cores]
    if FUSED:
        nc = _get("F")
        keys = [t for t in per[0].keys()]
        res = run_bass_kernel_spmd(nc, per, core_ids=cores)
        outs = [r["out"] for r in res.results]
    else:
        ncA = _get("A")
        inA = [{k: v for k, v in p.items() if k not in ("biasarr", "ch")} for p in per]
        resA = run_bass_kernel_spmd(ncA, inA, core_ids=cores).results
        ncB = _get("B")
        inB = []
        for c in cores:
            p = {k: v for k, v in per[c].items() if k not in ("xs", "hv")}
            p["kv_all"] = resA[c]["kv_all"]
            p["x1_scr"] = resA[c]["x1_scr"]
            p["qt_scr"] = resA[c]["qt_scr"]
            inB.append(p)
        resB = run_bass_kernel_spmd(ncB, inB, core_ids=cores).results
        outs = [r["out"] for r in resB]
    y = np.empty((NB, SEQ, D), np.float32)
    for c in cores:
        b, r = c // 2, c % 2
        o = np.asarray(outs[c]).reshape(NG, GR, D)
        for i in range(NG):
            G = 2 * i + r
            y[b, G * GR:(G + 1) * GR] = o[i]
    return y
```

```python
import math
from contextlib import ExitStack

import numpy as np
import concourse.bass as bass
import concourse.mybir as mybir
from concourse.bass_utils import run_bass_kernel_spmd

F32 = mybir.dt.float32
BF16 = mybir.dt.bfloat16
AF = mybir.ActivationFunctionType
ALU = mybir.AluOpType

D = 1024
SEQ = 8192
NB = 4
DFF = 4096
GR = 512
HALO = 32
TT = GR + HALO
NG = 8
NH = 8
ALPHA = 4.0 ** 0.25
LN_EPS = 1e-5
LAMBDA_INIT = 0.8 - 0.6 * math.exp(-0.3 * 1)
CONVW = 31
NEG = -30000.0

ENGS = ("pe", "act", "dve", "pool", "sp")


class Sched:
    def __init__(self, nc, stack, strict_same=True):
        self.nc = nc
        self.stack = stack
        self.sems = {}
        self.cnt = {}
        self.prog = {e: [] for e in ENGS}
        self.seen = {e: {} for e in ENGS}
        self.last_w = {}
        self.readers = {}
        self.strict_same = strict_same
        for e in ENGS:
            self._mk(e)

    def _mk(self, name):
        self.sems[name] = self.stack.enter_context(self.nc.semaphore("s_" + name))
        self.cnt[name] = 0

    def _deps(self, reads, writes):
        d = {}
        for k in reads:
            w = self.last_w.get(k)
            if w:
                d[w[0]] = max(d.get(w[0], 0), w[1])
        for k in writes:
            w = self.last_w.get(k)
            if w:
                d[w[0]] = max(d.get(w[0], 0), w[1])
            for e2, c in self.readers.get(k, {}).items():
                d[e2] = max(d.get(e2, 0), c)
        return d

    def _wait(self, eng, d):
        for src, c in d.items():
            if src == eng and (eng == "pe" or not self.strict_same):
                continue
            if self.seen[eng].get(src, 0) >= c:
                continue
            self.seen[eng][src] = c
            if c > self.cnt[src]:
                print("SCHED WARNING: %s waits on future signal of %s (%d > %d)" % (eng, src, c, self.cnt[src]))
            unit = 1 if src in ENGS else 16
            self.prog[eng].append(("wait", src, c * unit))

    def _record(self, src, n, reads, writes):
        for k in reads:
            self.readers.setdefault(k, {})[src] = n
        for k in writes:
            self.last_w[k] = (src, n)
            self.readers[k] = {}

    def op(self, eng, fn, reads=(), writes=(), signal=True):
        self._wait(eng, self._deps(reads, writes))
        n = self.cnt[eng] + 1
        if signal:
            self.cnt[eng] = n
        self.prog[eng].append(("op", fn, eng if signal else None, 1))
        self._record(eng, n, reads, writes)

    def dma(self, issuer, lane, fn, reads=(), writes=()):
        if lane not in self.sems:
            self._mk(lane)
        d = self._deps(reads, writes)
        if self.cnt[lane] > 0:
            d[lane] = max(d.get(lane, 0), self.cnt[lane])
        self._wait(issuer, d)
        self.cnt[lane] += 1
        self.prog[issuer].append(("op", fn, lane, 16))
        self._record(lane, self.cnt[lane], reads, writes)

    def wait_all(self, eng, lanes):
        d = {l: self.cnt[l] for l in lanes if self.cnt.get(l, 0) > 0}
        self._wait(eng, d)

    def flush(self):
        nc = self.nc
        prog = self.prog
        sems = self.sems
        self.prog = {e: [] for e in ENGS}

        def run(e, lst):
            for it in lst:
                if it[0] == "wait":
                    e.wait_ge(sems[it[1]], it[2])
                else:
                    ins = it[1](e)
                    if it[2] is not None:
                        ins.then_inc(sems[it[2]], it[3])

        with nc.Block() as block:
            @block.tensor
            def _(e):
                run(e, prog["pe"])

            @block.scalar
            def _(e):
                run(e, prog["act"])

            @block.vector
            def _(e):
                run(e, prog["dve"])

            @block.gpsimd
            def _(e):
                run(e, prog["pool"])

            @block.sync
            def _(e):
                run(e, prog["sp"])


W_INPUTS = [
    ("conv_mod_w", [D, 3 * D]), ("conv_mod_b", [3 * D]),
    ("conv_pw1_w", [D, 2 * D]), ("conv_pw1_b", [2 * D]),
    ("conv_dw_w", [CONVW, D]), ("conv_dw_b", [D]),
    ("conv_norm_g", [D]), ("conv_norm_b", [D]),
    ("conv_pw2_w", [D, D]), ("conv_pw2_b", [D]),
    ("attn_mod_w", [D, 3 * D]), ("attn_mod_b", [3 * D]),
    ("attn_qkv_w", [D, 3 * D]),
    ("attn_lam_q1", [64]), ("attn_lam_k1", [64]), ("attn_lam_q2", [64]), ("attn_lam_k2", [64]),
    ("attn_subln_g", [128]),
    ("attn_out_w", [D, D]),
    ("mlp_mod_w", [2, D, 3 * D]), ("mlp_mod_b", [2, 3 * D]),
    ("mlp_w1", [2, D, DFF]), ("mlp_w2", [2, DFF, D]),
    ("post_mix_g", [2, D]), ("post_mix_b", [2, D]),
    ("post_mlp_g", [2, D]), ("post_mlp_b", [2, D]),
]


def build(mode, ng=NG, strict_same=True, debug=False):
    doA = mode in ("A", "F")
    doB = mode in ("B", "F")
    nc = bass.Bass("TRN2", target_bir_lowering=False)
    I = {}

    def din(name, shape, dt=F32):
        I[name] = nc.dram_tensor(name, list(shape), dt, kind="ExternalInput").ap()
        return I[name]

    for name, shape in W_INPUTS:
        din(name, shape)
    din("cvec", [D])
    din("ident", [128, 128])
    if doA:
        din("xs", [2 * ng, TT, D])
        din("hv", [128, 2 * NG])
    if doB:
        din("biasarr", [NH, 128, 2304])
        din("ch", [128, NH])

    inter = "Internal" if mode == "F" else None
    x1_scr = nc.dram_tensor("x1_scr", [ng, GR, D], F32,
                            kind=inter or ("ExternalOutput" if mode == "A" else "ExternalInput")).ap()
    qt_scr = nc.dram_tensor("qt_scr", [16 * 64, NG * GR], BF16,
                            kind=inter or ("ExternalOutput" if mode == "A" else "ExternalInput")).ap()
    kv_all = nc.dram_tensor("kv_all", [4096, NG * GR], BF16,
                            kind=inter or ("ExternalOutput" if mode == "A" else "ExternalInput")).ap()
    if doB:
        out = nc.dram_tensor("out", [ng, GR, D], F32, kind="ExternalOutput").ap()
    NCH = 46
    if debug and doB:
        dbg_at = nc.dram_tensor("dbg_at", [128, 4, D], BF16, kind="ExternalOutput").ap()
        dbg_xq = nc.dram_tensor("dbg_xq", [128, 4, D], F32, kind="ExternalOutput").ap()
    if debug and doA:
        dbg_ht = nc.dram_tensor("dbg_ht", [128, 8, TT], BF16, kind="ExternalOutput").ap()
        dbg_acc = nc.dram_tensor("dbg_acc", [128, 8, GR], F32, kind="ExternalOutput").ap()
        dbg_vt = nc.dram_tensor("dbg_vt", [128, 8, GR], BF16, kind="ExternalOutput").ap()
        dbg_x0 = nc.dram_tensor("dbg_x0", [128, 4, D], F32, kind="ExternalOutput").ap()
        dbg_cv = nc.dram_tensor("dbg_cv", [128, 64], F32, kind="ExternalOutput").ap()
    wscr = nc.dram_tensor("wscr", [NCH, 128, 4096], BF16, kind="Internal").ap()
    modscr = nc.dram_tensor("modscr", [4, 3 * D], F32, kind="Internal").ap()

    st = ExitStack()
    with st:
        sch = Sched(nc, st, strict_same=strict_same)

        def sb(name, shape, dt, stack=st):
            return stack.enter_context(nc.sbuf_tensor(name, list(shape), dt))

        NWR = 3
        WR = sb("WR", [128, NWR, 4096], BF16)
        X = sb("X", [128, 5, D], F32)
        XB = sb("XB", [128, 5, D], BF16)
        HT = sb("HT", [128, 8, TT], BF16)
        HID = sb("HID", [128, 32, GR], BF16)
        Z = sb("Z", [128, 2, D], F32)
        ZN = sb("ZN", [128, 2, D], F32)
        XO = sb("XO", [128, 2, D], F32)
        BC = sb("BC", [128, 6, D], F32)
        RT = sb("RT", [128, 2, GR], F32)
        CV = sb("CV", [128, 8 * 8], F32)
        IDB = sb("IDB", [128, 128], BF16)
        STt = sb("STt", [128, 2, 12], F32)
        MV = sb("MV", [128, 2, 4], F32)
        NHALF = sb("NHALF", [128, GR], F32)
        PS = [st.enter_context(nc.psum_tensor("PS%d" % i, [128, 1024], F32)) for i in range(4)]

        def bank(b):
            return PS[b // 2][:, (b % 2) * 512:(b % 2) * 512 + 512]

        bptr = [0]
        nbmod = [8]

        def nb():
            b = bptr[0]
            bptr[0] = (b + 1) % nbmod[0]
            return b

        def nb2():
            if bptr[0] % 2:
                bptr[0] = (bptr[0] + 1) % 8
            b = bptr[0]
            bptr[0] = (b + 2) % 8
            return b

        def mm(out_ap, lhsT, rhs, start, stop, reads, writes, signal=None):
            sig = stop if signal is None else signal
            sch.op("pe", lambda e: e.matmul(out_ap, lhsT=lhsT, rhs=rhs, start=start, stop=stop),
                   reads, writes, signal=sig)

        def act(out_ap, in_ap, func, reads, writes, bias=None, scale=None):
            kw = {}
            if bias is not None:
                kw["bias"] = bias
            if scale is not None:
                kw["scale"] = scale
            sch.op("act", lambda e: e.activation(out=out_ap, in_=in_ap, func=func, **kw), reads, writes)

        def tt_(eng, out_ap, a, b, op, reads, writes):
            sch.op(eng, lambda e: e.tensor_tensor(out=out_ap, in0=a, in1=b, op=op), reads, writes)

        def ts_(eng, out_ap, a, s1, s2, op0, op1, reads, writes):
            if s2 is None:
                sch.op(eng, lambda e: e.tensor_scalar(out=out_ap, in0=a, scalar1=s1, scalar2=None, op0=op0),
                       reads, writes)
            else:
                sch.op(eng, lambda e: e.tensor_scalar(out=out_ap, in0=a, scalar1=s1, scalar2=s2, op0=op0, op1=op1),
                       reads, writes)

        def stt(out_ap, a, s, b, op0, op1, reads, writes, accum=None):
            if accum is None:
                sch.op("dve", lambda e: e.scalar_tensor_tensor(out=out_ap, in0=a, scalar=s, in1=b, op0=op0, op1=op1),
                       reads, writes)
            else:
                sch.op("dve", lambda e: e.scalar_tensor_tensor(out=out_ap, in0=a, scalar=s, in1=b, op0=op0, op1=op1,
                                                               accum_out=accum), reads, writes)

        def cp(eng, out_ap, in_ap, reads, writes):
            if eng == "act":
                sch.op("act", lambda e: e.copy(out=out_ap, in_=in_ap), reads, writes)
            else:
                sch.op(eng, lambda e: e.tensor_copy(out=out_ap, in_=in_ap), reads, writes)

        def dma(issuer, lane, out_ap, in_ap, reads, writes, nonc=False):
            if nonc:
                sch.dma(issuer, lane, lambda e: e.dma_start(out=out_ap, in_=in_ap, allow_slow_non_contiguous=True),
                        reads, writes)
            else:
                sch.dma(issuer, lane, lambda e: e.dma_start(out=out_ap, in_=in_ap), reads, writes)

        chunk_id = {}
        cvl = [0]

        cvt_jobs = []

        def cvt(out_ap, in_ap, ci):
            cvt_jobs.append((out_ap, in_ap, ci))

        def emit_cvts(nmax=None, nodep=False):
            k = 0
            while cvt_jobs and (nmax is None or k < nmax):
                (out_ap, in_ap, ci) = cvt_jobs.pop(0)
                lane = "cv%d" % (cvl[0] % 4)
                first = (k == 0) and not nodep
                cvl[0] += 1
                k += 1
                dma("pool", lane, out_ap, in_ap, [("modscr", s_, n_) for s_ in range(4) for n_ in range(6)] if first else [], [("wscr", ci)])

        def add_kmajor(name, W, ncols):
            Wv = W.rearrange("(k p) n -> p k n", p=128)
            for n in range(ncols // 512):
                ci = len(chunk_id)
                chunk_id[(name, n)] = ci
                cvt(wscr[ci].rearrange("p (k n) -> p k n", k=8), Wv[:, :, n * 512:(n + 1) * 512], ci)

        def add_w2(name, W):
            Wv = W.rearrange("(f p) n -> p f n", p=128)
            for c in range(8):
                ci = len(chunk_id)
                chunk_id[(name, c)] = ci
                cvt(wscr[ci].rearrange("p (f n) -> p f n", f=4), Wv[:, 4 * c:4 * c + 4, :], ci)

        if doA:
            Wv = I["conv_pw1_w"].rearrange("(k p) n -> p k n", p=128)
            for j in range(4):
                ci = len(chunk_id)
                chunk_id[("pw1", j)] = ci
                dst = wscr[ci].rearrange("p (k n) -> p k n", k=8)
                cvt(dst[:, :, 0:256], Wv[:, :, 256 * j:256 * j + 256], ci)
                cvt(dst[:, :, 256:512], Wv[:, :, 1024 + 256 * j:1024 + 256 * j + 256], ci)
            add_kmajor("pw2", I["conv_pw2_w"], D)
            add_kmajor("w1_0", I["mlp_w1"][0], DFF)
            add_w2("w2_0", I["mlp_w2"][0])
            add_kmajor("qkv", I["attn_qkv_w"], 3 * D)
        if doB:
            add_kmajor("wo", I["attn_out_w"], D)
            add_kmajor("w1_1", I["mlp_w1"][1], DFF)
            add_w2("w2_1", I["mlp_w2"][1])

        class WStream:
            def __init__(self, seq):
                self.seq = seq
                self.issued = 0
                self.pos = 0
                self.released = 0

            def _issue(self):
                k = self.issued
                slot = k % NWR
                ci = chunk_id[self.seq[k]]
                dma("sp", "wr%d" % slot, WR[:, slot, :], wscr[ci], [("wscr", ci)], [("WR", slot)])
                self.issued += 1

            def prefetch(self):
                while self.issued < len(self.seq) and self.issued - NWR < self.released:
                    self._issue()

            def get(self, name):
                assert self.seq[self.pos] == name, (self.seq[self.pos], name)
                self.prefetch()
                assert self.issued > self.pos
                slot = self.pos % NWR
                self.pos += 1
                return slot

            def release(self, n=1):
                self.released += n
                self.prefetch()

        if doA:
            CA = sb("CA", [128, 8 * 5 + CONVW * 8], F32)
            HVt = sb("HVt", [128, 2 * NG], F32)
            ONES = sb("ONES", [128, 128], F32)
            ONEB = sb("ONEB", [1, 128], BF16)
            PB2 = sb("PB2", [1, D], BF16)
            dma("sp", "pl6", HVt[:, :], I["hv"], [], [("HV",)])
            dma("pool", "pl0", PB2[:, :], I["conv_pw2_b"].rearrange("(o n) -> o n", o=1), [], [("PB2",)])
            sch.op("dve", lambda e: e.memset(ONES[:, :], 1.0 / D), [], [("ONES",)])
            sch.op("dve", lambda e: e.memset(ONEB[:, :], 1.0), [], [("ONEB",)])
        pst = ExitStack()
        with pst:
            MW = sb("MW", [128, 2, 8, 512], F32, pst)
            SREP = sb("SREP", [128, 8, 128], F32, pst)
            CCOL = sb("CCOL", [128, 8], F32, pst)
            SCOL = sb("SCOL", [128, 8], F32, pst)
            MB = sb("MB", [128, 2, 512], F32, pst)
            MROW = sb("MROW", [128, 2, 512], F32, pst)
            mwc = [0]
            TMPC4 = sb("TMPC", [128, 4, 48], F32, pst)
            IDF = sb("IDF", [32, 32], F32, pst)
            E0 = sb("E0", [128, 1], F32, pst)
            ROWS = sb("ROWS", [16, D], F32, pst)
            TAPR = sb("TAPR", [32, D], F32, pst)
            COLV = sb("COLV", [128, 8, 11], F32, pst)
            MCOL = sb("MCOL", [128, 64], F32, pst)
            nbmod[0] = 5
            dma("sp", "pl7", IDF[:, :], I["ident"][0:32, 0:32], [], [("IDF",)])
            dma("sp", "pl7", E0[:, :], I["ident"][:, 0:1], [], [("E0",)], nonc=True)
            rowvecs = [I["post_mix_g"][0], I["post_mix_b"][0], I["post_mlp_g"][0], I["post_mlp_b"][0],
                       I["post_mix_g"][1], I["post_mix_b"][1]]
            if doA:
                rowvecs += [I["conv_pw1_b"][0:D], I["conv_pw1_b"][D:2 * D], I["conv_dw_b"], I["conv_norm_g"], I["conv_norm_b"]]
            for vi, vec in enumerate(rowvecs):
                dma("sp", "pl8", ROWS[vi:vi + 1, :], vec.rearrange("(o n) -> o n", o=1), [], [("ROWS",)])
            nv = len(rowvecs)
            for c in range(8):
                mm(bank(6)[:, c * 11:c * 11 + nv], ROWS[0:nv, c * 128:(c + 1) * 128], IDF[0:nv, 0:nv], True, True,
                   [("ROWS",), ("IDF",)], [("ps", 6)], signal=(c == 7))
            cp("dve", COLV[:, :, :], bank(6)[:, 0:88].rearrange("p (c v) -> p c v", v=11), [("ps", 6)], [("COLV",)])
            if doA:
                dma("sp", "pl8", TAPR[0:CONVW, :], I["conv_dw_w"], [], [("TAPR",)])
                for c in range(8):
                    mm(bank(5)[:, c * CONVW:(c + 1) * CONVW], TAPR[0:CONVW, c * 128:(c + 1) * 128], IDF[0:CONVW, 0:CONVW], True, True,
                       [("TAPR",), ("IDF",)], [("ps", 5)], signal=(c == 7))
                cp("dve", CA[:, 40:40 + CONVW * 8].rearrange("p (j c) -> p j c", c=8),
                   bank(5)[:, 0:CONVW * 8].rearrange("p (c j) -> p j c", j=CONVW), [("ps", 5)], [("CAt",)])
                for k5 in range(5):
                    cp("dve", CA[:, k5 * 8:(k5 + 1) * 8], COLV[:, :, 6 + k5], [("COLV",)], [("CA", k5)])

            dma("pool", "pl0", IDB[:, :], I["ident"], [], [("IDB",)])
            sch.op("dve", lambda e: e.memset(NHALF[:, :], -0.5), [], [("NHALF",)])
            dma("sp", "pl1", CCOL[:, :], I["cvec"].rearrange("(k p) -> p k", p=128), [], [("CCOL",)], nonc=True)
            act(SCOL[:, :], CCOL[:, :], AF.Sigmoid, [("CCOL",)], [("SCOL",)])
            tt_("dve", SCOL[:, :], SCOL[:, :], CCOL[:, :], ALU.mult, [("SCOL",), ("CCOL",)], [("SCOL",)])
            for k in range(8):
                cp("dve", SREP[:, k, :], SCOL[:, k:k + 1].to_broadcast([128, 128]), [("SCOL",)], [("SREP",)])
            if doA:
                emit_cvts(8, nodep=True)
            mods = []
            if doA:
                mods += [(0, I["conv_mod_w"], I["conv_mod_b"]), (1, I["mlp_mod_w"][0], I["mlp_mod_b"][0])]
            mods += [(2, I["attn_mod_w"], I["attn_mod_b"])]
            if doB:
                mods += [(3, I["mlp_mod_w"][1], I["mlp_mod_b"][1])]
            for (s, mw, mb) in mods:
                mwv = mw.rearrange("(k p) n -> p k n", p=128)
                for n in range(6):
                    q = mwc[0] % 2
                    mwc[0] += 1
                    dma("sp", "pl2%d" % q, MW[:, q, :, :], mwv[:, :, n * 512:(n + 1) * 512], [], [("MW", q)])
                    dma("sp", "pl3%d" % q, MB[:, q, :], mb[n * 512:(n + 1) * 512].partition_broadcast(128), [], [("MB", q)])
                    b = nb()
                    for k in range(8):
                        mm(bank(b), SREP[:, k, :], MW[:, q, k, :], k == 0, k == 7,
                           [("SREP",), ("MW", q)], [("ps", b)])
                    tt_("dve", MROW[:, q, :], bank(b), MB[:, q, :], ALU.add, [("ps", b), ("MB", q)], [("MROW", q)])
                    if n < 4:
                        for fc in range(4):
                            col = s * 16 + n * 4 + fc
                            mm(bank(7)[:, col:col + 1], MROW[:, q, fc * 128:(fc + 1) * 128], E0[:, 0:1], True, True,
                               [("MROW", q), ("E0",)], [("ps", 7)], signal=(fc == 3))
                    dma("act", "pl4%d" % q, modscr[s:s + 1, n * 512:(n + 1) * 512], MROW[0:1, q, :], [("MROW", q)], [("modscr", s, n)])

            emit_cvts(28 if mode == "F" else None)
            cp("dve", MCOL[:, :], bank(7)[:, 0:64], [("ps", 7)], [("MCOL",)])
            lnv = {1: (0, 1), 2: (2, 3), 3: (4, 5)}
            for (s, _, _) in mods:
                G2 = CV[:, s * 16:s * 16 + 8]
                B2 = CV[:, s * 16 + 8:s * 16 + 16]
                TMPC = TMPC4[:, s, :]
                SH = MCOL[:, s * 16:s * 16 + 8]
                ts_("dve", TMPC[:, 8:16], MCOL[:, s * 16 + 8:s * 16 + 16], 1.0, None, ALU.add, None, [("MCOL",)], [("TMPC", s, 1)])
                if s == 0:
                    cp("dve", G2, TMPC[:, 8:16], [("TMPC", s, 1)], [("CV", s)])
                    cp("dve", B2, SH, [("MCOL",)], [("CV", s)])
                else:
                    gi, bi = lnv[s]
                    tt_("dve", G2, COLV[:, :, gi], TMPC[:, 8:16], ALU.mult, [("COLV",), ("TMPC", s, 1)], [("CV", s)])
                    tt_("dve", TMPC[:, 32:40], COLV[:, :, bi], TMPC[:, 8:16], ALU.mult, [("COLV",), ("TMPC", s, 1)], [("TMPC", s, 4)])
                    tt_("dve", B2, TMPC[:, 32:40], SH, ALU.add, [("TMPC", s, 4), ("MCOL",)], [("CV", s)])
            nbmod[0] = 8
            bptr[0] = 0
            sch.flush()

        def load_bc(slot, vec, lane="bc"):
            dma("sp", "bc%d" % slot, BC[:, slot, :], vec.partition_broadcast(128),
                [("modscr", s_, n_) for s_ in range(4) for n_ in range(6)], [("BC", slot)])

        def to_hT(s, halo):
            for c in range(8):
                b = nb()
                for t in range(4):
                    mm(bank(b)[:, t * 128:(t + 1) * 128], XB[:, 1 + t, c * 128:(c + 1) * 128], IDB[:, :], True, True,
                       [("XB", 1 + t), ("IDB",)], [("ps", b)], signal=(t == 3))
                act(HT[:, c, HALO:TT], bank(b), AF.Identity, [("ps", b), ("CV", s)], [("HT", c)],
                    bias=CV[:, s * 16 + 8 + c:s * 16 + 9 + c], scale=CV[:, s * 16 + c:s * 16 + c + 1])
            if halo:
                b = nb()
                for c in range(8):
                    mm(bank(b)[:, c * 32:(c + 1) * 32], XB[0:32, 0, c * 128:(c + 1) * 128], IDB[0:32, 0:32], True, True,
                       [("XB", 0), ("IDB",)], [("ps", b)], signal=(c == 7))
                for c in range(8):
                    act(HT[:, c, 0:HALO], bank(b)[:, c * 32:(c + 1) * 32], AF.Identity, [("ps", b), ("CV", s)], [("HT", c)],
                        bias=CV[:, s * 16 + 8 + c:s * 16 + 9 + c], scale=CV[:, s * 16 + c:s * 16 + c + 1])

        def epilogue(t, pb, res_ap, res_key, bcs, out_ap, out_key, znb):
            zb = t % 2
            P = PS[pb // 2][:, :]
            kz, kn = ("Z", zb), ("ZN", zb)
            tt_("dve", Z[:, zb, :], P, BC[:, bcs, :], ALU.mult, [("ps", pb), ("ps", pb + 1), ("BC", bcs)], [kz])
            stt(Z[:, zb, :], res_ap, ALPHA, Z[:, zb, :], ALU.mult, ALU.add, [res_key, kz], [kz])
            for hh in range(2):
                sch.op("dve", lambda e, hh=hh: e.bn_stats(out=STt[:, zb, hh * 6:hh * 6 + 6], in_=Z[:, zb, hh * 512:(hh + 1) * 512]),
                       [kz], [("ST", zb)])
            sch.op("dve", lambda e: e.bn_aggr(out=MV[:, zb, 0:2], in_=STt[:, zb, :]), [("ST", zb)], [("MV", zb)])
            ts_("dve", MV[:, zb, 1:2], MV[:, zb, 1:2], LN_EPS, None, ALU.add, None, [("MV", zb)], [("MV", zb)])
            tt_("pool", MV[:, zb, 2:3], MV[:, zb, 1:2], NHALF[:, 0:1], ALU.pow, [("MV", zb), ("NHALF",)], [("MV", zb)])
            stt(MV[:, zb, 3:4], MV[:, zb, 0:1], -1.0, MV[:, zb, 2:3], ALU.mult, ALU.mult, [("MV", zb)], [("MV", zb)])
            if znb:
                act(XB[:, 1 + t, :], Z[:, zb, :], AF.Identity, [kz, ("MV", zb)], [("XB", 1 + t)],
                    bias=MV[:, zb, 3:4], scale=MV[:, zb, 2:3])
            if out_ap is None:
                return lambda: None
            act(ZN[:, zb, :], Z[:, zb, :], AF.Identity, [kz, ("MV", zb)], [kn],
                bias=MV[:, zb, 3:4], scale=MV[:, zb, 2:3])

            def tail():
                tt_("pool", out_ap, ZN[:, zb, :], BC[:, bcs + 1, :], ALU.mult, [kn, ("BC", bcs + 1)], [out_key])
                tt_("pool", out_ap, out_ap, BC[:, bcs + 2, :], ALU.add, [out_key, ("BC", bcs + 2)], [out_key])
            return tail

        def mlp(ws, s, w1n, w2n):
            to_hT(s, False)
            for f in range(8):
                slot = ws.get((w1n, f))
                for fl in range(4):
                    fc = 4 * f + fl
                    b = nb()
                    for k in range(8):
                        mm(bank(b), WR[:, slot, k * 512 + fl * 128:k * 512 + fl * 128 + 128], HT[:, k, HALO:TT], k == 0, k == 7,
                           [("WR", slot), ("HT", k)], [("ps", b)])
                    rb = fc % 2
                    act(RT[:, rb, :], bank(b), AF.Relu, [("ps", b)], [("RT", rb)])
                    tt_("pool", HID[:, fc, :], RT[:, rb, :], RT[:, rb, :], ALU.mult, [("RT", rb)], [("HID", fc)])
                ws.release()
            for c in range(8):
                slot = ws.get((w2n, c))
                for t in range(4):
                    for half in range(2):
                        b = 2 * t + half
                        for fl in range(4):
                            mm(bank(b), HID[:, 4 * c + fl, t * 128:(t + 1) * 128],
                               WR[:, slot, fl * 1024 + half * 512:fl * 1024 + half * 512 + 512],
                               c == 0 and fl == 0, c == 7 and fl == 3,
                               [("HID", 4 * c + fl), ("WR", slot)], [("ps", b)], signal=(fl == 3))
                ws.release()
            bptr[0] = 0

        if doA:
            ast = ExitStack()
            with ast:
                U = sb("U", [128, 2, TT], BF16, ast)
                NDG = 8
                DG = sb("DG", [128, NDG, 128], BF16, ast)
                dgc = [0]
                SG = sb("SG", [128, 2, TT], F32, ast)
                ACC = sb("ACC", [128, 8, GR], F32, ast)
                SQ = sb("SQ", [128, 2, GR], F32, ast)
                UN = sb("UN", [128, 2, GR], F32, ast)
                VT = sb("VT", [128, 8, GR], BF16, ast)
                STB = sb("STB", [128, 4, GR], F32, ast)
                QS = sb("QS", [128, 2, GR], BF16, ast)
                VS = sb("VS", [128, 2, D], BF16, ast)
                load_bc(0, modscr[0, 2 * D:3 * D])
                load_bc(1, I["post_mix_g"][0])
                load_bc(2, I["post_mix_b"][0])
                load_bc(3, modscr[1, 2 * D:3 * D])
                load_bc(4, I["post_mlp_g"][0])
                load_bc(5, I["post_mlp_b"][0])

                seqA = []
                for pp in range(2 * ng):
                    seqA += [("pw1", j) for j in range(4)] + [("pw2", n) for n in range(2)]
                    seqA += [("w1_0", f) for f in range(8)] + [("w2_0", c) for c in range(8)]
                    seqA += [("qkv", n) for n in range(0 if pp % 2 == 0 else 2, 6)]
                ws = WStream(seqA)

                def load_x(g):
                    dma("sp", "xl0", X[0:HALO, 0, :], I["xs"][g, 0:HALO, :], [], [("X", 0)])
                    dma("sp", "xl1", X[:, 1:5, :], I["xs"][g, HALO:TT, :].rearrange("(t p) d -> p t d", p=128),
                        [], [("X", 1), ("X", 2), ("X", 3), ("X", 4)])

                load_x(0)
                ws.prefetch()
                for pp in range(2 * ng):
                    g = pp // 2
                    own = (pp % 2 == 0)
                    kvl = 0 if own else 1
                    cp("dve", XB[0:HALO, 0, :], X[0:HALO, 0, :], [("X", 0)], [("XB", 0)])
                    for t in range(4):
                        cp("dve" if t % 2 else "act", XB[:, 1 + t, :], X[:, 1 + t, :], [("X", 1 + t)], [("XB", 1 + t)])
                    to_hT(0, True)
                    if debug and pp == 0:
                        dma("sp", "dbg", dbg_ht[:, :, :], HT[:, :, :], [("HT", c) for c in range(8)], [("dbg", 0)])
                        dma("sp", "dbg", dbg_cv[:, :], CV[:, :], [("CV", 0)], [("dbg", 4)])
                    nbmod[0] = 6
                    bptr[0] = 0
                    pw1_slot = {}
                    pw1_banks = {}

                    def pw1_glu(i):
                        j, s2 = i // 2, i % 2
                        if s2 == 0:
                            pw1_slot[j] = ws.get(("pw1", j))
                        slot = pw1_slot[j]
                        ub = i % 2
                        ba, bg, bh = nb(), nb(), nb()
                        for (bk, col0) in ((ba, 128 * s2), (bg, 256 + 128 * s2)):
                            for k in range(8):
                                mm(bank(bk), WR[:, slot, k * 512 + col0:k * 512 + col0 + 128], HT[:, k, HALO:TT], k == 0, k == 7,
                                   [("WR", slot), ("HT", k)], [("ps", bk)])
                        for hi, col0 in enumerate((128 * s2, 256 + 128 * s2)):
                            for k in range(8):
                                mm(bank(bh)[:, hi * 32:hi * 32 + 32], WR[:, slot, k * 512 + col0:k * 512 + col0 + 128], HT[:, k, 0:HALO],
                                   k == 0, k == 7, [("WR", slot), ("HT", k)], [("ps", bh)], signal=(k == 7 and hi == 1))
                        if s2 == 1:
                            ws.release()
                        pw1_banks[i] = (ba, bg, bh)

                    def glu_elem(i):
                        ub = i % 2
                        ba, bg, bh = pw1_banks[i]
                        act(SG[:, ub, HALO:TT], bank(bg), AF.Sigmoid, [("ps", bg), ("CA", 0), ("CA", 1), ("CA", 2), ("CA", 3), ("CA", 4), ("CAt",)], [("SG", ub)], bias=CA[:, 8 + i:9 + i])
                        act(SG[:, ub, 0:HALO], bank(bh)[:, 32:64], AF.Sigmoid, [("ps", bh), ("CA", 0), ("CA", 1), ("CA", 2), ("CA", 3), ("CA", 4), ("CAt",)], [("SG", ub)], bias=CA[:, 8 + i:9 + i])
                        stt(U[:, ub, HALO:TT], bank(ba), CA[:, i:i + 1], SG[:, ub, HALO:TT], ALU.add, ALU.mult,
                            [("ps", ba), ("SG", ub), ("CA", 0), ("CA", 1), ("CA", 2), ("CA", 3), ("CA", 4), ("CAt",)], [("U", ub)])
                        stt(U[:, ub, 0:HALO], bank(bh)[:, 0:32], CA[:, i:i + 1], SG[:, ub, 0:HALO], ALU.add, ALU.mult,
                            [("ps", bh), ("SG", ub), ("CA", 0), ("CA", 1), ("CA", 2), ("CA", 3), ("CA", 4), ("CAt",)], [("U", ub)])
                        ts_("dve", U[:, ub, 0:HALO], U[:, ub, 0:HALO], HVt[:, pp:pp + 1], None, ALU.mult, None,
                            [("U", ub), ("HV",)], [("U", ub)])

                    def dwconv(i):
                        ub = i % 2
                        bc_ = nb()
                        for jt in range(CONVW):
                            dgs = dgc[0] % NDG
                            dgc[0] += 1
                            if jt % 2 == 0:
                                ts_("dve", DG[:, dgs, :], IDB[:, :], CA[:, 40 + jt * 8 + i:41 + jt * 8 + i], None, ALU.mult, None,
                                    [("IDB",), ("CA", 0), ("CA", 1), ("CA", 2), ("CA", 3), ("CA", 4), ("CAt",)], [("DG", dgs)])
                            else:
                                act(DG[:, dgs, :], IDB[:, :], AF.Copy, [("IDB",), ("CA", 0), ("CA", 1), ("CA", 2), ("CA", 3), ("CA", 4), ("CAt",)], [("DG", dgs)],
                                    scale=CA[:, 40 + jt * 8 + i:41 + jt * 8 + i])
                            mm(bank(bc_), DG[:, dgs, :], U[:, ub, 2 + jt:2 + jt + GR], jt == 0, jt == CONVW - 1,
                               [("DG", dgs), ("U", ub)], [("ps", bc_)], signal=True)
                        act(ACC[:, i, :], bank(bc_), AF.Identity, [("ps", bc_), ("CA", 0), ("CA", 1), ("CA", 2), ("CA", 3), ("CA", 4), ("CAt",)], [("ACC", i)], bias=CA[:, 16 + i:17 + i])
                        act(SQ[:, ub, :], ACC[:, i, :], AF.Square, [("ACC", i)], [("SQ", ub)])
                        mm(bank(6), ONES[:, :], ACC[:, i, :], i == 0, i == 7, [("ONES",), ("ACC", i)], [("ps", 6)], signal=True)
                        mm(bank(7), ONES[:, :], SQ[:, ub, :], i == 0, i == 7, [("ONES",), ("SQ", ub)], [("ps", 7)], signal=True)

                    pw1_glu(0)
                    glu_elem(0)
                    for i in range(8):
                        if i + 1 < 8:
                            pw1_glu(i + 1)
                        dwconv(i)
                        if i + 1 < 8:
                            glu_elem(i + 1)
                    bm, bq = 6, 7
                    nbmod[0] = 8
                    bptr[0] = 0
                    cp("dve", STB[:, 0, :], bank(bm), [("ps", bm)], [("STB", 0)])
                    tt_("dve", STB[:, 3, :], STB[:, 0, :], STB[:, 0, :], ALU.mult, [("STB", 0)], [("STB", 3)])
                    tt_("dve", STB[:, 1, :], bank(bq), STB[:, 3, :], ALU.subtract, [("ps", bq), ("STB", 3)], [("STB", 1)])
                    act(STB[:, 1, :], STB[:, 1, :], AF.Ln, [("STB", 1)], [("STB", 1)], bias=LN_EPS)
                    act(STB[:, 1, :], STB[:, 1, :], AF.Exp, [("STB", 1)], [("STB", 1)], scale=-0.5)
                    stt(STB[:, 2, :], STB[:, 0, :], -1.0, STB[:, 1, :], ALU.mult, ALU.mult, [("STB", 0), ("STB", 1)], [("STB", 2)])
                    for i in range(8):
                        ub = i % 2
                        tt_("dve", UN[:, ub, :], ACC[:, i, :], STB[:, 1, :], ALU.mult, [("ACC", i), ("STB", 1)], [("UN", ub)])
                        tt_("dve", UN[:, ub, :], UN[:, ub, :], STB[:, 2, :], ALU.add, [("UN", ub), ("STB", 2)], [("UN", ub)])
                        act(SQ[:, ub, :], UN[:, ub, :], AF.Identity, [("UN", ub), ("CA", 0), ("CA", 1), ("CA", 2), ("CA", 3), ("CA", 4), ("CAt",)], [("SQ", ub)],
                            bias=CA[:, 32 + i:33 + i], scale=CA[:, 24 + i:25 + i])
                        act(RT[:, ub, :], UN[:, ub, :], AF.Sigmoid, [("UN", ub), ("CA", 0), ("CA", 1), ("CA", 2), ("CA", 3), ("CA", 4), ("CAt",)], [("RT", ub)],
                            bias=CA[:, 32 + i:33 + i], scale=CA[:, 24 + i:25 + i])
                        tt_("pool", VT[:, i, :], SQ[:, ub, :], RT[:, ub, :], ALU.mult, [("SQ", ub), ("RT", ub)], [("VT", i)])
                    if debug and pp == 0:
                        dma("sp", "dbg", dbg_acc[:, :, :], ACC[:, :, :], [("ACC", c) for c in range(8)], [("dbg", 1)])
                        dma("sp", "dbg", dbg_vt[:, :, :], VT[:, :, :], [("VT", c) for c in range(8)], [("dbg", 2)])
                    s0 = ws.get(("pw2", 0))
                    s1 = ws.get(("pw2", 1))
                    for t in range(4):
                        pb = nb2()
                        for half, slot in ((0, s0), (1, s1)):
                            b = pb + half
                            mm(bank(b), ONEB[0:1, :], PB2[0:1, half * 512:(half + 1) * 512], True, False,
                               [("ONEB",), ("PB2",)], [("ps", b)])
                            for k in range(8):
                                mm(bank(b), VT[:, k, t * 128:(t + 1) * 128], WR[:, slot, k * 512:(k + 1) * 512], False, k == 7,
                                   [("VT", k), ("WR", slot)], [("ps", b)])
                        if t == 3:
                            ws.release(2)
                        tl = epilogue(t, pb, X[:, 1 + t, :], ("X", 1 + t), 0, X[:, 1 + t, :], ("X", 1 + t), True)
                        if t > 0:
                            prev_tail()
                        prev_tail = tl
                    prev_tail()
                    if debug and pp == 0:
                        dma("sp", "dbg", dbg_x0[:, :, :], X[:, 1:5, :], [("X", 1 + t) for t in range(4)], [("dbg", 3)])
                    mlp(ws, 1, "w1_0", "w2_0")
                    tails = []
                    for t in range(4):
                        if own:
                            tl = epilogue(t, 2 * t, X[:, 1 + t, :], ("X", 1 + t), 3, XO[:, t % 2, :], ("XO", t % 2), True)

                            def fin(t=t, tl=tl):
                                tl()
                                dma("sp", "xo%d" % (t % 2), x1_scr[g, t * 128:(t + 1) * 128, :], XO[:, t % 2, :],
                                    [("XO", t % 2)], [("x1", g, t)])
                            tails.append(fin)
                            if t > 0:
                                tails[t - 1]()
                        else:
                            epilogue(t, 2 * t, X[:, 1 + t, :], ("X", 1 + t), 3, None, None, True)
                    if own:
                        tails[3]()
                    if pp + 1 < 2 * ng:
                        load_x(pp + 1)
                    emit_cvts(2)
                    to_hT(2, False)
                    for n in range(0 if own else 2, 4):
                        slot = ws.get(("qkv", n))
                        for cg in range(4):
                            b = nb()
                            for k in range(8):
                                mm(bank(b), WR[:, slot, k * 512 + cg * 128:k * 512 + cg * 128 + 128], HT[:, k, HALO:TT], k == 0, k == 7,
                                   [("WR", slot), ("HT", k)], [("ps", b)])
                            qb = (n * 4 + cg) % 2
                            if n < 2:
                                act(QS[:, qb, :], bank(b), AF.Copy, [("ps", b)], [("QS", qb)], scale=0.125)
                            else:
                                cp("dve", QS[:, qb, :], bank(b), [("ps", b)], [("QS", qb)])
                            r0 = ((n % 2) * 8 + cg * 2) * 64
                            if n < 2:
                                dst = qt_scr[r0:r0 + 128, g * GR:(g + 1) * GR]
                            else:
                                dst = kv_all[kvl * 2048 + r0:kvl * 2048 + r0 + 128, g * GR:(g + 1) * GR]
                            dma("sp", "qs%d" % qb, dst, QS[:, qb, :], [("QS", qb)],
                                [("qk", n, cg, g)] if n < 2 else [("kw", kvl, n, cg, g)])
                        ws.release()
                    sv0 = ws.get(("qkv", 4))
                    sv1 = ws.get(("qkv", 5))
                    Vv = kv_all[kvl * 2048 + 1024:kvl * 2048 + 2048, :].rearrange("r (a c) -> (r a) c", a=4)
                    for t in range(4):
                        pb = nb2()
                        for half, slot in ((0, sv0), (1, sv1)):
                            b = pb + half
                            for k in range(8):
                                mm(bank(b), HT[:, k, HALO + t * 128:HALO + (t + 1) * 128], WR[:, slot, k * 512:(k + 1) * 512], k == 0, k == 7,
                                   [("HT", k), ("WR", slot)], [("ps", b)])
                        if t == 3:
                            ws.release(2)
                        vb = t % 2
                        cp("act" if t % 2 else "dve", VS[:, vb, :], PS[pb // 2][:, :], [("ps", pb), ("ps", pb + 1)], [("VS", vb)])
                        dma("sp", "vs%d" % vb, Vv[g * GR + t * 128:g * GR + (t + 1) * 128, :], VS[:, vb, :], [("VS", vb)], [("vw", kvl, g, t)])
                emit_cvts()
                lanesA = ["xo0", "xo1", "qs0", "qs1", "vs0", "vs1"]
                sch.wait_all("sp", lanesA)
                sch.flush()

        if doB:
            bst = ExitStack()
            with bst:
                QT = sb("QT", [128, 2, 2, GR], BF16, bst)
                KT = sb("KT", [128, 3, 2, GR], BF16, bst)
                VA = sb("VA", [128, 3, 4, 132], BF16, bst)
                PT = sb("PT", [128, 2, 1024], BF16, bst)
                BA = sb("BA", [128, 2, 2304], BF16, bst)
                CH = sb("CH", [128, NH], F32, bst)
                LM = sb("LM", [128, 4, 64], F32, bst)
                LS = sb("LS", [128, 8], F32, bst)
                GSUB = sb("GSUB", [128, 128], F32, bst)
                OS = sb("OS", [128, 2, 128], F32, bst)
                SM = sb("SM", [128, 2, 8], F32, bst)
                OJ = sb("OJ", [128, 128], F32, bst)
                OAC = sb("OAC", [128, 1032], F32, bst)

                dma("sp", "pl6", CH[:, :], I["ch"], [], [("CH",)])
                for q, nm in enumerate(("attn_lam_q1", "attn_lam_k1", "attn_lam_q2", "attn_lam_k2")):
                    dma("sp", "pl7", LM[:, q, :], I[nm].partition_broadcast(128), [], [("LM", q)])
                dma("sp", "pl7", GSUB[:, :], I["attn_subln_g"].partition_broadcast(128), [], [("GSUB",)])
                ts_("dve", GSUB[:, :], GSUB[:, :], 1.0 - LAMBDA_INIT, None, ALU.mult, None, [("GSUB",)], [("GSUB",)])
                for q in range(2):
                    stt(LM[:, 2 * q, :], LM[:, 2 * q, :], 1.0, LM[:, 2 * q + 1, :], ALU.mult, ALU.mult,
                        [("LM", 2 * q), ("LM", 2 * q + 1)], [("LM", 2 * q)], accum=LS[:, q:q + 1])
                act(LS[:, 2:4], LS[:, 0:2], AF.Exp, [("LM", 0), ("LM", 2)], [("LS",)])
                tt_("dve", LS[:, 4:5], LS[:, 2:3], LS[:, 3:4], ALU.subtract, [("LS",)], [("LS",)])
                ts_("dve", LS[:, 5:6], LS[:, 4:5], LAMBDA_INIT, -1.0, ALU.add, ALU.mult, [("LS",)], [("LS",)])
                sch.op("dve", lambda e: e.memset(VA[:, :, :, 128:129], 1.0), [], [("VA", 0), ("VA", 1), ("VA", 2)])
                sch.op("dve", lambda e: e.memset(QT[64:128, :, :, :], 0.0), [], [("QT", 0), ("QT", 1)])
                sch.op("dve", lambda e: e.memset(KT[64:128, :, :, :], 0.0), [], [("KT", 0), ("KT", 1), ("KT", 2)])
                load_bc(0, modscr[2, 2 * D:3 * D])
                load_bc(1, I["post_mix_g"][1])
                load_bc(2, I["post_mix_b"][1])
                load_bc(3, modscr[3, 2 * D:3 * D])
                load_bc(4, I["post_mlp_g"][1])
                load_bc(5, I["post_mlp_b"][1])

                seqB = []
                for g in range(ng):
                    seqB += [("wo", n) for n in range(2)] + [("w1_1", f) for f in range(8)] + [("w2_1", c) for c in range(8)]
                ws = WStream(seqB)

                Vall = [kv_all[r * 2048 + 1024:r * 2048 + 2048, :].rearrange("r (a c) -> (r a) c", a=4) for r in range(2)]

                def acc_ap(m, qs):
                    a = m * 4 + qs
                    bk, sl = a // 3, a % 3
                    if bk < 2:
                        return PS[2][:, bk * 512 + sl * 129:bk * 512 + sl * 129 + 129], ("ps", 4 + bk)
                    return PS[3][:, sl * 129:sl * 129 + 129], ("ps", 6)

                kvc = [0]
                hcount = [0]
                for j in range(ng):
                    dma("sp", "xl1", X[:, 1:5, :], x1_scr[j].rearrange("(t p) d -> p t d", p=128),
                        [("x1", j, t) for t in range(4)], [("X", 1), ("X", 2), ("X", 3), ("X", 4)])
                    for h in range(NH):
                        hb = hcount[0] % 2
                        hcount[0] += 1

                        def load_head(jj, hh, hbb):
                            dma("sp", "qt%d" % hbb, QT[0:64, hbb, :, :],
                                qt_scr[2 * hh * 64:2 * hh * 64 + 128, jj * GR:(jj + 1) * GR].rearrange("(m d) t -> d m t", m=2),
                                [("qk", n, cg, jj) for n in range(2) for cg in range(4)], [("QT", hbb)])
                            dma("pool", "ba%d" % hbb, BA[:, hbb, :], I["biasarr"][hh], [], [("BA", hbb)])

                        if j == 0 and h == 0:
                            load_head(0, 0, hb)
                        if h + 1 < NH:
                            load_head(j, h + 1, 1 - hb)
                        elif j + 1 < ng:
                            load_head(j + 1, 0, 1 - hb)
                        blocks = []
                        for i in range(j + 1):
                            for r in range(2):
                                for kb in range(4):
                                    sp_off = None
                                    if i == j:
                                        sp_off = (0 if r == 0 else 896) + 384 - kb * 128
                                    elif i == j - 1 and r == 1 and kb == 3:
                                        sp_off = 1792
                                    blocks.append((r, i, kb, sp_off))
                        kvslot = {}

                        def load_kv(r, i):
                            sl = kvc[0] % 3
                            kvc[0] += 1
                            kvslot[(r, i)] = sl
                            dma("sp", "kt%d" % sl, KT[0:64, sl, :, :],
                                kv_all[r * 2048 + 2 * h * 64:r * 2048 + 2 * h * 64 + 128, i * GR:(i + 1) * GR].rearrange("(m d) t -> d m t", m=2),
                                [], [("KT", sl)])
                            dma("sp", "va%d" % sl, VA[:, sl, :, 0:128],
                                Vall[r][i * GR:(i + 1) * GR, h * 128:(h + 1) * 128].rearrange("(kb p) e -> p kb e", p=128),
                                [], [("VA", sl)], nonc=True)

                        def qk(n):
                            r, i, kb, _ = blocks[n]
                            if (r, i) not in kvslot:
                                load_kv(r, i)
                            sl = kvslot[(r, i)]
                            sbuf_ = n % 2
                            spo = blocks[n][3]
                            for m in range(2):
                                b = 2 * sbuf_ + m
                                mm(bank(b), KT[:, sl, m, kb * 128:(kb + 1) * 128], QT[:, hb, m, :], True, spo is None,
                                   [("KT", sl), ("QT", hb)], [("ps", b)], signal=(m == 1 and spo is None))
                                if spo is not None:
                                    mm(bank(b), IDB[:, :], BA[:, hb, spo:spo + 512], False, True,
                                       [("IDB",), ("BA", hb)], [("ps", b)], signal=(m == 1))

                        qk(0)
                        nblk = len(blocks)
                        for n in range(nblk):
                            r, i, kb, sp_off = blocks[n]
                            if n + 1 < nblk:
                                qk(n + 1)
                            sbuf_ = n % 2
                            sl = kvslot[(r, i)]
                            b0 = 2 * sbuf_
                            if sp_off is not None:
                                act(PT[:, sbuf_, :], PS[sbuf_][:, :], AF.Exp, [("ps", b0), ("ps", b0 + 1)], [("PT", sbuf_)])
                            else:
                                act(PT[:, sbuf_, :], PS[sbuf_][:, :], AF.Exp, [("ps", b0), ("ps", b0 + 1), ("CH",)], [("PT", sbuf_)],
                                    bias=CH[:, h:h + 1])
                            for m in range(2):
                                for qs in range(4):
                                    ap_, key_ = acc_ap(m, qs)
                                    mm(ap_, PT[:, sbuf_, m * 512 + qs * 128:m * 512 + qs * 128 + 128], VA[:, sl, kb, 0:129],
                                       n == 0 and (m * 4 + qs) % 3 == 0, n == nblk - 1, [("PT", sbuf_), ("VA", sl)], [key_],
                                       signal=(m == 1 and qs == 3))
                        cp("dve", OAC[:, 0:387], PS[2][:, 0:387], [("ps", 4)], [("OAC", 0)])
                        cp("dve", OAC[:, 387:774], PS[2][:, 512:899], [("ps", 5)], [("OAC", 1)])
                        cp("dve", OAC[:, 774:1032], PS[3][:, 0:258], [("ps", 6)], [("OAC", 2)])
                        for qs in range(4):
                            ob = qs % 2
                            i1, i2 = qs, 4 + qs
                            a1, k1 = OAC[:, i1 * 129:(i1 + 1) * 129], ("OAC", i1 // 3)
                            a2, k2 = OAC[:, i2 * 129:(i2 + 1) * 129], ("OAC", i2 // 3)
                            ko, ks = ("OS", ob), ("SM", ob)
                            sch.op("dve", lambda e, a1=a1, ob=ob: e.reciprocal(out=SM[:, ob, 0:1], in_=a1[:, 128:129]), [k1], [ks])
                            sch.op("dve", lambda e, a2=a2, ob=ob: e.reciprocal(out=SM[:, ob, 1:2], in_=a2[:, 128:129]), [k2], [ks])
                            ts_("dve", SM[:, ob, 2:3], SM[:, ob, 1:2], LS[:, 5:6], None, ALU.mult, None, [ks, ("LS",)], [ks])
                            ts_("dve", OS[:, ob, :], a1[:, 0:128], SM[:, ob, 0:1], None, ALU.mult, None, [k1, ks], [ko])
                            stt(OS[:, ob, :], a2[:, 0:128], SM[:, ob, 2:3], OS[:, ob, :], ALU.mult, ALU.add, [k2, ks, ko], [ko])
                            stt(OJ[:, :], OS[:, ob, :], 1.0, OS[:, ob, :], ALU.mult, ALU.mult, [ko], [("OJ",)], accum=SM[:, ob, 3:4])
                            ts_("dve", SM[:, ob, 4:5], SM[:, ob, 3:4], 1.0 / 128.0, LN_EPS, ALU.mult, ALU.add, [("OJ",), ks], [ks])
                            tt_("pool", SM[:, ob, 5:6], SM[:, ob, 4:5], NHALF[:, 0:1], ALU.pow, [ks, ("NHALF",)], [ks])
                            stt(XB[:, 1 + qs, h * 128:(h + 1) * 128], OS[:, ob, :], SM[:, ob, 5:6], GSUB[:, :], ALU.mult, ALU.mult,
                                [ko, ks, ("GSUB",)], [("XB", 1 + qs)])
                    if debug and j == 0:
                        dma("sp", "dbg", dbg_at[:, :, :], XB[:, 1:5, :], [("XB", 1 + t) for t in range(4)], [("dbg", 5)])
                    bptr[0] = 7
                    for c in range(8):
                        b = 7
                        for t in range(4):
                            mm(bank(b)[:, t * 128:(t + 1) * 128], XB[:, 1 + t, c * 128:(c + 1) * 128], IDB[:, :], True, True,
                               [("XB", 1 + t), ("IDB",)], [("ps", b)], signal=(t == 3))
                        cp("act" if c % 2 else "dve", HT[:, c, HALO:TT], bank(b), [("ps", b)], [("HT", c)])
                    bptr[0] = 0
                    s0 = ws.get(("wo", 0))
                    s1 = ws.get(("wo", 1))
                    for t in range(4):
                        pb = nb2()
                        for half, slot in ((0, s0), (1, s1)):
                            b = pb + half
                            for k in range(8):
                                mm(bank(b), HT[:, k, HALO + t * 128:HALO + (t + 1) * 128], WR[:, slot, k * 512:(k + 1) * 512], k == 0, k == 7,
                                   [("HT", k), ("WR", slot)], [("ps", b)])
                        if t == 3:
                            ws.release(2)
                        tl = epilogue(t, pb, X[:, 1 + t, :], ("X", 1 + t), 0, X[:, 1 + t, :], ("X", 1 + t), True)
                        if t > 0:
                            prev_tail()
                        prev_tail = tl
                    prev_tail()
                    if debug and j == 0:
                        dma("sp", "dbg", dbg_xq[:, :, :], X[:, 1:5, :], [("X", 1 + t) for t in range(4)], [("dbg", 6)])
                    mlp(ws, 3, "w1_1", "w2_1")
                    tails = []
                    for t in range(4):
                        tl = epilogue(t, 2 * t, X[:, 1 + t, :], ("X", 1 + t), 3, XO[:, t % 2, :], ("XO", t % 2), False)

                        def fin(t=t, tl=tl):
                            tl()
                            dma("sp", "xo%d" % (t % 2), out[j, t * 128:(t + 1) * 128, :], XO[:, t % 2, :],
                                [("XO", t % 2)], [("out", j, t)])
                        tails.append(fin)
                        if t > 0:
                            tails[t - 1]()
                    tails[3]()
                sch.wait_all("sp", ["xo0", "xo1"])
                sch.flush()
    return nc


def _t5_bucket(rel):
    n = np.maximum(rel, 0)
    nf = np.maximum(n, 1).astype(np.float32)
    large = 16 + (np.log(nf / np.float32(16)) / np.float32(math.log(128 / 16)) * np.float32(16)).astype(np.int32)
    large = np.minimum(large, 31)
    return np.where(n < 16, n, large)


def _bias_arrays(rel_bias, role):
    p = np.arange(128)[:, None]
    out = np.empty((NH, 128, 2304), np.float32)

    def fill(width, base_rel):
        j = np.arange(width)[None, :]
        rel = j - p + base_rel
        bk = _t5_bucket(rel)
        v = rel_bias[bk]
        v = np.where((rel >= 0)[:, :, None], v, np.float32(NEG))
        return np.transpose(v, (2, 0, 1))

    if role == 0:
        dA, dB, relC = 0, -512, 128
    else:
        dA, dB, relC = 0, 512, 1152
    out[:, :, 0:896] = fill(896, -384 + dA)
    out[:, :, 896:1792] = fill(896, -384 + dB)
    out[:, :, 1792:2304] = fill(512, relC)
    return out


_CACHE = {}


def _get(mode):
    if mode not in _CACHE:
        _CACHE[mode] = build(mode)
    return _CACHE[mode]


def _core_inputs(inputs, c):
    b, r = c // 2, c % 2
    x = inputs["x"]
    d = {}
    for name, shape in W_INPUTS:
        a = np.ascontiguousarray(inputs[name], dtype=np.float32).reshape(shape)
        d[name] = a
    d["cvec"] = np.ascontiguousarray(inputs["c"][b])
    d["ident"] = np.eye(128, dtype=np.float32)
    xs = np.zeros((2 * NG, TT, D), np.float32)
    hv = np.ones((128, 2 * NG), np.float32)
    for pp in range(2 * NG):
        i = pp // 2
        G = 2 * i + (r if pp % 2 == 0 else 1 - r)
        lo = G * GR - HALO
        if lo < 0:
            xs[pp, HALO:] = x[b, 0:GR]
            hv[:, pp] = 0.0
        else:
            xs[pp] = x[b, lo:lo + TT]
    d["xs"] = xs
    d["hv"] = hv
    d["biasarr"] = _bias_arrays(np.asarray(inputs["rel_bias"], np.float32), r)
    d["ch"] = np.ascontiguousarray(np.broadcast_to(np.asarray(inputs["rel_bias"], np.float32)[31][None, :], (128, NH)))
    return d


FUSED = True


def kernel(**inputs):
    inputs = {k: np.asarray(v) for k, v in inputs.items()}
    cores = list(range(8))
    per = [_core_inputs(inputs, c) for c in cores]
    if FUSED:
        nc = _get("F")
        keys = [t for t in per[0].keys()]
        res = run_bass_kernel_spmd(nc, per, core_ids=cores)
        outs = [r["out"] for r in res.results]
    else:
        ncA = _get("A")
        inA = [{k: v for k, v in p.items() if k not in ("biasarr", "ch")} for p in per]
        resA = run_bass_kernel_spmd(ncA, inA, core_ids=cores).results
        ncB = _get("B")
        inB = []
        for c in cores:
            p = {k: v for k, v in per[c].items() if k not in ("xs", "hv")}
            p["kv_all"] = resA[c]["kv_all"]
            p["x1_scr"] = resA[c]["x1_scr"]
            p["qt_scr"] = resA[c]["qt_scr"]
            inB.append(p)
        resB = run_bass_kernel_spmd(ncB, inB, core_ids=cores).results
        outs = [r["out"] for r in resB]
    y = np.empty((NB, SEQ, D), np.float32)
    for c in cores:
        b, r = c // 2, c % 2
        o = np.asarray(outs[c]).reshape(NG, GR, D)
        for i in range(NG):
            G = 2 * i + r
            y[b, G * GR:(G + 1) * GR] = o[i]
    return y
```

```python
import math
from contextlib import ExitStack

import numpy as np
import concourse.bass as bass
import concourse.mybir as mybir
from concourse.bass_utils import run_bass_kernel_spmd

F32 = mybir.dt.float32
BF16 = mybir.dt.bfloat16
AF = mybir.ActivationFunctionType
ALU = mybir.AluOpType

D = 1024
SEQ = 8192
NB = 4
DFF = 4096
GR = 512
HALO = 32
TT = GR + HALO
NG = 8
NH = 8
ALPHA = 4.0 ** 0.25
LN_EPS = 1e-5
LAMBDA_INIT = 0.8 - 0.6 * math.exp(-0.3 * 1)
CONVW = 31
NEG = -30000.0

ENGS = ("pe", "act", "dve", "pool", "sp")


class Sched:
    def __init__(self, nc, stack, strict_same=True):
        self.nc = nc
        self.stack = stack
        self.sems = {}
        self.cnt = {}
        self.prog = {e: [] for e in ENGS}
        self.seen = {e: {} for e in ENGS}
        self.last_w = {}
        self.readers = {}
        self.strict_same = strict_same
        for e in ENGS:
            self._mk(e)

    def _mk(self, name):
        self.sems[name] = self.stack.enter_context(self.nc.semaphore("s_" + name))
        self.cnt[name] = 0

    def _deps(self, reads, writes):
        d = {}
        for k in reads:
            w = self.last_w.get(k)
            if w:
                d[w[0]] = max(d.get(w[0], 0), w[1])
        for k in writes:
            w = self.last_w.get(k)
            if w:
                d[w[0]] = max(d.get(w[0], 0), w[1])
            for e2, c in self.readers.get(k, {}).items():
                d[e2] = max(d.get(e2, 0), c)
        return d

    def _wait(self, eng, d):
        for src, c in d.items():
            if src == eng and (eng == "pe" or not self.strict_same):
                continue
            if self.seen[eng].get(src, 0) >= c:
                continue
            self.seen[eng][src] = c
            if c > self.cnt[src]:
                print("SCHED WARNING: %s waits on future signal of %s (%d > %d)" % (eng, src, c, self.cnt[src]))
            unit = 1 if src in ENGS else 16
            self.prog[eng].append(("wait", src, c * unit))

    def _record(self, src, n, reads, writes):
        for k in reads:
            self.readers.setdefault(k, {})[src] = n
        for k in writes:
            self.last_w[k] = (src, n)
            self.readers[k] = {}

    def op(self, eng, fn, reads=(), writes=(), signal=True):
        self._wait(eng, self._deps(reads, writes))
        n = self.cnt[eng] + 1
        if signal:
            self.cnt[eng] = n
        self.prog[eng].append(("op", fn, eng if signal else None, 1))
        self._record(eng, n, reads, writes)

    def dma(self, issuer, lane, fn, reads=(), writes=()):
        if lane not in self.sems:
            self._mk(lane)
        d = self._deps(reads, writes)
        if self.cnt[lane] > 0:
            d[lane] = max(d.get(lane, 0), self.cnt[lane])
        self._wait(issuer, d)
        self.cnt[lane] += 1
        self.prog[issuer].append(("op", fn, lane, 16))
        self._record(lane, self.cnt[lane], reads, writes)

    def wait_all(self, eng, lanes):
        d = {l: self.cnt[l] for l in lanes if self.cnt.get(l, 0) > 0}
        self._wait(eng, d)

    def flush(self):
        nc = self.nc
        prog = self.prog
        sems = self.sems
        self.prog = {e: [] for e in ENGS}

        def run(e, lst):
            for it in lst:
                if it[0] == "wait":
                    e.wait_ge(sems[it[1]], it[2])
                else:
                    ins = it[1](e)
                    if it[2] is not None:
                        ins.then_inc(sems[it[2]], it[3])

        with nc.Block() as block:
            @block.tensor
            def _(e):
                run(e, prog["pe"])

            @block.scalar
            def _(e):
                run(e, prog["act"])

            @block.vector
            def _(e):
                run(e, prog["dve"])

            @block.gpsimd
            def _(e):
                run(e, prog["pool"])

            @block.sync
            def _(e):
                run(e, prog["sp"])


W_INPUTS = [
    ("conv_mod_w", [D, 3 * D]), ("conv_mod_b", [3 * D]),
    ("conv_pw1_w", [D, 2 * D]), ("conv_pw1_b", [2 * D]),
    ("conv_dw_w", [CONVW, D]), ("conv_dw_b", [D]),
    ("conv_norm_g", [D]), ("conv_norm_b", [D]),
    ("conv_pw2_w", [D, D]), ("conv_pw2_b", [D]),
    ("attn_mod_w", [D, 3 * D]), ("attn_mod_b", [3 * D]),
    ("attn_qkv_w", [D, 3 * D]),
    ("attn_lam_q1", [64]), ("attn_lam_k1", [64]), ("attn_lam_q2", [64]), ("attn_lam_k2", [64]),
    ("attn_subln_g", [128]),
    ("attn_out_w", [D, D]),
    ("mlp_mod_w", [2, D, 3 * D]), ("mlp_mod_b", [2, 3 * D]),
    ("mlp_w1", [2, D, DFF]), ("mlp_w2", [2, DFF, D]),
    ("post_mix_g", [2, D]), ("post_mix_b", [2, D]),
    ("post_mlp_g", [2, D]), ("post_mlp_b", [2, D]),
]


def build(mode, ng=NG, strict_same=True, debug=False):
    doA = mode in ("A", "F")
    doB = mode in ("B", "F")
    nc = bass.Bass("TRN2", target_bir_lowering=False)
    I = {}

    def din(name, shape, dt=F32):
        I[name] = nc.dram_tensor(name, list(shape), dt, kind="ExternalInput").ap()
        return I[name]

    for name, shape in W_INPUTS:
        din(name, shape)
    din("cvec", [D])
    din("ident", [128, 128])
    if doA:
        din("xs", [2 * ng, TT, D])
        din("hv", [128, 2 * NG])
    if doB:
        din("biasarr", [NH, 128, 2304])
        din("ch", [128, NH])

    inter = "Internal" if mode == "F" else None
    x1_scr = nc.dram_tensor("x1_scr", [ng, GR, D], F32,
                            kind=inter or ("ExternalOutput" if mode == "A" else "ExternalInput")).ap()
    qt_scr = nc.dram_tensor("qt_scr", [16 * 64, NG * GR], BF16,
                            kind=inter or ("ExternalOutput" if mode == "A" else "ExternalInput")).ap()
    kv_all = nc.dram_tensor("kv_all", [4096, NG * GR], BF16,
                            kind=inter or ("ExternalOutput" if mode == "A" else "ExternalInput")).ap()
    if doB:
        out = nc.dram_tensor("out", [ng, GR, D], F32, kind="ExternalOutput").ap()
    NCH = 46
    if debug and doB:
        dbg_at = nc.dram_tensor("dbg_at", [128, 4, D], BF16, kind="ExternalOutput").ap()
        dbg_xq = nc.dram_tensor("dbg_xq", [128, 4, D], F32, kind="ExternalOutput").ap()
    if debug and doA:
        dbg_ht = nc.dram_tensor("dbg_ht", [128, 8, TT], BF16, kind="ExternalOutput").ap()
        dbg_acc = nc.dram_tensor("dbg_acc", [128, 8, GR], F32, kind="ExternalOutput").ap()
        dbg_vt = nc.dram_tensor("dbg_vt", [128, 8, GR], BF16, kind="ExternalOutput").ap()
        dbg_x0 = nc.dram_tensor("dbg_x0", [128, 4, D], F32, kind="ExternalOutput").ap()
        dbg_cv = nc.dram_tensor("dbg_cv", [128, 64], F32, kind="ExternalOutput").ap()
    wscr = nc.dram_tensor("wscr", [NCH, 128, 4096], BF16, kind="Internal").ap()
    modscr = nc.dram_tensor("modscr", [4, 3 * D], F32, kind="Internal").ap()

    st = ExitStack()
    with st:
        sch = Sched(nc, st, strict_same=strict_same)

        def sb(name, shape, dt, stack=st):
            return stack.enter_context(nc.sbuf_tensor(name, list(shape), dt))

        NWR = 3
        WR = sb("WR", [128, NWR, 4096], BF16)
        X = sb("X", [128, 5, D], F32)
        XB = sb("XB", [128, 5, D], BF16)
        HT = sb("HT", [128, 8, TT], BF16)
        HID = sb("HID", [128, 32, GR], BF16)
        Z = sb("Z", [128, 2, D], F32)
        ZN = sb("ZN", [128, 2, D], F32)
        XO = sb("XO", [128, 2, D], F32)
        BC = sb("BC", [128, 6, D], F32)
        RT = sb("RT", [128, 2, GR], F32)
        CV = sb("CV", [128, 8 * 8], F32)
        IDB = sb("IDB", [128, 128], BF16)
        STt = sb("STt", [128, 2, 12], F32)
        MV = sb("MV", [128, 2, 4], F32)
        NHALF = sb("NHALF", [128, GR], F32)
        PS = [st.enter_context(nc.psum_tensor("PS%d" % i, [128, 1024], F32)) for i in range(4)]

        def bank(b):
            return PS[b // 2][:, (b % 2) * 512:(b % 2) * 512 + 512]

        bptr = [0]
        nbmod = [8]

        def nb():
            b = bptr[0]
            bptr[0] = (b + 1) % nbmod[0]
            return b

        def nb2():
            if bptr[0] % 2:
                bptr[0] = (bptr[0] + 1) % 8
            b = bptr[0]
            bptr[0] = (b + 2) % 8
            return b

        def mm(out_ap, lhsT, rhs, start, stop, reads, writes, signal=None):
            sig = stop if signal is None else signal
            sch.op("pe", lambda e: e.matmul(out_ap, lhsT=lhsT, rhs=rhs, start=start, stop=stop),
                   reads, writes, signal=sig)

        def act(out_ap, in_ap, func, reads, writes, bias=None, scale=None):
            kw = {}
            if bias is not None:
                kw["bias"] = bias
            if scale is not None:
                kw["scale"] = scale
            sch.op("act", lambda e: e.activation(out=out_ap, in_=in_ap, func=func, **kw), reads, writes)

        def tt_(eng, out_ap, a, b, op, reads, writes):
            sch.op(eng, lambda e: e.tensor_tensor(out=out_ap, in0=a, in1=b, op=op), reads, writes)

        def ts_(eng, out_ap, a, s1, s2, op0, op1, reads, writes):
            if s2 is None:
                sch.op(eng, lambda e: e.tensor_scalar(out=out_ap, in0=a, scalar1=s1, scalar2=None, op0=op0),
                       reads, writes)
            else:
                sch.op(eng, lambda e: e.tensor_scalar(out=out_ap, in0=a, scalar1=s1, scalar2=s2, op0=op0, op1=op1),
                       reads, writes)

        def stt(out_ap, a, s, b, op0, op1, reads, writes, accum=None):
            if accum is None:
                sch.op("dve", lambda e: e.scalar_tensor_tensor(out=out_ap, in0=a, scalar=s, in1=b, op0=op0, op1=op1),
                       reads, writes)
            else:
                sch.op("dve", lambda e: e.scalar_tensor_tensor(out=out_ap, in0=a, scalar=s, in1=b, op0=op0, op1=op1,
                                                               accum_out=accum), reads, writes)

        def cp(eng, out_ap, in_ap, reads, writes):
            if eng == "act":
                sch.op("act", lambda e: e.copy(out=out_ap, in_=in_ap), reads, writes)
            else:
                sch.op(eng, lambda e: e.tensor_copy(out=out_ap, in_=in_ap), reads, writes)

        def dma(issuer, lane, out_ap, in_ap, reads, writes, nonc=False):
            if nonc:
                sch.dma(issuer, lane, lambda e: e.dma_start(out=out_ap, in_=in_ap, allow_slow_non_contiguous=True),
                        reads, writes)
            else:
                sch.dma(issuer, lane, lambda e: e.dma_start(out=out_ap, in_=in_ap), reads, writes)

        chunk_id = {}
        cvl = [0]

        cvt_jobs = []

        def cvt(out_ap, in_ap, ci):
            cvt_jobs.append((out_ap, in_ap, ci))

        def emit_cvts(nmax=None, nodep=False):
            k = 0
            while cvt_jobs and (nmax is None or k < nmax):
                (out_ap, in_ap, ci) = cvt_jobs.pop(0)
                lane = "cv%d" % (cvl[0] % 4)
                first = (k == 0) and not nodep
                cvl[0] += 1
                k += 1
                dma("pool", lane, out_ap, in_ap, [("modscr", s_, n_) for s_ in range(4) for n_ in range(6)] if first else [], [("wscr", ci)])

        def add_kmajor(name, W, ncols):
            Wv = W.rearrange("(k p) n -> p k n", p=128)
            for n in range(ncols // 512):
                ci = len(chunk_id)
                chunk_id[(name, n)] = ci
                cvt(wscr[ci].rearrange("p (k n) -> p k n", k=8), Wv[:, :, n * 512:(n + 1) * 512], ci)

        def add_w2(name, W):
            Wv = W.rearrange("(f p) n -> p f n", p=128)
            for c in range(8):
                ci = len(chunk_id)
                chunk_id[(name, c)] = ci
                cvt(wscr[ci].rearrange("p (f n) -> p f n", f=4), Wv[:, 4 * c:4 * c + 4, :], ci)

        if doA:
            Wv = I["conv_pw1_w"].rearrange("(k p) n -> p k n", p=128)
            for j in range(4):
                ci = len(chunk_id)
                chunk_id[("pw1", j)] = ci
                dst = wscr[ci].rearrange("p (k n) -> p k n", k=8)
                cvt(dst[:, :, 0:256], Wv[:, :, 256 * j:256 * j + 256], ci)
                cvt(dst[:, :, 256:512], Wv[:, :, 1024 + 256 * j:1024 + 256 * j + 256], ci)
            add_kmajor("pw2", I["conv_pw2_w"], D)
            add_kmajor("w1_0", I["mlp_w1"][0], DFF)
            add_w2("w2_0", I["mlp_w2"][0])
            add_kmajor("qkv", I["attn_qkv_w"], 3 * D)
        if doB:
            add_kmajor("wo", I["attn_out_w"], D)
            add_kmajor("w1_1", I["mlp_w1"][1], DFF)
            add_w2("w2_1", I["mlp_w2"][1])

        class WStream:
            def __init__(self, seq):
                self.seq = seq
                self.issued = 0
                self.pos = 0
                self.released = 0

            def _issue(self):
                k = self.issued
                slot = k % NWR
                ci = chunk_id[self.seq[k]]
                assert ("wscr", ci) in sch.last_w, ("weight chunk loaded before its conversion was emitted", self.seq[k])
                dma("sp", "wr%d" % slot, WR[:, slot, :], wscr[ci], [("wscr", ci)], [("WR", slot)])
                self.issued += 1

            def prefetch(self):
                while self.issued < len(self.seq) and self.issued - NWR < self.released:
                    self._issue()

            def get(self, name):
                assert self.seq[self.pos] == name, (self.seq[self.pos], name)
                self.prefetch()
                assert self.issued > self.pos
                slot = self.pos % NWR
                self.pos += 1
                return slot

            def release(self, n=1):
                self.released += n
                self.prefetch()

        if doA:
            CA = sb("CA", [128, 8 * 5 + CONVW * 8], F32)
            HVt = sb("HVt", [128, 2 * NG], F32)
            ONES = sb("ONES", [128, 128], F32)
            ONEB = sb("ONEB", [1, 128], BF16)
            PB2 = sb("PB2", [1, D], BF16)
            dma("sp", "pl6", HVt[:, :], I["hv"], [], [("HV",)])
            dma("pool", "pl0", PB2[:, :], I["conv_pw2_b"].rearrange("(o n) -> o n", o=1), [], [("PB2",)])
            sch.op("dve", lambda e: e.memset(ONES[:, :], 1.0 / D), [], [("ONES",)])
            sch.op("dve", lambda e: e.memset(ONEB[:, :], 1.0), [], [("ONEB",)])
        pst = ExitStack()
        with pst:
            MW = sb("MW", [128, 2, 8, 512], F32, pst)
            SREP = sb("SREP", [128, 8, 128], F32, pst)
            CCOL = sb("CCOL", [128, 8], F32, pst)
            SCOL = sb("SCOL", [128, 8], F32, pst)
            MB = sb("MB", [128, 2, 512], F32, pst)
            MROW = sb("MROW", [128, 2, 512], F32, pst)
            mwc = [0]
            TMPC4 = sb("TMPC", [128, 4, 48], F32, pst)
            IDF = sb("IDF", [32, 32], F32, pst)
            E0 = sb("E0", [128, 1], F32, pst)
            ROWS = sb("ROWS", [16, D], F32, pst)
            TAPR = sb("TAPR", [32, D], F32, pst)
            COLV = sb("COLV", [128, 8, 11], F32, pst)
            MCOL = sb("MCOL", [128, 64], F32, pst)
            nbmod[0] = 5
            dma("sp", "pl7", IDF[:, :], I["ident"][0:32, 0:32], [], [("IDF",)])
            dma("sp", "pl7", E0[:, :], I["ident"][:, 0:1], [], [("E0",)], nonc=True)
            rowvecs = [I["post_mix_g"][0], I["post_mix_b"][0], I["post_mlp_g"][0], I["post_mlp_b"][0],
                       I["post_mix_g"][1], I["post_mix_b"][1]]
            if doA:
                rowvecs += [I["conv_pw1_b"][0:D], I["conv_pw1_b"][D:2 * D], I["conv_dw_b"], I["conv_norm_g"], I["conv_norm_b"]]
            for vi, vec in enumerate(rowvecs):
                dma("sp", "pl8", ROWS[vi:vi + 1, :], vec.rearrange("(o n) -> o n", o=1), [], [("ROWS",)])
            nv = len(rowvecs)
            for c in range(8):
                mm(bank(6)[:, c * 11:c * 11 + nv], ROWS[0:nv, c * 128:(c + 1) * 128], IDF[0:nv, 0:nv], True, True,
                   [("ROWS",), ("IDF",)], [("ps", 6)], signal=(c == 7))
            cp("dve", COLV[:, :, :], bank(6)[:, 0:88].rearrange("p (c v) -> p c v", v=11), [("ps", 6)], [("COLV",)])
            if doA:
                dma("sp", "pl8", TAPR[0:CONVW, :], I["conv_dw_w"], [], [("TAPR",)])
                for c in range(8):
                    mm(bank(5)[:, c * CONVW:(c + 1) * CONVW], TAPR[0:CONVW, c * 128:(c + 1) * 128], IDF[0:CONVW, 0:CONVW], True, True,
                       [("TAPR",), ("IDF",)], [("ps", 5)], signal=(c == 7))
                cp("dve", CA[:, 40:40 + CONVW * 8].rearrange("p (j c) -> p j c", c=8),
                   bank(5)[:, 0:CONVW * 8].rearrange("p (c j) -> p j c", j=CONVW), [("ps", 5)], [("CAt",)])
                for k5 in range(5):
                    cp("dve", CA[:, k5 * 8:(k5 + 1) * 8], COLV[:, :, 6 + k5], [("COLV",)], [("CA", k5)])

            dma("pool", "pl0", IDB[:, :], I["ident"], [], [("IDB",)])
            sch.op("dve", lambda e: e.memset(NHALF[:, :], -0.5), [], [("NHALF",)])
            dma("sp", "pl1", CCOL[:, :], I["cvec"].rearrange("(k p) -> p k", p=128), [], [("CCOL",)], nonc=True)
            act(SCOL[:, :], CCOL[:, :], AF.Sigmoid, [("CCOL",)], [("SCOL",)])
            tt_("dve", SCOL[:, :], SCOL[:, :], CCOL[:, :], ALU.mult, [("SCOL",), ("CCOL",)], [("SCOL",)])
            for k in range(8):
                cp("dve", SREP[:, k, :], SCOL[:, k:k + 1].to_broadcast([128, 128]), [("SCOL",)], [("SREP",)])
            if doA:
                emit_cvts(8, nodep=True)
            mods = []
            if doA:
                mods += [(0, I["conv_mod_w"], I["conv_mod_b"]), (1, I["mlp_mod_w"][0], I["mlp_mod_b"][0])]
            mods += [(2, I["attn_mod_w"], I["attn_mod_b"])]
            if doB:
                mods += [(3, I["mlp_mod_w"][1], I["mlp_mod_b"][1])]
            for (s, mw, mb) in mods:
                mwv = mw.rearrange("(k p) n -> p k n", p=128)
                for n in range(6):
                    q = mwc[0] % 2
                    mwc[0] += 1
                    dma("sp", "pl2%d" % q, MW[:, q, :, :], mwv[:, :, n * 512:(n + 1) * 512], [], [("MW", q)])
                    dma("sp", "pl3%d" % q, MB[:, q, :], mb[n * 512:(n + 1) * 512].partition_broadcast(128), [], [("MB", q)])
                    b = nb()
                    for k in range(8):
                        mm(bank(b), SREP[:, k, :], MW[:, q, k, :], k == 0, k == 7,
                           [("SREP",), ("MW", q)], [("ps", b)])
                    tt_("dve", MROW[:, q, :], bank(b), MB[:, q, :], ALU.add, [("ps", b), ("MB", q)], [("MROW", q)])
                    if n < 4:
                        for fc in range(4):
                            col = s * 16 + n * 4 + fc
                            mm(bank(7)[:, col:col + 1], MROW[:, q, fc * 128:(fc + 1) * 128], E0[:, 0:1], True, True,
                               [("MROW", q), ("E0",)], [("ps", 7)], signal=(fc == 3))
                    dma("act", "pl4%d" % q, modscr[s:s + 1, n * 512:(n + 1) * 512], MROW[0:1, q, :], [("MROW", q)], [("modscr", s, n)])

            if mode != "F":
                emit_cvts(None, nodep=True)
            cp("dve", MCOL[:, :], bank(7)[:, 0:64], [("ps", 7)], [("MCOL",)])
            lnv = {1: (0, 1), 2: (2, 3), 3: (4, 5)}
            for (s, _, _) in mods:
                G2 = CV[:, s * 16:s * 16 + 8]
                B2 = CV[:, s * 16 + 8:s * 16 + 16]
                TMPC = TMPC4[:, s, :]
                SH = MCOL[:, s * 16:s * 16 + 8]
                ts_("dve", TMPC[:, 8:16], MCOL[:, s * 16 + 8:s * 16 + 16], 1.0, None, ALU.add, None, [("MCOL",)], [("TMPC", s, 1)])
                if s == 0:
                    cp("dve", G2, TMPC[:, 8:16], [("TMPC", s, 1)], [("CV", s)])
                    cp("dve", B2, SH, [("MCOL",)], [("CV", s)])
                else:
                    gi, bi = lnv[s]
                    tt_("dve", G2, COLV[:, :, gi], TMPC[:, 8:16], ALU.mult, [("COLV",), ("TMPC", s, 1)], [("CV", s)])
                    tt_("dve", TMPC[:, 32:40], COLV[:, :, bi], TMPC[:, 8:16], ALU.mult, [("COLV",), ("TMPC", s, 1)], [("TMPC", s, 4)])
                    tt_("dve", B2, TMPC[:, 32:40], SH, ALU.add, [("TMPC", s, 4), ("MCOL",)], [("CV", s)])
            nbmod[0] = 8
            bptr[0] = 0
            sch.flush()

        def load_bc(slot, vec, lane="bc"):
            dma("sp", "bc%d" % slot, BC[:, slot, :], vec.partition_broadcast(128),
                [("modscr", s_, n_) for s_ in range(4) for n_ in range(6)], [("BC", slot)])

        def to_hT(s, halo):
            for c in range(8):
                b = nb()
                for t in range(4):
                    mm(bank(b)[:, t * 128:(t + 1) * 128], XB[:, 1 + t, c * 128:(c + 1) * 128], IDB[:, :], True, True,
                       [("XB", 1 + t), ("IDB",)], [("ps", b)], signal=(t == 3))
                act(HT[:, c, HALO:TT], bank(b), AF.Identity, [("ps", b), ("CV", s)], [("HT", c)],
                    bias=CV[:, s * 16 + 8 + c:s * 16 + 9 + c], scale=CV[:, s * 16 + c:s * 16 + c + 1])
            if halo:
                b = nb()
                for c in range(8):
                    mm(bank(b)[:, c * 32:(c + 1) * 32], XB[0:32, 0, c * 128:(c + 1) * 128], IDB[0:32, 0:32], True, True,
                       [("XB", 0), ("IDB",)], [("ps", b)], signal=(c == 7))
                for c in range(8):
                    act(HT[:, c, 0:HALO], bank(b)[:, c * 32:(c + 1) * 32], AF.Identity, [("ps", b), ("CV", s)], [("HT", c)],
                        bias=CV[:, s * 16 + 8 + c:s * 16 + 9 + c], scale=CV[:, s * 16 + c:s * 16 + c + 1])

        def epilogue(t, pb, res_ap, res_key, bcs, out_ap, out_key, znb):
            zb = t % 2
            P = PS[pb // 2][:, :]
            kz, kn = ("Z", zb), ("ZN", zb)
            tt_("dve", Z[:, zb, :], P, BC[:, bcs, :], ALU.mult, [("ps", pb), ("ps", pb + 1), ("BC", bcs)], [kz])
            stt(Z[:, zb, :], res_ap, ALPHA, Z[:, zb, :], ALU.mult, ALU.add, [res_key, kz], [kz])
            for hh in range(2):
                sch.op("dve", lambda e, hh=hh: e.bn_stats(out=STt[:, zb, hh * 6:hh * 6 + 6], in_=Z[:, zb, hh * 512:(hh + 1) * 512]),
                       [kz], [("ST", zb)])
            sch.op("dve", lambda e: e.bn_aggr(out=MV[:, zb, 0:2], in_=STt[:, zb, :]), [("ST", zb)], [("MV", zb)])
            ts_("dve", MV[:, zb, 1:2], MV[:, zb, 1:2], LN_EPS, None, ALU.add, None, [("MV", zb)], [("MV", zb)])
            tt_("pool", MV[:, zb, 2:3], MV[:, zb, 1:2], NHALF[:, 0:1], ALU.pow, [("MV", zb), ("NHALF",)], [("MV", zb)])
            stt(MV[:, zb, 3:4], MV[:, zb, 0:1], -1.0, MV[:, zb, 2:3], ALU.mult, ALU.mult, [("MV", zb)], [("MV", zb)])
            if znb:
                act(XB[:, 1 + t, :], Z[:, zb, :], AF.Identity, [kz, ("MV", zb)], [("XB", 1 + t)],
                    bias=MV[:, zb, 3:4], scale=MV[:, zb, 2:3])
            if out_ap is None:
                return lambda: None
            act(ZN[:, zb, :], Z[:, zb, :], AF.Identity, [kz, ("MV", zb)], [kn],
                bias=MV[:, zb, 3:4], scale=MV[:, zb, 2:3])

            def tail():
                tt_("pool", out_ap, ZN[:, zb, :], BC[:, bcs + 1, :], ALU.mult, [kn, ("BC", bcs + 1)], [out_key])
                tt_("pool", out_ap, out_ap, BC[:, bcs + 2, :], ALU.add, [out_key, ("BC", bcs + 2)], [out_key])
            return tail

        def mlp(ws, s, w1n, w2n):
            to_hT(s, False)
            for f in range(8):
                slot = ws.get((w1n, f))
                for fl in range(4):
                    fc = 4 * f + fl
                    b = nb()
                    for k in range(8):
                        mm(bank(b), WR[:, slot, k * 512 + fl * 128:k * 512 + fl * 128 + 128], HT[:, k, HALO:TT], k == 0, k == 7,
                           [("WR", slot), ("HT", k)], [("ps", b)])
                    rb = fc % 2
                    act(RT[:, rb, :], bank(b), AF.Relu, [("ps", b)], [("RT", rb)])
                    tt_("pool", HID[:, fc, :], RT[:, rb, :], RT[:, rb, :], ALU.mult, [("RT", rb)], [("HID", fc)])
                ws.release()
            for c in range(8):
                slot = ws.get((w2n, c))
                for t in range(4):
                    for half in range(2):
                        b = 2 * t + half
                        for fl in range(4):
                            mm(bank(b), HID[:, 4 * c + fl, t * 128:(t + 1) * 128],
                               WR[:, slot, fl * 1024 + half * 512:fl * 1024 + half * 512 + 512],
                               c == 0 and fl == 0, c == 7 and fl == 3,
                               [("HID", 4 * c + fl), ("WR", slot)], [("ps", b)], signal=(fl == 3))
                ws.release()
            bptr[0] = 0

        if doA:
            ast = ExitStack()
            with ast:
                U = sb("U", [128, 2, TT], BF16, ast)
                NDG = 8
                DG = sb("DG", [128, NDG, 128], BF16, ast)
                dgc = [0]
                SG = sb("SG", [128, 2, TT], F32, ast)
                ACC = sb("ACC", [128, 8, GR], F32, ast)
                SQ = sb("SQ", [128, 2, GR], F32, ast)
                UN = sb("UN", [128, 2, GR], F32, ast)
                VT = sb("VT", [128, 8, GR], BF16, ast)
                STB = sb("STB", [128, 4, GR], F32, ast)
                QS = sb("QS", [128, 2, GR], BF16, ast)
                VS = sb("VS", [128, 2, D], BF16, ast)
                load_bc(0, modscr[0, 2 * D:3 * D])
                load_bc(1, I["post_mix_g"][0])
                load_bc(2, I["post_mix_b"][0])
                load_bc(3, modscr[1, 2 * D:3 * D])
                load_bc(4, I["post_mlp_g"][0])
                load_bc(5, I["post_mlp_b"][0])

                seqA = []
                for pp in range(2 * ng):
                    seqA += [("pw1", j) for j in range(4)] + [("pw2", n) for n in range(2)]
                    seqA += [("w1_0", f) for f in range(8)] + [("w2_0", c) for c in range(8)]
                    seqA += [("qkv", n) for n in range(0 if pp % 2 == 0 else 2, 6)]
                ws = WStream(seqA)

                def load_x(g):
                    dma("sp", "xl0", X[0:HALO, 0, :], I["xs"][g, 0:HALO, :], [], [("X", 0)])
                    dma("sp", "xl1", X[:, 1:5, :], I["xs"][g, HALO:TT, :].rearrange("(t p) d -> p t d", p=128),
                        [], [("X", 1), ("X", 2), ("X", 3), ("X", 4)])

                load_x(0)
                ws.prefetch()
                for pp in range(2 * ng):
                    g = pp // 2
                    own = (pp % 2 == 0)
                    kvl = 0 if own else 1
                    if pp == 0:
                        emit_cvts(4, nodep=True)
                    cp("dve", XB[0:HALO, 0, :], X[0:HALO, 0, :], [("X", 0)], [("XB", 0)])
                    for t in range(4):
                        cp("dve" if t % 2 else "act", XB[:, 1 + t, :], X[:, 1 + t, :], [("X", 1 + t)], [("XB", 1 + t)])
                    to_hT(0, True)
                    if pp == 0:
                        emit_cvts(4, nodep=True)
                    if debug and pp == 0:
                        dma("sp", "dbg", dbg_ht[:, :, :], HT[:, :, :], [("HT", c) for c in range(8)], [("dbg", 0)])
                        dma("sp", "dbg", dbg_cv[:, :], CV[:, :], [("CV", 0)], [("dbg", 4)])
                    nbmod[0] = 6
                    bptr[0] = 0
                    pw1_slot = {}
                    pw1_banks = {}

                    def pw1_glu(i):
                        j, s2 = i // 2, i % 2
                        if s2 == 0:
                            pw1_slot[j] = ws.get(("pw1", j))
                        slot = pw1_slot[j]
                        ub = i % 2
                        ba, bg, bh = nb(), nb(), nb()
                        for (bk, col0) in ((ba, 128 * s2), (bg, 256 + 128 * s2)):
                            for k in range(8):
                                mm(bank(bk), WR[:, slot, k * 512 + col0:k * 512 + col0 + 128], HT[:, k, HALO:TT], k == 0, k == 7,
                                   [("WR", slot), ("HT", k)], [("ps", bk)])
                        for hi, col0 in enumerate((128 * s2, 256 + 128 * s2)):
                            for k in range(8):
                                mm(bank(bh)[:, hi * 32:hi * 32 + 32], WR[:, slot, k * 512 + col0:k * 512 + col0 + 128], HT[:, k, 0:HALO],
                                   k == 0, k == 7, [("WR", slot), ("HT", k)], [("ps", bh)], signal=(k == 7 and hi == 1))
                        if s2 == 1:
                            ws.release()
                        pw1_banks[i] = (ba, bg, bh)

                    def glu_elem(i):
                        ub = i % 2
                        ba, bg, bh = pw1_banks[i]
                        act(SG[:, ub, HALO:TT], bank(bg), AF.Sigmoid, [("ps", bg), ("CA", 0), ("CA", 1), ("CA", 2), ("CA", 3), ("CA", 4), ("CAt",)], [("SG", ub)], bias=CA[:, 8 + i:9 + i])
                        act(SG[:, ub, 0:HALO], bank(bh)[:, 32:64], AF.Sigmoid, [("ps", bh), ("CA", 0), ("CA", 1), ("CA", 2), ("CA", 3), ("CA", 4), ("CAt",)], [("SG", ub)], bias=CA[:, 8 + i:9 + i])
                        stt(U[:, ub, HALO:TT], bank(ba), CA[:, i:i + 1], SG[:, ub, HALO:TT], ALU.add, ALU.mult,
                            [("ps", ba), ("SG", ub), ("CA", 0), ("CA", 1), ("CA", 2), ("CA", 3), ("CA", 4), ("CAt",)], [("U", ub)])
                        stt(U[:, ub, 0:HALO], bank(bh)[:, 0:32], CA[:, i:i + 1], SG[:, ub, 0:HALO], ALU.add, ALU.mult,
                            [("ps", bh), ("SG", ub), ("CA", 0), ("CA", 1), ("CA", 2), ("CA", 3), ("CA", 4), ("CAt",)], [("U", ub)])
                        ts_("dve", U[:, ub, 0:HALO], U[:, ub, 0:HALO], HVt[:, pp:pp + 1], None, ALU.mult, None,
                            [("U", ub), ("HV",)], [("U", ub)])

                    def dwconv(i):
                        ub = i % 2
                        bc_ = nb()
                        for jt in range(CONVW):
                            dgs = dgc[0] % NDG
                            dgc[0] += 1
                            if jt % 2 == 0:
                                ts_("dve", DG[:, dgs, :], IDB[:, :], CA[:, 40 + jt * 8 + i:41 + jt * 8 + i], None, ALU.mult, None,
                                    [("IDB",), ("CA", 0), ("CA", 1), ("CA", 2), ("CA", 3), ("CA", 4), ("CAt",)], [("DG", dgs)])
                            else:
                                act(DG[:, dgs, :], IDB[:, :], AF.Copy, [("IDB",), ("CA", 0), ("CA", 1), ("CA", 2), ("CA", 3), ("CA", 4), ("CAt",)], [("DG", dgs)],
                                    scale=CA[:, 40 + jt * 8 + i:41 + jt * 8 + i])
                            mm(bank(bc_), DG[:, dgs, :], U[:, ub, 2 + jt:2 + jt + GR], jt == 0, jt == CONVW - 1,
                               [("DG", dgs), ("U", ub)], [("ps", bc_)], signal=True)
                        act(ACC[:, i, :], bank(bc_), AF.Identity, [("ps", bc_), ("CA", 0), ("CA", 1), ("CA", 2), ("CA", 3), ("CA", 4), ("CAt",)], [("ACC", i)], bias=CA[:, 16 + i:17 + i])
                        act(SQ[:, ub, :], ACC[:, i, :], AF.Square, [("ACC", i)], [("SQ", ub)])
                        mm(bank(6), ONES[:, :], ACC[:, i, :], i == 0, i == 7, [("ONES",), ("ACC", i)], [("ps", 6)], signal=True)
                        mm(bank(7), ONES[:, :], SQ[:, ub, :], i == 0, i == 7, [("ONES",), ("SQ", ub)], [("ps", 7)], signal=True)

                    pw1_glu(0)
                    glu_elem(0)
                    for i in range(8):
                        if i + 1 < 8:
                            pw1_glu(i + 1)
                        dwconv(i)
                        if pp == 0 and i in (1, 4):
                            emit_cvts(4, nodep=True)
                        if i + 1 < 8:
                            glu_elem(i + 1)
                    bm, bq = 6, 7
                    nbmod[0] = 8
                    bptr[0] = 0
                    cp("dve", STB[:, 0, :], bank(bm), [("ps", bm)], [("STB", 0)])
                    tt_("dve", STB[:, 3, :], STB[:, 0, :], STB[:, 0, :], ALU.mult, [("STB", 0)], [("STB", 3)])
                    tt_("dve", STB[:, 1, :], bank(bq), STB[:, 3, :], ALU.subtract, [("ps", bq), ("STB", 3)], [("STB", 1)])
                    act(STB[:, 1, :], STB[:, 1, :], AF.Ln, [("STB", 1)], [("STB", 1)], bias=LN_EPS)
                    act(STB[:, 1, :], STB[:, 1, :], AF.Exp, [("STB", 1)], [("STB", 1)], scale=-0.5)
                    stt(STB[:, 2, :], STB[:, 0, :], -1.0, STB[:, 1, :], ALU.mult, ALU.mult, [("STB", 0), ("STB", 1)], [("STB", 2)])
                    for i in range(8):
                        ub = i % 2
                        tt_("dve", UN[:, ub, :], ACC[:, i, :], STB[:, 1, :], ALU.mult, [("ACC", i), ("STB", 1)], [("UN", ub)])
                        tt_("dve", UN[:, ub, :], UN[:, ub, :], STB[:, 2, :], ALU.add, [("UN", ub), ("STB", 2)], [("UN", ub)])
                        act(SQ[:, ub, :], UN[:, ub, :], AF.Identity, [("UN", ub), ("CA", 0), ("CA", 1), ("CA", 2), ("CA", 3), ("CA", 4), ("CAt",)], [("SQ", ub)],
                            bias=CA[:, 32 + i:33 + i], scale=CA[:, 24 + i:25 + i])
                        act(RT[:, ub, :], UN[:, ub, :], AF.Sigmoid, [("UN", ub), ("CA", 0), ("CA", 1), ("CA", 2), ("CA", 3), ("CA", 4), ("CAt",)], [("RT", ub)],
                            bias=CA[:, 32 + i:33 + i], scale=CA[:, 24 + i:25 + i])
                        tt_("pool", VT[:, i, :], SQ[:, ub, :], RT[:, ub, :], ALU.mult, [("SQ", ub), ("RT", ub)], [("VT", i)])
                    if debug and pp == 0:
                        dma("sp", "dbg", dbg_acc[:, :, :], ACC[:, :, :], [("ACC", c) for c in range(8)], [("dbg", 1)])
                        dma("sp", "dbg", dbg_vt[:, :, :], VT[:, :, :], [("VT", c) for c in range(8)], [("dbg", 2)])
                    s0 = ws.get(("pw2", 0))
                    s1 = ws.get(("pw2", 1))
                    for t in range(4):
                        pb = nb2()
                        for half, slot in ((0, s0), (1, s1)):
                            b = pb + half
                            mm(bank(b), ONEB[0:1, :], PB2[0:1, half * 512:(half + 1) * 512], True, False,
                               [("ONEB",), ("PB2",)], [("ps", b)])
                            for k in range(8):
                                mm(bank(b), VT[:, k, t * 128:(t + 1) * 128], WR[:, slot, k * 512:(k + 1) * 512], False, k == 7,
                                   [("VT", k), ("WR", slot)], [("ps", b)])
                        if t == 3:
                            ws.release(2)
                        tl = epilogue(t, pb, X[:, 1 + t, :], ("X", 1 + t), 0, X[:, 1 + t, :], ("X", 1 + t), True)
                        if t > 0:
                            prev_tail()
                        prev_tail = tl
                    prev_tail()
                    if debug and pp == 0:
                        dma("sp", "dbg", dbg_x0[:, :, :], X[:, 1:5, :], [("X", 1 + t) for t in range(4)], [("dbg", 3)])
                    if pp == 0:
                        emit_cvts(8, nodep=True)
                    mlp(ws, 1, "w1_0", "w2_0")
                    tails = []
                    for t in range(4):
                        if own:
                            tl = epilogue(t, 2 * t, X[:, 1 + t, :], ("X", 1 + t), 3, XO[:, t % 2, :], ("XO", t % 2), True)

                            def fin(t=t, tl=tl):
                                tl()
                                dma("sp", "xo%d" % (t % 2), x1_scr[g, t * 128:(t + 1) * 128, :], XO[:, t % 2, :],
                                    [("XO", t % 2)], [("x1", g, t)])
                            tails.append(fin)
                            if t > 0:
                                tails[t - 1]()
                        else:
                            epilogue(t, 2 * t, X[:, 1 + t, :], ("X", 1 + t), 3, None, None, True)
                    if own:
                        tails[3]()
                    if pp + 1 < 2 * ng:
                        load_x(pp + 1)
                    emit_cvts(2)
                    to_hT(2, False)
                    for n in range(0 if own else 2, 4):
                        slot = ws.get(("qkv", n))
                        for cg in range(4):
                            b = nb()
                            for k in range(8):
                                mm(bank(b), WR[:, slot, k * 512 + cg * 128:k * 512 + cg * 128 + 128], HT[:, k, HALO:TT], k == 0, k == 7,
                                   [("WR", slot), ("HT", k)], [("ps", b)])
                            qb = (n * 4 + cg) % 2
                            if n < 2:
                                act(QS[:, qb, :], bank(b), AF.Copy, [("ps", b)], [("QS", qb)], scale=0.125)
                            else:
                                cp("dve", QS[:, qb, :], bank(b), [("ps", b)], [("QS", qb)])
                            r0 = ((n % 2) * 8 + cg * 2) * 64
                            if n < 2:
                                dst = qt_scr[r0:r0 + 128, g * GR:(g + 1) * GR]
                            else:
                                dst = kv_all[kvl * 2048 + r0:kvl * 2048 + r0 + 128, g * GR:(g + 1) * GR]
                            dma("sp", "qs%d" % qb, dst, QS[:, qb, :], [("QS", qb)],
                                [("qk", n, cg, g)] if n < 2 else [("kw", kvl, n, cg, g)])
                        ws.release()
                    sv0 = ws.get(("qkv", 4))
                    sv1 = ws.get(("qkv", 5))
                    Vv = kv_all[kvl * 2048 + 1024:kvl * 2048 + 2048, :].rearrange("r (a c) -> (r a) c", a=4)
                    for t in range(4):
                        pb = nb2()
                        for half, slot in ((0, sv0), (1, sv1)):
                            b = pb + half
                            for k in range(8):
                                mm(bank(b), HT[:, k, HALO + t * 128:HALO + (t + 1) * 128], WR[:, slot, k * 512:(k + 1) * 512], k == 0, k == 7,
                                   [("HT", k), ("WR", slot)], [("ps", b)])
                        if t == 3:
                            ws.release(2)
                        vb = t % 2
                        cp("act" if t % 2 else "dve", VS[:, vb, :], PS[pb // 2][:, :], [("ps", pb), ("ps", pb + 1)], [("VS", vb)])
                        dma("sp", "vs%d" % vb, Vv[g * GR + t * 128:g * GR + (t + 1) * 128, :], VS[:, vb, :], [("VS", vb)], [("vw", kvl, g, t)])
                emit_cvts()
                lanesA = ["xo0", "xo1", "qs0", "qs1", "vs0", "vs1"]
                sch.wait_all("sp", lanesA)
                sch.flush()

        if doB:
            bst = ExitStack()
            with bst:
                QT = sb("QT", [128, 2, 2, GR], BF16, bst)
                KT = sb("KT", [128, 3, 2, GR], BF16, bst)
                VA = sb("VA", [128, 3, 4, 132], BF16, bst)
                PT = sb("PT", [128, 2, 1024], BF16, bst)
                BA = sb("BA", [128, 2, 2304], BF16, bst)
                CH = sb("CH", [128, NH], F32, bst)
                LM = sb("LM", [128, 4, 64], F32, bst)
                LS = sb("LS", [128, 8], F32, bst)
                GSUB = sb("GSUB", [128, 128], F32, bst)
                OS = sb("OS", [128, 2, 128], F32, bst)
                SM = sb("SM", [128, 2, 8], F32, bst)
                OJ = sb("OJ", [128, 128], F32, bst)
                OAC = sb("OAC", [128, 1032], F32, bst)

                dma("sp", "pl6", CH[:, :], I["ch"], [], [("CH",)])
                for q, nm in enumerate(("attn_lam_q1", "attn_lam_k1", "attn_lam_q2", "attn_lam_k2")):
                    dma("sp", "pl7", LM[:, q, :], I[nm].partition_broadcast(128), [], [("LM", q)])
                dma("sp", "pl7", GSUB[:, :], I["attn_subln_g"].partition_broadcast(128), [], [("GSUB",)])
                ts_("dve", GSUB[:, :], GSUB[:, :], 1.0 - LAMBDA_INIT, None, ALU.mult, None, [("GSUB",)], [("GSUB",)])
                for q in range(2):
                    stt(LM[:, 2 * q, :], LM[:, 2 * q, :], 1.0, LM[:, 2 * q + 1, :], ALU.mult, ALU.mult,
                        [("LM", 2 * q), ("LM", 2 * q + 1)], [("LM", 2 * q)], accum=LS[:, q:q + 1])
                act(LS[:, 2:4], LS[:, 0:2], AF.Exp, [("LM", 0), ("LM", 2)], [("LS",)])
                tt_("dve", LS[:, 4:5], LS[:, 2:3], LS[:, 3:4], ALU.subtract, [("LS",)], [("LS",)])
                ts_("dve", LS[:, 5:6], LS[:, 4:5], LAMBDA_INIT, -1.0, ALU.add, ALU.mult, [("LS",)], [("LS",)])
                sch.op("dve", lambda e: e.memset(VA[:, :, :, 128:129], 1.0), [], [("VA", 0), ("VA", 1), ("VA", 2)])
                sch.op("dve", lambda e: e.memset(QT[64:128, :, :, :], 0.0), [], [("QT", 0), ("QT", 1)])
                sch.op("dve", lambda e: e.memset(KT[64:128, :, :, :], 0.0), [], [("KT", 0), ("KT", 1), ("KT", 2)])
                load_bc(0, modscr[2, 2 * D:3 * D])
                load_bc(1, I["post_mix_g"][1])
                load_bc(2, I["post_mix_b"][1])
                load_bc(3, modscr[3, 2 * D:3 * D])
                load_bc(4, I["post_mlp_g"][1])
                load_bc(5, I["post_mlp_b"][1])

                seqB = []
                for g in range(ng):
                    seqB += [("wo", n) for n in range(2)] + [("w1_1", f) for f in range(8)] + [("w2_1", c) for c in range(8)]
                ws = WStream(seqB)

                Vall = [kv_all[r * 2048 + 1024:r * 2048 + 2048, :].rearrange("r (a c) -> (r a) c", a=4) for r in range(2)]

                def acc_ap(m, qs):
                    a = m * 4 + qs
                    bk, sl = a // 3, a % 3
                    if bk < 2:
                        return PS[2][:, bk * 512 + sl * 129:bk * 512 + sl * 129 + 129], ("ps", 4 + bk)
                    return PS[3][:, sl * 129:sl * 129 + 129], ("ps", 6)

                kvc = [0]
                hcount = [0]
                for j in range(ng):
                    dma("sp", "xl1", X[:, 1:5, :], x1_scr[j].rearrange("(t p) d -> p t d", p=128),
                        [("x1", j, t) for t in range(4)], [("X", 1), ("X", 2), ("X", 3), ("X", 4)])
                    for h in range(NH):
                        hb = hcount[0] % 2
                        hcount[0] += 1

                        def load_head(jj, hh, hbb):
                            dma("sp", "qt%d" % hbb, QT[0:64, hbb, :, :],
                                qt_scr[2 * hh * 64:2 * hh * 64 + 128, jj * GR:(jj + 1) * GR].rearrange("(m d) t -> d m t", m=2),
                                [("qk", n, cg, jj) for n in range(2) for cg in range(4)], [("QT", hbb)])
                            dma("pool", "ba%d" % hbb, BA[:, hbb, :], I["biasarr"][hh], [], [("BA", hbb)])

                        if j == 0 and h == 0:
                            load_head(0, 0, hb)
                        if h + 1 < NH:
                            load_head(j, h + 1, 1 - hb)
                        elif j + 1 < ng:
                            load_head(j + 1, 0, 1 - hb)
                        blocks = []
                        for i in range(j + 1):
                            for r in range(2):
                                for kb in range(4):
                                    sp_off = None
                                    if i == j:
                                        sp_off = (0 if r == 0 else 896) + 384 - kb * 128
                                    elif i == j - 1 and r == 1 and kb == 3:
                                        sp_off = 1792
                                    blocks.append((r, i, kb, sp_off))
                        kvslot = {}

                        def load_kv(r, i):
                            sl = kvc[0] % 3
                            kvc[0] += 1
                            kvslot[(r, i)] = sl
                            dma("sp", "kt%d" % sl, KT[0:64, sl, :, :],
                                kv_all[r * 2048 + 2 * h * 64:r * 2048 + 2 * h * 64 + 128, i * GR:(i + 1) * GR].rearrange("(m d) t -> d m t", m=2),
                                [], [("KT", sl)])
                            dma("sp", "va%d" % sl, VA[:, sl, :, 0:128],
                                Vall[r][i * GR:(i + 1) * GR, h * 128:(h + 1) * 128].rearrange("(kb p) e -> p kb e", p=128),
                                [], [("VA", sl)], nonc=True)

                        def qk(n):
                            r, i, kb, _ = blocks[n]
                            if (r, i) not in kvslot:
                                load_kv(r, i)
                            sl = kvslot[(r, i)]
                            sbuf_ = n % 2
                            spo = blocks[n][3]
                            for m in range(2):
                                b = 2 * sbuf_ + m
                                mm(bank(b), KT[:, sl, m, kb * 128:(kb + 1) * 128], QT[:, hb, m, :], True, spo is None,
                                   [("KT", sl), ("QT", hb)], [("ps", b)], signal=(m == 1 and spo is None))
                                if spo is not None:
                                    mm(bank(b), IDB[:, :], BA[:, hb, spo:spo + 512], False, True,
                                       [("IDB",), ("BA", hb)], [("ps", b)], signal=(m == 1))

                        qk(0)
                        nblk = len(blocks)
                        for n in range(nblk):
                            r, i, kb, sp_off = blocks[n]
                            if n + 1 < nblk:
                                qk(n + 1)
                            sbuf_ = n % 2
                            sl = kvslot[(r, i)]
                            b0 = 2 * sbuf_
                            if sp_off is not None:
                                act(PT[:, sbuf_, :], PS[sbuf_][:, :], AF.Exp, [("ps", b0), ("ps", b0 + 1)], [("PT", sbuf_)])
                            else:
                                act(PT[:, sbuf_, :], PS[sbuf_][:, :], AF.Exp, [("ps", b0), ("ps", b0 + 1), ("CH",)], [("PT", sbuf_)],
                                    bias=CH[:, h:h + 1])
                            for m in range(2):
                                for qs in range(4):
                                    ap_, key_ = acc_ap(m, qs)
                                    mm(ap_, PT[:, sbuf_, m * 512 + qs * 128:m * 512 + qs * 128 + 128], VA[:, sl, kb, 0:129],
                                       n == 0 and (m * 4 + qs) % 3 == 0, n == nblk - 1, [("PT", sbuf_), ("VA", sl)], [key_],
                                       signal=(m == 1 and qs == 3))
                        cp("dve", OAC[:, 0:387], PS[2][:, 0:387], [("ps", 4)], [("OAC", 0)])
                        cp("dve", OAC[:, 387:774], PS[2][:, 512:899], [("ps", 5)], [("OAC", 1)])
                        cp("dve", OAC[:, 774:1032], PS[3][:, 0:258], [("ps", 6)], [("OAC", 2)])
                        for qs in range(4):
                            ob = qs % 2
                            i1, i2 = qs, 4 + qs
                            a1, k1 = OAC[:, i1 * 129:(i1 + 1) * 129], ("OAC", i1 // 3)
                            a2, k2 = OAC[:, i2 * 129:(i2 + 1) * 129], ("OAC", i2 // 3)
                            ko, ks = ("OS", ob), ("SM", ob)
                            sch.op("dve", lambda e, a1=a1, ob=ob: e.reciprocal(out=SM[:, ob, 0:1], in_=a1[:, 128:129]), [k1], [ks])
                            sch.op("dve", lambda e, a2=a2, ob=ob: e.reciprocal(out=SM[:, ob, 1:2], in_=a2[:, 128:129]), [k2], [ks])
                            ts_("dve", SM[:, ob, 2:3], SM[:, ob, 1:2], LS[:, 5:6], None, ALU.mult, None, [ks, ("LS",)], [ks])
                            ts_("dve", OS[:, ob, :], a1[:, 0:128], SM[:, ob, 0:1], None, ALU.mult, None, [k1, ks], [ko])
                            stt(OS[:, ob, :], a2[:, 0:128], SM[:, ob, 2:3], OS[:, ob, :], ALU.mult, ALU.add, [k2, ks, ko], [ko])
                            stt(OJ[:, :], OS[:, ob, :], 1.0, OS[:, ob, :], ALU.mult, ALU.mult, [ko], [("OJ",)], accum=SM[:, ob, 3:4])
                            ts_("dve", SM[:, ob, 4:5], SM[:, ob, 3:4], 1.0 / 128.0, LN_EPS, ALU.mult, ALU.add, [("OJ",), ks], [ks])
                            tt_("pool", SM[:, ob, 5:6], SM[:, ob, 4:5], NHALF[:, 0:1], ALU.pow, [ks, ("NHALF",)], [ks])
                            stt(XB[:, 1 + qs, h * 128:(h + 1) * 128], OS[:, ob, :], SM[:, ob, 5:6], GSUB[:, :], ALU.mult, ALU.mult,
                                [ko, ks, ("GSUB",)], [("XB", 1 + qs)])
                    if debug and j == 0:
                        dma("sp", "dbg", dbg_at[:, :, :], XB[:, 1:5, :], [("XB", 1 + t) for t in range(4)], [("dbg", 5)])
                    bptr[0] = 7
                    for c in range(8):
                        b = 7
                        for t in range(4):
                            mm(bank(b)[:, t * 128:(t + 1) * 128], XB[:, 1 + t, c * 128:(c + 1) * 128], IDB[:, :], True, True,
                               [("XB", 1 + t), ("IDB",)], [("ps", b)], signal=(t == 3))
                        cp("act" if c % 2 else "dve", HT[:, c, HALO:TT], bank(b), [("ps", b)], [("HT", c)])
                    bptr[0] = 0
                    s0 = ws.get(("wo", 0))
                    s1 = ws.get(("wo", 1))
                    for t in range(4):
                        pb = nb2()
                        for half, slot in ((0, s0), (1, s1)):
                            b = pb + half
                            for k in range(8):
                                mm(bank(b), HT[:, k, HALO + t * 128:HALO + (t + 1) * 128], WR[:, slot, k * 512:(k + 1) * 512], k == 0, k == 7,
                                   [("HT", k), ("WR", slot)], [("ps", b)])
                        if t == 3:
                            ws.release(2)
                        tl = epilogue(t, pb, X[:, 1 + t, :], ("X", 1 + t), 0, X[:, 1 + t, :], ("X", 1 + t), True)
                        if t > 0:
                            prev_tail()
                        prev_tail = tl
                    prev_tail()
                    if debug and j == 0:
                        dma("sp", "dbg", dbg_xq[:, :, :], X[:, 1:5, :], [("X", 1 + t) for t in range(4)], [("dbg", 6)])
                    mlp(ws, 3, "w1_1", "w2_1")
                    tails = []
                    for t in range(4):
                        tl = epilogue(t, 2 * t, X[:, 1 + t, :], ("X", 1 + t), 3, XO[:, t % 2, :], ("XO", t % 2), False)

                        def fin(t=t, tl=tl):
                            tl()
                            dma("sp", "xo%d" % (t % 2), out[j, t * 128:(t + 1) * 128, :], XO[:, t % 2, :],
                                [("XO", t % 2)], [("out", j, t)])
                        tails.append(fin)
                        if t > 0:
                            tails[t - 1]()
                    tails[3]()
                sch.wait_all("sp", ["xo0", "xo1"])
                sch.flush()
    return nc


def _t5_bucket(rel):
    n = np.maximum(rel, 0)
    nf = np.maximum(n, 1).astype(np.float32)
    large = 16 + (np.log(nf / np.float32(16)) / np.float32(math.log(128 / 16)) * np.float32(16)).astype(np.int32)
    large = np.minimum(large, 31)
    return np.where(n < 16, n, large)


def _bias_arrays(rel_bias, role):
    p = np.arange(128)[:, None]
    out = np.empty((NH, 128, 2304), np.float32)

    def fill(width, base_rel):
        j = np.arange(width)[None, :]
        rel = j - p + base_rel
        bk = _t5_bucket(rel)
        v = rel_bias[bk]
        v = np.where((rel >= 0)[:, :, None], v, np.float32(NEG))
        return np.transpose(v, (2, 0, 1))

    if role == 0:
        dA, dB, relC = 0, -512, 128
    else:
        dA, dB, relC = 0, 512, 1152
    out[:, :, 0:896] = fill(896, -384 + dA)
    out[:, :, 896:1792] = fill(896, -384 + dB)
    out[:, :, 1792:2304] = fill(512, relC)
    return out


_CACHE = {}


def _get(mode):
    if mode not in _CACHE:
        _CACHE[mode] = build(mode)
    return _CACHE[mode]


def _core_inputs(inputs, c):
    b, r = c // 2, c % 2
    x = inputs["x"]
    d = {}
    for name, shape in W_INPUTS:
        a = np.ascontiguousarray(inputs[name], dtype=np.float32).reshape(shape)
        d[name] = a
    d["cvec"] = np.ascontiguousarray(inputs["c"][b])
    d["ident"] = np.eye(128, dtype=np.float32)
    xs = np.zeros((2 * NG, TT, D), np.float32)
    hv = np.ones((128, 2 * NG), np.float32)
    for pp in range(2 * NG):
        i = pp // 2
        G = 2 * i + (r if pp % 2 == 0 else 1 - r)
        lo = G * GR - HALO
        if lo < 0:
            xs[pp, HALO:] = x[b, 0:GR]
            hv[:, pp] = 0.0
        else:
            xs[pp] = x[b, lo:lo + TT]
    d["xs"] = xs
    d["hv"] = hv
    d["biasarr"] = _bias_arrays(np.asarray(inputs["rel_bias"], np.float32), r)
    d["ch"] = np.ascontiguousarray(np.broadcast_to(np.asarray(inputs["rel_bias"], np.float32)[31][None, :], (128, NH)))
    return d


FUSED = True


def kernel(**inputs):
    inputs = {k: np.asarray(v) for k, v in inputs.items()}
    cores = list(range(8))
    per = [_core_inputs(inputs, c) for c in cores]
    if FUSED:
        nc = _get("F")
        keys = [t for t in per[0].keys()]
        res = run_bass_kernel_spmd(nc, per, core_ids=cores)
        outs = [r["out"] for r in res.results]
    else:
        ncA = _get("A")
        inA = [{k: v for k, v in p.items() if k not in ("biasarr", "ch")} for p in per]
        resA = run_bass_kernel_spmd(ncA, inA, core_ids=cores).results
        ncB = _get("B")
        inB = []
        for c in cores:
            p = {k: v for k, v in per[c].items() if k not in ("xs", "hv")}
            p["kv_all"] = resA[c]["kv_all"]
            p["x1_scr"] = resA[c]["x1_scr"]
            p["qt_scr"] = resA[c]["qt_scr"]
            inB.append(p)
        resB = run_bass_kernel_spmd(ncB, inB, core_ids=cores).results
        outs = [r["out"] for r in resB]
    y = np.empty((NB, SEQ, D), np.float32)
    for c in cores:
        b, r = c // 2, c % 2
        o = np.asarray(outs[c]).reshape(NG, GR, D)
        for i in range(NG):
            G = 2 * i + r
            y[b, G * GR:(G + 1) * GR] = o[i]
    return y
```

```python
import math
from contextlib import ExitStack

import numpy as np
import concourse.bass as bass
import concourse.mybir as mybir
from concourse.bass_utils import run_bass_kernel_spmd

F32 = mybir.dt.float32
BF16 = mybir.dt.bfloat16
AF = mybir.ActivationFunctionType
ALU = mybir.AluOpType

D = 1024
SEQ = 8192
NB = 4
DFF = 4096
GR = 512
HALO = 32
TT = GR + HALO
NG = 8
NH = 8
ALPHA = 4.0 ** 0.25
LN_EPS = 1e-5
LAMBDA_INIT = 0.8 - 0.6 * math.exp(-0.3 * 1)
CONVW = 31
NEG = -30000.0

ENGS = ("pe", "act", "dve", "pool", "sp")


class Sched:
    def __init__(self, nc, stack, strict_same=True):
        self.nc = nc
        self.stack = stack
        self.sems = {}
        self.cnt = {}
        self.prog = {e: [] for e in ENGS}
        self.seen = {e: {} for e in ENGS}
        self.last_w = {}
        self.readers = {}
        self.strict_same = strict_same
        for e in ENGS:
            self._mk(e)

    def _mk(self, name):
        self.sems[name] = self.stack.enter_context(self.nc.semaphore("s_" + name))
        self.cnt[name] = 0

    def _deps(self, reads, writes):
        d = {}
        for k in reads:
            w = self.last_w.get(k)
            if w:
                d[w[0]] = max(d.get(w[0], 0), w[1])
        for k in writes:
            w = self.last_w.get(k)
            if w:
                d[w[0]] = max(d.get(w[0], 0), w[1])
            for e2, c in self.readers.get(k, {}).items():
                d[e2] = max(d.get(e2, 0), c)
        return d

    def _wait(self, eng, d):
        for src, c in d.items():
            if src == eng and (eng == "pe" or not self.strict_same):
                continue
            if self.seen[eng].get(src, 0) >= c:
                continue
            self.seen[eng][src] = c
            if c > self.cnt[src]:
                print("SCHED WARNING: %s waits on future signal of %s (%d > %d)" % (eng, src, c, self.cnt[src]))
            unit = 1 if src in ENGS else 16
            self.prog[eng].append(("wait", src, c * unit))

    def _record(self, src, n, reads, writes):
        for k in reads:
            self.readers.setdefault(k, {})[src] = n
        for k in writes:
            self.last_w[k] = (src, n)
            self.readers[k] = {}

    def op(self, eng, fn, reads=(), writes=(), signal=True):
        self._wait(eng, self._deps(reads, writes))
        n = self.cnt[eng] + 1
        if signal:
            self.cnt[eng] = n
        self.prog[eng].append(("op", fn, eng if signal else None, 1))
        self._record(eng, n, reads, writes)

    def dma(self, issuer, lane, fn, reads=(), writes=()):
        if lane not in self.sems:
            self._mk(lane)
        d = self._deps(reads, writes)
        if self.cnt[lane] > 0:
            d[lane] = max(d.get(lane, 0), self.cnt[lane])
        self._wait(issuer, d)
        self.cnt[lane] += 1
        self.prog[issuer].append(("op", fn, lane, 16))
        self._record(lane, self.cnt[lane], reads, writes)

    def wait_all(self, eng, lanes):
        d = {l: self.cnt[l] for l in lanes if self.cnt.get(l, 0) > 0}
        self._wait(eng, d)

    def flush(self):
        nc = self.nc
        prog = self.prog
        sems = self.sems
        self.prog = {e: [] for e in ENGS}

        def run(e, lst):
            for it in lst:
                if it[0] == "wait":
                    e.wait_ge(sems[it[1]], it[2])
                else:
                    ins = it[1](e)
                    if it[2] is not None:
                        ins.then_inc(sems[it[2]], it[3])

        with nc.Block() as block:
            @block.tensor
            def _(e):
                run(e, prog["pe"])

            @block.scalar
            def _(e):
                run(e, prog["act"])

            @block.vector
            def _(e):
                run(e, prog["dve"])

            @block.gpsimd
            def _(e):
                run(e, prog["pool"])

            @block.sync
            def _(e):
                run(e, prog["sp"])


W_INPUTS = [
    ("conv_mod_w", [D, 3 * D]), ("conv_mod_b", [3 * D]),
    ("conv_pw1_w", [D, 2 * D]), ("conv_pw1_b", [2 * D]),
    ("conv_dw_w", [CONVW, D]), ("conv_dw_b", [D]),
    ("conv_norm_g", [D]), ("conv_norm_b", [D]),
    ("conv_pw2_w", [D, D]), ("conv_pw2_b", [D]),
    ("attn_mod_w", [D, 3 * D]), ("attn_mod_b", [3 * D]),
    ("attn_qkv_w", [D, 3 * D]),
    ("attn_lam_q1", [64]), ("attn_lam_k1", [64]), ("attn_lam_q2", [64]), ("attn_lam_k2", [64]),
    ("attn_subln_g", [128]),
    ("attn_out_w", [D, D]),
    ("mlp_mod_w", [2, D, 3 * D]), ("mlp_mod_b", [2, 3 * D]),
    ("mlp_w1", [2, D, DFF]), ("mlp_w2", [2, DFF, D]),
    ("post_mix_g", [2, D]), ("post_mix_b", [2, D]),
    ("post_mlp_g", [2, D]), ("post_mlp_b", [2, D]),
]


def build(mode, ng=NG, strict_same=True, debug=False):
    doA = mode in ("A", "F")
    doB = mode in ("B", "F")
    nc = bass.Bass("TRN2", target_bir_lowering=False)
    I = {}

    def din(name, shape, dt=F32):
        I[name] = nc.dram_tensor(name, list(shape), dt, kind="ExternalInput").ap()
        return I[name]

    for name, shape in W_INPUTS:
        din(name, shape)
    din("cvec", [D])
    din("ident", [128, 128])
    if doA:
        din("xs", [2 * ng, TT, D])
        din("hv", [128, 2 * NG])
    if doB:
        din("biasarr", [NH, 128, 2304])
        din("ch", [128, NH])

    inter = "Internal" if mode == "F" else None
    x1_scr = nc.dram_tensor("x1_scr", [ng, GR, D], F32,
                            kind=inter or ("ExternalOutput" if mode == "A" else "ExternalInput")).ap()
    qt_scr = nc.dram_tensor("qt_scr", [16 * 64, NG * GR], BF16,
                            kind=inter or ("ExternalOutput" if mode == "A" else "ExternalInput")).ap()
    kv_all = nc.dram_tensor("kv_all", [4096, NG * GR], BF16,
                            kind=inter or ("ExternalOutput" if mode == "A" else "ExternalInput")).ap()
    if doB:
        out = nc.dram_tensor("out", [ng, GR, D], F32, kind="ExternalOutput").ap()
    NCH = 46
    if debug and doB:
        dbg_at = nc.dram_tensor("dbg_at", [128, 4, D], BF16, kind="ExternalOutput").ap()
        dbg_xq = nc.dram_tensor("dbg_xq", [128, 4, D], F32, kind="ExternalOutput").ap()
    if debug and doA:
        dbg_ht = nc.dram_tensor("dbg_ht", [128, 8, TT], BF16, kind="ExternalOutput").ap()
        dbg_acc = nc.dram_tensor("dbg_acc", [128, 8, GR], F32, kind="ExternalOutput").ap()
        dbg_vt = nc.dram_tensor("dbg_vt", [128, 8, GR], BF16, kind="ExternalOutput").ap()
        dbg_x0 = nc.dram_tensor("dbg_x0", [128, 4, D], F32, kind="ExternalOutput").ap()
        dbg_cv = nc.dram_tensor("dbg_cv", [128, 64], F32, kind="ExternalOutput").ap()
    wscr = nc.dram_tensor("wscr", [NCH, 128, 4096], BF16, kind="Internal").ap()
    modscr = nc.dram_tensor("modscr", [4, 3 * D], F32, kind="Internal").ap()

    st = ExitStack()
    with st:
        sch = Sched(nc, st, strict_same=strict_same)

        def sb(name, shape, dt, stack=st):
            return stack.enter_context(nc.sbuf_tensor(name, list(shape), dt))

        NWR = 3
        WR = sb("WR", [128, NWR, 4096], BF16)
        X = sb("X", [128, 5, D], F32)
        XB = sb("XB", [128, 5, D], BF16)
        HT = sb("HT", [128, 8, TT], BF16)
        HID = sb("HID", [128, 32, GR], BF16)
        Z = sb("Z", [128, 2, D], F32)
        ZN = sb("ZN", [128, 2, D], F32)
        XO = sb("XO", [128, 2, D], F32)
        BC = sb("BC", [128, 6, D], F32)
        RT = sb("RT", [128, 2, GR], F32)
        CV = sb("CV", [128, 8 * 8], F32)
        IDB = sb("IDB", [128, 128], BF16)
        STt = sb("STt", [128, 2, 12], F32)
        MV = sb("MV", [128, 2, 4], F32)
        NHALF = sb("NHALF", [128, GR], F32)
        PBR = sb("PBR", [1, 2, D], BF16)
        if not doA:
            ONEB = sb("ONEB", [1, 128], BF16)
        PS = [st.enter_context(nc.psum_tensor("PS%d" % i, [128, 1024], F32)) for i in range(4)]

        def bank(b):
            return PS[b // 2][:, (b % 2) * 512:(b % 2) * 512 + 512]

        bptr = [0]
        nbmod = [8]

        def nb():
            b = bptr[0]
            bptr[0] = (b + 1) % nbmod[0]
            return b

        def nb2():
            if bptr[0] % 2:
                bptr[0] = (bptr[0] + 1) % 8
            b = bptr[0]
            bptr[0] = (b + 2) % 8
            return b

        def mm(out_ap, lhsT, rhs, start, stop, reads, writes, signal=None):
            sig = stop if signal is None else signal
            sch.op("pe", lambda e: e.matmul(out_ap, lhsT=lhsT, rhs=rhs, start=start, stop=stop),
                   reads, writes, signal=sig)

        def act(out_ap, in_ap, func, reads, writes, bias=None, scale=None):
            kw = {}
            if bias is not None:
                kw["bias"] = bias
            if scale is not None:
                kw["scale"] = scale
            sch.op("act", lambda e: e.activation(out=out_ap, in_=in_ap, func=func, **kw), reads, writes)

        def tt_(eng, out_ap, a, b, op, reads, writes):
            sch.op(eng, lambda e: e.tensor_tensor(out=out_ap, in0=a, in1=b, op=op), reads, writes)

        def ts_(eng, out_ap, a, s1, s2, op0, op1, reads, writes):
            if s2 is None:
                sch.op(eng, lambda e: e.tensor_scalar(out=out_ap, in0=a, scalar1=s1, scalar2=None, op0=op0),
                       reads, writes)
            else:
                sch.op(eng, lambda e: e.tensor_scalar(out=out_ap, in0=a, scalar1=s1, scalar2=s2, op0=op0, op1=op1),
                       reads, writes)

        def stt(out_ap, a, s, b, op0, op1, reads, writes, accum=None):
            if accum is None:
                sch.op("dve", lambda e: e.scalar_tensor_tensor(out=out_ap, in0=a, scalar=s, in1=b, op0=op0, op1=op1),
                       reads, writes)
            else:
                sch.op("dve", lambda e: e.scalar_tensor_tensor(out=out_ap, in0=a, scalar=s, in1=b, op0=op0, op1=op1,
                                                               accum_out=accum), reads, writes)

        def cp(eng, out_ap, in_ap, reads, writes):
            if eng == "act":
                sch.op("act", lambda e: e.copy(out=out_ap, in_=in_ap), reads, writes)
            else:
                sch.op(eng, lambda e: e.tensor_copy(out=out_ap, in_=in_ap), reads, writes)

        def dma(issuer, lane, out_ap, in_ap, reads, writes, nonc=False):
            if nonc:
                sch.dma(issuer, lane, lambda e: e.dma_start(out=out_ap, in_=in_ap, allow_slow_non_contiguous=True),
                        reads, writes)
            else:
                sch.dma(issuer, lane, lambda e: e.dma_start(out=out_ap, in_=in_ap), reads, writes)

        chunk_id = {}
        cvl = [0]

        cvt_jobs = []

        def cvt(out_ap, in_ap, ci):
            cvt_jobs.append((out_ap, in_ap, ci))

        def emit_cvts(nmax=None, nodep=False):
            k = 0
            while cvt_jobs and (nmax is None or k < nmax):
                (out_ap, in_ap, ci) = cvt_jobs.pop(0)
                lane = "cv%d" % (cvl[0] % 4)
                first = (k == 0) and not nodep
                cvl[0] += 1
                k += 1
                dma("pool", lane, out_ap, in_ap, [("modscr", s_, n_) for s_ in range(4) for n_ in range(6)] if first else [], [("wscr", ci)])

        def add_kmajor(name, W, ncols):
            Wv = W.rearrange("(k p) n -> p k n", p=128)
            for n in range(ncols // 512):
                ci = len(chunk_id)
                chunk_id[(name, n)] = ci
                cvt(wscr[ci].rearrange("p (k n) -> p k n", k=8), Wv[:, :, n * 512:(n + 1) * 512], ci)

        def add_w2(name, W):
            Wv = W.rearrange("(f p) n -> p f n", p=128)
            for c in range(8):
                ci = len(chunk_id)
                chunk_id[(name, c)] = ci
                cvt(wscr[ci].rearrange("p (f n) -> p f n", f=4), Wv[:, 4 * c:4 * c + 4, :], ci)

        if doA:
            Wv = I["conv_pw1_w"].rearrange("(k p) n -> p k n", p=128)
            for j in range(4):
                ci = len(chunk_id)
                chunk_id[("pw1", j)] = ci
                dst = wscr[ci].rearrange("p (k n) -> p k n", k=8)
                cvt(dst[:, :, 0:256], Wv[:, :, 256 * j:256 * j + 256], ci)
                cvt(dst[:, :, 256:512], Wv[:, :, 1024 + 256 * j:1024 + 256 * j + 256], ci)
            add_kmajor("pw2", I["conv_pw2_w"], D)
            add_kmajor("w1_0", I["mlp_w1"][0], DFF)
            add_w2("w2_0", I["mlp_w2"][0])
            add_kmajor("qkv", I["attn_qkv_w"], 3 * D)
        if doB:
            add_kmajor("wo", I["attn_out_w"], D)
            add_kmajor("w1_1", I["mlp_w1"][1], DFF)
            add_w2("w2_1", I["mlp_w2"][1])

        GATED = {"pw2": (0, "k"), "w2_0": (3, "f"), "wo": (0, "k"), "w2_1": (3, "f")}
        scaled_chunks = set()

        class WStream:
            def __init__(self, seq):
                self.seq = seq
                self.issued = 0
                self.pos = 0
                self.released = 0

            def _issue(self):
                k = self.issued
                slot = k % NWR
                ci = chunk_id[self.seq[k]]
                assert ("wscr", ci) in sch.last_w, ("weight chunk loaded before its conversion was emitted", self.seq[k])
                dma("sp", "wr%d" % slot, WR[:, slot, :], wscr[ci], [("wscr", ci)], [("WR", slot)])
                self.issued += 1

            def prefetch(self):
                while self.issued < len(self.seq) and self.issued - NWR < self.released:
                    self._issue()

            def get(self, name):
                assert self.seq[self.pos] == name, (self.seq[self.pos], name)
                self.prefetch()
                assert self.issued > self.pos
                slot = self.pos % NWR
                self.pos += 1
                if name[0] in GATED and name not in scaled_chunks:
                    scaled_chunks.add(name)
                    gslot, kind = GATED[name[0]]
                    ci = chunk_id[name]
                    if kind == "k":
                        n0 = name[1] * 512
                        for k in range(8):
                            tt_("dve", WR[:, slot, k * 512:(k + 1) * 512], WR[:, slot, k * 512:(k + 1) * 512],
                                BC[:, gslot, n0:n0 + 512], ALU.mult, [("WR", slot), ("BC", gslot)], [("WR", slot)])
                    else:
                        for f in range(4):
                            tt_("dve", WR[:, slot, f * 1024:(f + 1) * 1024], WR[:, slot, f * 1024:(f + 1) * 1024],
                                BC[:, gslot, :], ALU.mult, [("WR", slot), ("BC", gslot)], [("WR", slot)])
                    dma("sp", "wsb%d" % slot, wscr[ci], WR[:, slot, :], [("WR", slot)], [("wscr", ci)])
                return slot

            def release(self, n=1):
                self.released += n
                self.prefetch()

        if doA:
            CA = sb("CA", [128, 8 * 5 + CONVW * 8], F32)
            HVt = sb("HVt", [128, 2 * NG], F32)
            ONES = sb("ONES", [128, 128], F32)
            ONEB = sb("ONEB", [1, 128], BF16)
            dma("sp", "pl6", HVt[:, :], I["hv"], [], [("HV",)])
            sch.op("dve", lambda e: e.memset(ONES[:, :], 1.0 / D), [], [("ONES",)])
            sch.op("dve", lambda e: e.memset(ONEB[:, :], 1.0), [], [("ONEB",)])
        pst = ExitStack()
        with pst:
            MW = sb("MW", [128, 2, 8, 512], F32, pst)
            SREP = sb("SREP", [128, 8, 128], F32, pst)
            CCOL = sb("CCOL", [128, 8], F32, pst)
            SCOL = sb("SCOL", [128, 8], F32, pst)
            MB = sb("MB", [128, 2, 512], F32, pst)
            MROW = sb("MROW", [128, 2, 512], F32, pst)
            mwc = [0]
            TMPC4 = sb("TMPC", [128, 4, 48], F32, pst)
            IDF = sb("IDF", [64, 32], F32, pst)
            E0 = sb("E0", [128, 1], F32, pst)
            RW = sb("RW", [64, D], F32, pst)
            ROWS = RW[0:16, :]
            TAPR = RW[32:64, :]
            COLV = sb("COLV", [128, 8, 11], F32, pst)
            MCOL = sb("MCOL", [128, 64], F32, pst)
            nbmod[0] = 5
            dma("sp", "pl7", IDF[0:32, :], I["ident"][0:32, 0:32], [], [("IDF",)])
            dma("sp", "pl7", IDF[32:64, :], I["ident"][0:32, 0:32], [], [("IDF",)])
            dma("sp", "pl7", E0[:, :], I["ident"][:, 0:1], [], [("E0",)], nonc=True)
            rowvecs = [I["post_mix_g"][0], I["post_mix_b"][0], I["post_mlp_g"][0], I["post_mlp_b"][0],
                       I["post_mix_g"][1], I["post_mix_b"][1]]
            if doA:
                rowvecs += [I["conv_pw1_b"][0:D], I["conv_pw1_b"][D:2 * D], I["conv_dw_b"], I["conv_norm_g"], I["conv_norm_b"]]
            for vi, vec in enumerate(rowvecs):
                dma("sp", "pl8", ROWS[vi:vi + 1, :], vec.rearrange("(o n) -> o n", o=1), [], [("ROWS",)])
            nv = len(rowvecs)
            for c in range(8):
                mm(bank(6)[:, c * 11:c * 11 + nv], ROWS[0:nv, c * 128:(c + 1) * 128], IDF[0:nv, 0:nv], True, True,
                   [("ROWS",), ("IDF",)], [("ps", 6)], signal=(c == 7))
            cp("dve", COLV[:, :, :], bank(6)[:, 0:88].rearrange("p (c v) -> p c v", v=11), [("ps", 6)], [("COLV",)])
            if doA:
                dma("sp", "pl8", TAPR[0:CONVW, :], I["conv_dw_w"], [], [("TAPR",)])
                for c in range(8):
                    mm(bank(5)[:, c * CONVW:(c + 1) * CONVW], TAPR[0:CONVW, c * 128:(c + 1) * 128], IDF[32:32 + CONVW, 0:CONVW], True, True,
                       [("TAPR",), ("IDF",)], [("ps", 5)], signal=(c == 7))
                cp("dve", CA[:, 40:40 + CONVW * 8].rearrange("p (j c) -> p j c", c=8),
                   bank(5)[:, 0:CONVW * 8].rearrange("p (c j) -> p j c", j=CONVW), [("ps", 5)], [("CAt",)])
                for k5 in range(5):
                    cp("dve", CA[:, k5 * 8:(k5 + 1) * 8], COLV[:, :, 6 + k5], [("COLV",)], [("CA", k5)])

            dma("pool", "pl0", IDB[:, :], I["ident"], [], [("IDB",)])
            sch.op("dve", lambda e: e.memset(NHALF[:, :], -0.5), [], [("NHALF",)])
            dma("sp", "pl1", CCOL[:, :], I["cvec"].rearrange("(k p) -> p k", p=128), [], [("CCOL",)], nonc=True)
            act(SCOL[:, :], CCOL[:, :], AF.Sigmoid, [("CCOL",)], [("SCOL",)])
            tt_("dve", SCOL[:, :], SCOL[:, :], CCOL[:, :], ALU.mult, [("SCOL",), ("CCOL",)], [("SCOL",)])
            for k in range(8):
                cp("dve", SREP[:, k, :], SCOL[:, k:k + 1].to_broadcast([128, 128]), [("SCOL",)], [("SREP",)])
            if doA:
                emit_cvts(8, nodep=True)
            mods = []
            if doA:
                mods += [(0, I["conv_mod_w"], I["conv_mod_b"]), (1, I["mlp_mod_w"][0], I["mlp_mod_b"][0])]
            mods += [(2, I["attn_mod_w"], I["attn_mod_b"])]
            if doB:
                mods += [(3, I["mlp_mod_w"][1], I["mlp_mod_b"][1])]
            for (s, mw, mb) in mods:
                mwv = mw.rearrange("(k p) n -> p k n", p=128)
                for n in range(6):
                    q = mwc[0] % 2
                    mwc[0] += 1
                    dma("sp", "pl2%d" % q, MW[:, q, :, :], mwv[:, :, n * 512:(n + 1) * 512], [], [("MW", q)])
                    dma("sp", "pl3%d" % q, MB[:, q, :], mb[n * 512:(n + 1) * 512].partition_broadcast(128), [], [("MB", q)])
                    b = nb()
                    for k in range(8):
                        mm(bank(b), SREP[:, k, :], MW[:, q, k, :], k == 0, k == 7,
                           [("SREP",), ("MW", q)], [("ps", b)])
                    tt_("dve", MROW[:, q, :], bank(b), MB[:, q, :], ALU.add, [("ps", b), ("MB", q)], [("MROW", q)])
                    if n < 4:
                        for fc in range(4):
                            col = s * 16 + n * 4 + fc
                            mm(bank(7)[:, col:col + 1], MROW[:, q, fc * 128:(fc + 1) * 128], E0[:, 0:1], True, True,
                               [("MROW", q), ("E0",)], [("ps", 7)], signal=(fc == 3))
                    dma("act", "pl4%d" % q, modscr[s:s + 1, n * 512:(n + 1) * 512], MROW[0:1, q, :], [("MROW", q)], [("modscr", s, n)])

            if mode != "F":
                emit_cvts(None, nodep=True)
            cp("dve", MCOL[:, :], bank(7)[:, 0:64], [("ps", 7)], [("MCOL",)])
            lnv = {1: (0, 1), 2: (2, 3), 3: (4, 5)}
            for (s, _, _) in mods:
                G2 = CV[:, s * 16:s * 16 + 8]
                B2 = CV[:, s * 16 + 8:s * 16 + 16]
                TMPC = TMPC4[:, s, :]
                SH = MCOL[:, s * 16:s * 16 + 8]
                ts_("dve", TMPC[:, 8:16], MCOL[:, s * 16 + 8:s * 16 + 16], 1.0, None, ALU.add, None, [("MCOL",)], [("TMPC", s, 1)])
                if s == 0:
                    cp("dve", G2, TMPC[:, 8:16], [("TMPC", s, 1)], [("CV", s)])
                    cp("dve", B2, SH, [("MCOL",)], [("CV", s)])
                else:
                    gi, bi = lnv[s]
                    tt_("dve", G2, COLV[:, :, gi], TMPC[:, 8:16], ALU.mult, [("COLV",), ("TMPC", s, 1)], [("CV", s)])
                    tt_("dve", TMPC[:, 32:40], COLV[:, :, bi], TMPC[:, 8:16], ALU.mult, [("COLV",), ("TMPC", s, 1)], [("TMPC", s, 4)])
                    tt_("dve", B2, TMPC[:, 32:40], SH, ALU.add, [("TMPC", s, 4), ("MCOL",)], [("CV", s)])
            nbmod[0] = 8
            bptr[0] = 0
            sch.flush()

        def load_bc(slot, vec, lane="bc"):
            dma("sp", "bc%d" % slot, BC[:, slot, :], vec.partition_broadcast(128),
                [("modscr", s_, n_) for s_ in range(4) for n_ in range(6)], [("BC", slot)])

        def to_hT(s, halo):
            for c in range(8):
                b = nb()
                for t in range(4):
                    mm(bank(b)[:, t * 128:(t + 1) * 128], XB[:, 1 + t, c * 128:(c + 1) * 128], IDB[:, :], True, True,
                       [("XB", 1 + t), ("IDB",)], [("ps", b)], signal=(t == 3))
                act(HT[:, c, HALO:TT], bank(b), AF.Identity, [("ps", b), ("CV", s)], [("HT", c)],
                    bias=CV[:, s * 16 + 8 + c:s * 16 + 9 + c], scale=CV[:, s * 16 + c:s * 16 + c + 1])
            if halo:
                b = nb()
                for c in range(8):
                    mm(bank(b)[:, c * 32:(c + 1) * 32], XB[0:32, 0, c * 128:(c + 1) * 128], IDB[0:32, 0:32], True, True,
                       [("XB", 0), ("IDB",)], [("ps", b)], signal=(c == 7))
                for c in range(8):
                    act(HT[:, c, 0:HALO], bank(b)[:, c * 32:(c + 1) * 32], AF.Identity, [("ps", b), ("CV", s)], [("HT", c)],
                        bias=CV[:, s * 16 + 8 + c:s * 16 + 9 + c], scale=CV[:, s * 16 + c:s * 16 + c + 1])

        def epilogue(t, pb, res_ap, res_key, ag, st_ap, st_key, znb, final=None):
            zb = t % 2
            P = PS[pb // 2][:, :]
            kz, kn = ("Z", zb), ("ZN", zb)
            pk = [("ps", pb), ("ps", pb + 1)]
            if ag is None:
                stt(Z[:, zb, :], res_ap, ALPHA, P, ALU.mult, ALU.add, [res_key] + pk, [kz])
            else:
                tt_("dve", Z[:, zb, :], res_ap, BC[:, ag, :], ALU.mult, [res_key, ("BC", ag)], [kz])
                tt_("dve", Z[:, zb, :], Z[:, zb, :], P, ALU.add, [kz] + pk, [kz])
            for hh in range(2):
                sch.op("dve", lambda e, hh=hh: e.bn_stats(out=STt[:, zb, hh * 6:hh * 6 + 6], in_=Z[:, zb, hh * 512:(hh + 1) * 512]),
                       [kz], [("ST", zb)])
            sch.op("dve", lambda e: e.bn_aggr(out=MV[:, zb, 0:2], in_=STt[:, zb, :]), [("ST", zb)], [("MV", zb)])
            ts_("dve", MV[:, zb, 1:2], MV[:, zb, 1:2], LN_EPS, None, ALU.add, None, [("MV", zb)], [("MV", zb)])
            tt_("pool", MV[:, zb, 2:3], MV[:, zb, 1:2], NHALF[:, 0:1], ALU.pow, [("MV", zb), ("NHALF",)], [("MV", zb)])
            stt(MV[:, zb, 3:4], MV[:, zb, 0:1], -1.0, MV[:, zb, 2:3], ALU.mult, ALU.mult, [("MV", zb)], [("MV", zb)])
            if znb:
                act(XB[:, 1 + t, :], Z[:, zb, :], AF.Identity, [kz, ("MV", zb)], [("XB", 1 + t)],
                    bias=MV[:, zb, 3:4], scale=MV[:, zb, 2:3])
            if st_ap is not None:
                act(st_ap, Z[:, zb, :], AF.Identity, [kz, ("MV", zb)], [st_key],
                    bias=MV[:, zb, 3:4], scale=MV[:, zb, 2:3])
            if final is None:
                return lambda: None
            gs, bs, out_ap, out_key = final
            act(ZN[:, zb, :], Z[:, zb, :], AF.Identity, [kz, ("MV", zb)], [kn],
                bias=MV[:, zb, 3:4], scale=MV[:, zb, 2:3])

            def tail():
                tt_("pool", out_ap, ZN[:, zb, :], BC[:, gs, :], ALU.mult, [kn, ("BC", gs)], [out_key])
                tt_("dve", out_ap, out_ap, BC[:, bs, :], ALU.add, [out_key, ("BC", bs)], [out_key])
            return tail

        def mlp(ws, s, w1n, w2n, brow):
            to_hT(s, False)
            for f in range(8):
                slot = ws.get((w1n, f))
                for fl in range(4):
                    fc = 4 * f + fl
                    b = nb()
                    for k in range(8):
                        mm(bank(b), WR[:, slot, k * 512 + fl * 128:k * 512 + fl * 128 + 128], HT[:, k, HALO:TT], k == 0, k == 7,
                           [("WR", slot), ("HT", k)], [("ps", b)])
                    rb = fc % 2
                    act(RT[:, rb, :], bank(b), AF.Relu, [("ps", b)], [("RT", rb)])
                    tt_("pool", HID[:, fc, :], RT[:, rb, :], RT[:, rb, :], ALU.mult, [("RT", rb)], [("HID", fc)])
                ws.release()
            for c in range(8):
                slot = ws.get((w2n, c))
                for t in range(4):
                    for half in range(2):
                        b = 2 * t + half
                        if c == 0:
                            mm(bank(b), ONEB[0:1, :], PBR[0:1, brow, half * 512:(half + 1) * 512], True, False,
                               [("ONEB",), ("PBR", brow)], [("ps", b)])
                        for fl in range(4):
                            mm(bank(b), HID[:, 4 * c + fl, t * 128:(t + 1) * 128],
                               WR[:, slot, fl * 1024 + half * 512:fl * 1024 + half * 512 + 512],
                               False, c == 7 and fl == 3,
                               [("HID", 4 * c + fl), ("WR", slot)], [("ps", b)], signal=(fl == 3))
                ws.release()
            bptr[0] = 0

        if doA:
            ast = ExitStack()
            with ast:
                U = sb("U", [128, 2, TT], BF16, ast)
                NDG = 8
                DG = sb("DG", [128, NDG, 128], BF16, ast)
                dgc = [0]
                SG = sb("SG", [128, 2, TT], F32, ast)
                ACC = sb("ACC", [128, 8, GR], F32, ast)
                SQ = sb("SQ", [128, 2, GR], F32, ast)
                UN = sb("UN", [128, 2, GR], F32, ast)
                VT = sb("VT", [128, 8, GR], BF16, ast)
                STB = sb("STB", [128, 4, GR], F32, ast)
                QS = sb("QS", [128, 2, GR], BF16, ast)
                VS = sb("VS", [128, 1, D], BF16, ast)
                load_bc(0, modscr[0, 2 * D:3 * D])
                load_bc(1, I["post_mix_g"][0])
                load_bc(3, modscr[1, 2 * D:3 * D])
                ts_("dve", BC[:, 1, :], BC[:, 1, :], ALPHA, None, ALU.mult, None, [("BC", 1)], [("BC", 1)])
                dma("sp", "pr0", Z[0:1, 0, :], I["conv_pw2_b"].rearrange("(o n) -> o n", o=1), [], [("Z", 0)])
                dma("sp", "pr1", Z[0:1, 1, :], I["post_mix_b"][0].rearrange("(o n) -> o n", o=1), [], [("Z", 1)])
                tt_("dve", PBR[0:1, 0, :], Z[0:1, 0, :], BC[0:1, 0, :], ALU.mult, [("Z", 0), ("BC", 0)], [("PBR", 0)])
                ts_("dve", PBR[0:1, 1, :], Z[0:1, 1, :], ALPHA, None, ALU.mult, None, [("Z", 1)], [("PBR", 1)])

                seqA = []
                for pp in range(2 * ng):
                    seqA += [("pw1", j) for j in range(4)] + [("pw2", n) for n in range(2)]
                    seqA += [("w1_0", f) for f in range(8)] + [("w2_0", c) for c in range(8)]
                    seqA += [("qkv", n) for n in range(0 if pp % 2 == 0 else 2, 6)]
                ws = WStream(seqA)

                def load_x(g):
                    dma("sp", "xl0", X[0:HALO, 0, :], I["xs"][g, 0:HALO, :], [], [("X", 0)])
                    dma("sp", "xl1", X[:, 1:5, :], I["xs"][g, HALO:TT, :].rearrange("(t p) d -> p t d", p=128),
                        [], [("X", 1), ("X", 2), ("X", 3), ("X", 4)])

                load_x(0)
                ws.prefetch()
                for pp in range(2 * ng):
                    g = pp // 2
                    own = (pp % 2 == 0)
                    kvl = 0 if own else 1
                    if pp == 0:
                        emit_cvts(4, nodep=True)
                    cp("dve", XB[0:HALO, 0, :], X[0:HALO, 0, :], [("X", 0)], [("XB", 0)])
                    for t in range(4):
                        cp("dve" if t % 2 else "act", XB[:, 1 + t, :], X[:, 1 + t, :], [("X", 1 + t)], [("XB", 1 + t)])
                    to_hT(0, True)
                    if pp == 0:
                        emit_cvts(4, nodep=True)
                    if debug and pp == 0:
                        dma("sp", "dbg", dbg_ht[:, :, :], HT[:, :, :], [("HT", c) for c in range(8)], [("dbg", 0)])
                        dma("sp", "dbg", dbg_cv[:, :], CV[:, :], [("CV", 0)], [("dbg", 4)])
                    nbmod[0] = 6
                    bptr[0] = 0
                    pw1_slot = {}
                    pw1_banks = {}

                    def pw1_glu(i):
                        j, s2 = i // 2, i % 2
                        if s2 == 0:
                            pw1_slot[j] = ws.get(("pw1", j))
                        slot = pw1_slot[j]
                        ub = i % 2
                        ba, bg, bh = nb(), nb(), nb()
                        for (bk, col0) in ((ba, 128 * s2), (bg, 256 + 128 * s2)):
                            for k in range(8):
                                mm(bank(bk), WR[:, slot, k * 512 + col0:k * 512 + col0 + 128], HT[:, k, HALO:TT], k == 0, k == 7,
                                   [("WR", slot), ("HT", k)], [("ps", bk)])
                        for hi, col0 in enumerate((128 * s2, 256 + 128 * s2)):
                            for k in range(8):
                                mm(bank(bh)[:, hi * 32:hi * 32 + 32], WR[:, slot, k * 512 + col0:k * 512 + col0 + 128], HT[:, k, 0:HALO],
                                   k == 0, k == 7, [("WR", slot), ("HT", k)], [("ps", bh)], signal=(k == 7 and hi == 1))
                        if s2 == 1:
                            ws.release()
                        pw1_banks[i] = (ba, bg, bh)

                    def glu_elem(i):
                        ub = i % 2
                        ba, bg, bh = pw1_banks[i]
                        act(SG[:, ub, HALO:TT], bank(bg), AF.Sigmoid, [("ps", bg), ("CA", 0), ("CA", 1), ("CA", 2), ("CA", 3), ("CA", 4), ("CAt",)], [("SG", ub)], bias=CA[:, 8 + i:9 + i])
                        act(SG[:, ub, 0:HALO], bank(bh)[:, 32:64], AF.Sigmoid, [("ps", bh), ("CA", 0), ("CA", 1), ("CA", 2), ("CA", 3), ("CA", 4), ("CAt",)], [("SG", ub)], bias=CA[:, 8 + i:9 + i])
                        stt(U[:, ub, HALO:TT], bank(ba), CA[:, i:i + 1], SG[:, ub, HALO:TT], ALU.add, ALU.mult,
                            [("ps", ba), ("SG", ub), ("CA", 0), ("CA", 1), ("CA", 2), ("CA", 3), ("CA", 4), ("CAt",)], [("U", ub)])
                        stt(U[:, ub, 0:HALO], bank(bh)[:, 0:32], CA[:, i:i + 1], SG[:, ub, 0:HALO], ALU.add, ALU.mult,
                            [("ps", bh), ("SG", ub), ("CA", 0), ("CA", 1), ("CA", 2), ("CA", 3), ("CA", 4), ("CAt",)], [("U", ub)])
                        ts_("dve", U[:, ub, 0:HALO], U[:, ub, 0:HALO], HVt[:, pp:pp + 1], None, ALU.mult, None,
                            [("U", ub), ("HV",)], [("U", ub)])

                    def dwconv(i):
                        ub = i % 2
                        bc_ = nb()
                        for jt in range(CONVW):
                            dgs = dgc[0] % NDG
                            dgc[0] += 1
                            if jt % 2 == 0:
                                ts_("dve", DG[:, dgs, :], IDB[:, :], CA[:, 40 + jt * 8 + i:41 + jt * 8 + i], None, ALU.mult, None,
                                    [("IDB",), ("CA", 0), ("CA", 1), ("CA", 2), ("CA", 3), ("CA", 4), ("CAt",)], [("DG", dgs)])
                            else:
                                act(DG[:, dgs, :], IDB[:, :], AF.Copy, [("IDB",), ("CA", 0), ("CA", 1), ("CA", 2), ("CA", 3), ("CA", 4), ("CAt",)], [("DG", dgs)],
                                    scale=CA[:, 40 + jt * 8 + i:41 + jt * 8 + i])
                            mm(bank(bc_), DG[:, dgs, :], U[:, ub, 2 + jt:2 + jt + GR], jt == 0, jt == CONVW - 1,
                               [("DG", dgs), ("U", ub)], [("ps", bc_)], signal=True)
                        act(ACC[:, i, :], bank(bc_), AF.Identity, [("ps", bc_), ("CA", 0), ("CA", 1), ("CA", 2), ("CA", 3), ("CA", 4), ("CAt",)], [("ACC", i)], bias=CA[:, 16 + i:17 + i])
                        act(SQ[:, ub, :], ACC[:, i, :], AF.Square, [("ACC", i)], [("SQ", ub)])
                        mm(bank(6), ONES[:, :], ACC[:, i, :], i == 0, i == 7, [("ONES",), ("ACC", i)], [("ps", 6)], signal=True)
                        mm(bank(7), ONES[:, :], SQ[:, ub, :], i == 0, i == 7, [("ONES",), ("SQ", ub)], [("ps", 7)], signal=True)

                    pw1_glu(0)
                    glu_elem(0)
                    for i in range(8):
                        if i + 1 < 8:
                            pw1_glu(i + 1)
                        dwconv(i)
                        if pp == 0 and i in (1, 4):
                            emit_cvts(4, nodep=True)
                        if i + 1 < 8:
                            glu_elem(i + 1)
                    bm, bq = 6, 7
                    nbmod[0] = 8
                    bptr[0] = 0
                    cp("dve", STB[:, 0, :], bank(bm), [("ps", bm)], [("STB", 0)])
                    tt_("dve", STB[:, 3, :], STB[:, 0, :], STB[:, 0, :], ALU.mult, [("STB", 0)], [("STB", 3)])
                    tt_("dve", STB[:, 1, :], bank(bq), STB[:, 3, :], ALU.subtract, [("ps", bq), ("STB", 3)], [("STB", 1)])
                    act(STB[:, 1, :], STB[:, 1, :], AF.Ln, [("STB", 1)], [("STB", 1)], bias=LN_EPS)
                    act(STB[:, 1, :], STB[:, 1, :], AF.Exp, [("STB", 1)], [("STB", 1)], scale=-0.5)
                    stt(STB[:, 2, :], STB[:, 0, :], -1.0, STB[:, 1, :], ALU.mult, ALU.mult, [("STB", 0), ("STB", 1)], [("STB", 2)])
                    for i in range(8):
                        ub = i % 2
                        tt_("dve", UN[:, ub, :], ACC[:, i, :], STB[:, 1, :], ALU.mult, [("ACC", i), ("STB", 1)], [("UN", ub)])
                        tt_("dve", UN[:, ub, :], UN[:, ub, :], STB[:, 2, :], ALU.add, [("UN", ub), ("STB", 2)], [("UN", ub)])
                        act(SQ[:, ub, :], UN[:, ub, :], AF.Identity, [("UN", ub), ("CA", 0), ("CA", 1), ("CA", 2), ("CA", 3), ("CA", 4), ("CAt",)], [("SQ", ub)],
                            bias=CA[:, 32 + i:33 + i], scale=CA[:, 24 + i:25 + i])
                        act(RT[:, ub, :], UN[:, ub, :], AF.Sigmoid, [("UN", ub), ("CA", 0), ("CA", 1), ("CA", 2), ("CA", 3), ("CA", 4), ("CAt",)], [("RT", ub)],
                            bias=CA[:, 32 + i:33 + i], scale=CA[:, 24 + i:25 + i])
                        tt_("pool", VT[:, i, :], SQ[:, ub, :], RT[:, ub, :], ALU.mult, [("SQ", ub), ("RT", ub)], [("VT", i)])
                    if debug and pp == 0:
                        dma("sp", "dbg", dbg_acc[:, :, :], ACC[:, :, :], [("ACC", c) for c in range(8)], [("dbg", 1)])
                        dma("sp", "dbg", dbg_vt[:, :, :], VT[:, :, :], [("VT", c) for c in range(8)], [("dbg", 2)])
                    s0 = ws.get(("pw2", 0))
                    s1 = ws.get(("pw2", 1))
                    for t in range(4):
                        pb = nb2()
                        for half, slot in ((0, s0), (1, s1)):
                            b = pb + half
                            mm(bank(b), ONEB[0:1, :], PBR[0:1, 0, half * 512:(half + 1) * 512], True, False,
                               [("ONEB",), ("PBR", 0)], [("ps", b)])
                            for k in range(8):
                                mm(bank(b), VT[:, k, t * 128:(t + 1) * 128], WR[:, slot, k * 512:(k + 1) * 512], False, k == 7,
                                   [("VT", k), ("WR", slot)], [("ps", b)])
                        if t == 3:
                            ws.release(2)
                        epilogue(t, pb, X[:, 1 + t, :], ("X", 1 + t), None, X[:, 1 + t, :], ("X", 1 + t), True)
                    if debug and pp == 0:
                        dma("sp", "dbg", dbg_x0[:, :, :], X[:, 1:5, :], [("X", 1 + t) for t in range(4)], [("dbg", 3)])
                    if pp == 0:
                        emit_cvts(8, nodep=True)
                    mlp(ws, 1, "w1_0", "w2_0", 1)
                    for t in range(4):
                        if own:
                            epilogue(t, 2 * t, X[:, 1 + t, :], ("X", 1 + t), 1, XO[:, t % 2, :], ("XO", t % 2), True)
                            dma("sp", "xo%d" % (t % 2), x1_scr[g, t * 128:(t + 1) * 128, :], XO[:, t % 2, :],
                                [("XO", t % 2)], [("x1", g, t)])
                        else:
                            epilogue(t, 2 * t, X[:, 1 + t, :], ("X", 1 + t), 1, None, None, True)
                    if pp + 1 < 2 * ng:
                        load_x(pp + 1)
                    emit_cvts(2)
                    to_hT(2, False)
                    for n in range(0 if own else 2, 4):
                        slot = ws.get(("qkv", n))
                        for cg in range(4):
                            b = nb()
                            for k in range(8):
                                mm(bank(b), WR[:, slot, k * 512 + cg * 128:k * 512 + cg * 128 + 128], HT[:, k, HALO:TT], k == 0, k == 7,
                                   [("WR", slot), ("HT", k)], [("ps", b)])
                            qb = (n * 4 + cg) % 2
                            if n < 2:
                                act(QS[:, qb, :], bank(b), AF.Copy, [("ps", b)], [("QS", qb)], scale=0.125)
                            else:
                                cp("dve", QS[:, qb, :], bank(b), [("ps", b)], [("QS", qb)])
                            r0 = ((n % 2) * 8 + cg * 2) * 64
                            if n < 2:
                                dst = qt_scr[r0:r0 + 128, g * GR:(g + 1) * GR]
                            else:
                                dst = kv_all[kvl * 2048 + r0:kvl * 2048 + r0 + 128, g * GR:(g + 1) * GR]
                            dma("sp", "qs%d" % qb, dst, QS[:, qb, :], [("QS", qb)],
                                [("qk", n, cg, g)] if n < 2 else [("kw", kvl, n, cg, g)])
                        ws.release()
                    sv0 = ws.get(("qkv", 4))
                    sv1 = ws.get(("qkv", 5))
                    Vv = kv_all[kvl * 2048 + 1024:kvl * 2048 + 2048, :].rearrange("r (a c) -> (r a) c", a=4)
                    for t in range(4):
                        pb = nb2()
                        for half, slot in ((0, sv0), (1, sv1)):
                            b = pb + half
                            for k in range(8):
                                mm(bank(b), HT[:, k, HALO + t * 128:HALO + (t + 1) * 128], WR[:, slot, k * 512:(k + 1) * 512], k == 0, k == 7,
                                   [("HT", k), ("WR", slot)], [("ps", b)])
                        if t == 3:
                            ws.release(2)
                        vb = 0
                        cp("act" if t % 2 else "dve", VS[:, vb, :], PS[pb // 2][:, :], [("ps", pb), ("ps", pb + 1)], [("VS", vb)])
                        dma("sp", "vs%d" % vb, Vv[g * GR + t * 128:g * GR + (t + 1) * 128, :], VS[:, vb, :], [("VS", vb)], [("vw", kvl, g, t)])
                emit_cvts()
                lanesA = ["xo0", "xo1", "qs0", "qs1", "vs0", "vs1"]
                sch.wait_all("sp", lanesA)
                sch.flush()

        if doB:
            bst = ExitStack()
            with bst:
                QT = sb("QT", [128, 2, 2, GR], BF16, bst)
                KT = sb("KT", [128, 3, 2, GR], BF16, bst)
                VA = sb("VA", [128, 3, 4, 132], BF16, bst)
                PT = sb("PT", [128, 2, 1024], BF16, bst)
                BA = sb("BA", [128, 2, 2304], BF16, bst)
                CH = sb("CH", [128, NH], F32, bst)
                LM = sb("LM", [128, 4, 64], F32, bst)
                LS = sb("LS", [128, 8], F32, bst)
                GSUB = sb("GSUB", [128, 128], F32, bst)
                OS = sb("OS", [128, 2, 128], F32, bst)
                SM = sb("SM", [128, 2, 8], F32, bst)
                OJ = sb("OJ", [128, 128], F32, bst)
                OAC = sb("OAC", [128, 1032], F32, bst)

                dma("sp", "pl6", CH[:, :], I["ch"], [], [("CH",)])
                for q, nm in enumerate(("attn_lam_q1", "attn_lam_k1", "attn_lam_q2", "attn_lam_k2")):
                    dma("sp", "pl7", LM[:, q, :], I[nm].partition_broadcast(128), [], [("LM", q)])
                dma("sp", "pl7", GSUB[:, :], I["attn_subln_g"].partition_broadcast(128), [], [("GSUB",)])
                ts_("dve", GSUB[:, :], GSUB[:, :], 1.0 - LAMBDA_INIT, None, ALU.mult, None, [("GSUB",)], [("GSUB",)])
                for q in range(2):
                    stt(LM[:, 2 * q, :], LM[:, 2 * q, :], 1.0, LM[:, 2 * q + 1, :], ALU.mult, ALU.mult,
                        [("LM", 2 * q), ("LM", 2 * q + 1)], [("LM", 2 * q)], accum=LS[:, q:q + 1])
                act(LS[:, 2:4], LS[:, 0:2], AF.Exp, [("LM", 0), ("LM", 2)], [("LS",)])
                tt_("dve", LS[:, 4:5], LS[:, 2:3], LS[:, 3:4], ALU.subtract, [("LS",)], [("LS",)])
                ts_("dve", LS[:, 5:6], LS[:, 4:5], LAMBDA_INIT, -1.0, ALU.add, ALU.mult, [("LS",)], [("LS",)])
                sch.op("dve", lambda e: e.memset(VA[:, :, :, 128:129], 1.0), [], [("VA", 0), ("VA", 1), ("VA", 2)])
                sch.op("dve", lambda e: e.memset(QT[64:128, :, :, :], 0.0), [], [("QT", 0), ("QT", 1)])
                sch.op("dve", lambda e: e.memset(KT[64:128, :, :, :], 0.0), [], [("KT", 0), ("KT", 1), ("KT", 2)])
                load_bc(0, modscr[2, 2 * D:3 * D])
                load_bc(1, I["post_mlp_g"][0])
                load_bc(2, I["post_mix_g"][1])
                load_bc(3, modscr[3, 2 * D:3 * D])
                load_bc(4, I["post_mlp_g"][1])
                load_bc(5, I["post_mlp_b"][1])
                ts_("dve", BC[:, 1, :], BC[:, 1, :], ALPHA, None, ALU.mult, None, [("BC", 1)], [("BC", 1)])
                ts_("dve", BC[:, 2, :], BC[:, 2, :], ALPHA, None, ALU.mult, None, [("BC", 2)], [("BC", 2)])
                dma("sp", "pr0", Z[0:1, 0, :], I["post_mlp_b"][0].rearrange("(o n) -> o n", o=1), [], [("Z", 0)])
                dma("sp", "pr1", Z[0:1, 1, :], I["post_mix_b"][1].rearrange("(o n) -> o n", o=1), [], [("Z", 1)])
                ts_("dve", PBR[0:1, 0, :], Z[0:1, 0, :], ALPHA, None, ALU.mult, None, [("Z", 0)], [("PBR", 0)])
                ts_("dve", PBR[0:1, 1, :], Z[0:1, 1, :], ALPHA, None, ALU.mult, None, [("Z", 1)], [("PBR", 1)])
                if not doA:
                    sch.op("dve", lambda e: e.memset(ONEB[:, :], 1.0), [], [("ONEB",)])

                seqB = []
                for g in range(ng):
                    seqB += [("wo", n) for n in range(2)] + [("w1_1", f) for f in range(8)] + [("w2_1", c) for c in range(8)]
                ws = WStream(seqB)

                Vall = [kv_all[r * 2048 + 1024:r * 2048 + 2048, :].rearrange("r (a c) -> (r a) c", a=4) for r in range(2)]

                def acc_ap(m, qs):
                    a = m * 4 + qs
                    bk, sl = a // 3, a % 3
                    if bk < 2:
                        return PS[2][:, bk * 512 + sl * 129:bk * 512 + sl * 129 + 129], ("ps", 4 + bk)
                    return PS[3][:, sl * 129:sl * 129 + 129], ("ps", 6)

                kvc = [0]
                hcount = [0]
                for j in range(ng):
                    dma("sp", "xl1", X[:, 1:5, :], x1_scr[j].rearrange("(t p) d -> p t d", p=128),
                        [("x1", j, t) for t in range(4)], [("X", 1), ("X", 2), ("X", 3), ("X", 4)])
                    for h in range(NH):
                        hb = hcount[0] % 2
                        hcount[0] += 1

                        def load_head(jj, hh, hbb):
                            dma("sp", "qt%d" % hbb, QT[0:64, hbb, :, :],
                                qt_scr[2 * hh * 64:2 * hh * 64 + 128, jj * GR:(jj + 1) * GR].rearrange("(m d) t -> d m t", m=2),
                                [("qk", n, cg, jj) for n in range(2) for cg in range(4)], [("QT", hbb)])
                            dma("pool", "ba%d" % hbb, BA[:, hbb, :], I["biasarr"][hh], [], [("BA", hbb)])

                        if j == 0 and h == 0:
                            load_head(0, 0, hb)
                        if h + 1 < NH:
                            load_head(j, h + 1, 1 - hb)
                        elif j + 1 < ng:
                            load_head(j + 1, 0, 1 - hb)
                        blocks = []
                        for i in range(j + 1):
                            for r in range(2):
                                for kb in range(4):
                                    sp_off = None
                                    if i == j:
                                        sp_off = (0 if r == 0 else 896) + 384 - kb * 128
                                    elif i == j - 1 and r == 1 and kb == 3:
                                        sp_off = 1792
                                    blocks.append((r, i, kb, sp_off))
                        kvslot = {}

                        def load_kv(r, i):
                            sl = kvc[0] % 3
                            kvc[0] += 1
                            kvslot[(r, i)] = sl
                            dma("sp", "kt%d" % sl, KT[0:64, sl, :, :],
                                kv_all[r * 2048 + 2 * h * 64:r * 2048 + 2 * h * 64 + 128, i * GR:(i + 1) * GR].rearrange("(m d) t -> d m t", m=2),
                                [], [("KT", sl)])
                            dma("sp", "va%d" % sl, VA[:, sl, :, 0:128],
                                Vall[r][i * GR:(i + 1) * GR, h * 128:(h + 1) * 128].rearrange("(kb p) e -> p kb e", p=128),
                                [], [("VA", sl)], nonc=True)

                        def qk(n):
                            r, i, kb, _ = blocks[n]
                            if (r, i) not in kvslot:
                                load_kv(r, i)
                            sl = kvslot[(r, i)]
                            sbuf_ = n % 2
                            spo = blocks[n][3]
                            for m in range(2):
                                b = 2 * sbuf_ + m
                                mm(bank(b), KT[:, sl, m, kb * 128:(kb + 1) * 128], QT[:, hb, m, :], True, spo is None,
                                   [("KT", sl), ("QT", hb)], [("ps", b)], signal=(m == 1 and spo is None))
                                if spo is not None:
                                    mm(bank(b), IDB[:, :], BA[:, hb, spo:spo + 512], False, True,
                                       [("IDB",), ("BA", hb)], [("ps", b)], signal=(m == 1))

                        qk(0)
                        nblk = len(blocks)
                        for n in range(nblk):
                            r, i, kb, sp_off = blocks[n]
                            if n + 1 < nblk:
                                qk(n + 1)
                            sbuf_ = n % 2
                            sl = kvslot[(r, i)]
                            b0 = 2 * sbuf_
                            if sp_off is not None:
                                act(PT[:, sbuf_, :], PS[sbuf_][:, :], AF.Exp, [("ps", b0), ("ps", b0 + 1)], [("PT", sbuf_)])
                            else:
                                act(PT[:, sbuf_, :], PS[sbuf_][:, :], AF.Exp, [("ps", b0), ("ps", b0 + 1), ("CH",)], [("PT", sbuf_)],
                                    bias=CH[:, h:h + 1])
                            for m in range(2):
                                for qs in range(4):
                                    ap_, key_ = acc_ap(m, qs)
                                    mm(ap_, PT[:, sbuf_, m * 512 + qs * 128:m * 512 + qs * 128 + 128], VA[:, sl, kb, 0:129],
                                       n == 0 and (m * 4 + qs) % 3 == 0, n == nblk - 1, [("PT", sbuf_), ("VA", sl)], [key_],
                                       signal=(m == 1 and qs == 3))
                        cp("dve", OAC[:, 0:387], PS[2][:, 0:387], [("ps", 4)], [("OAC", 0)])
                        cp("dve", OAC[:, 387:774], PS[2][:, 512:899], [("ps", 5)], [("OAC", 1)])
                        cp("dve", OAC[:, 774:1032], PS[3][:, 0:258], [("ps", 6)], [("OAC", 2)])
                        for qs in range(4):
                            ob = qs % 2
                            i1, i2 = qs, 4 + qs
                            a1, k1 = OAC[:, i1 * 129:(i1 + 1) * 129], ("OAC", i1 // 3)
                            a2, k2 = OAC[:, i2 * 129:(i2 + 1) * 129], ("OAC", i2 // 3)
                            ko, ks = ("OS", ob), ("SM", ob)
                            sch.op("dve", lambda e, a1=a1, ob=ob: e.reciprocal(out=SM[:, ob, 0:1], in_=a1[:, 128:129]), [k1], [ks])
                            sch.op("dve", lambda e, a2=a2, ob=ob: e.reciprocal(out=SM[:, ob, 1:2], in_=a2[:, 128:129]), [k2], [ks])
                            ts_("dve", SM[:, ob, 2:3], SM[:, ob, 1:2], LS[:, 5:6], None, ALU.mult, None, [ks, ("LS",)], [ks])
                            ts_("dve", OS[:, ob, :], a1[:, 0:128], SM[:, ob, 0:1], None, ALU.mult, None, [k1, ks], [ko])
                            stt(OS[:, ob, :], a2[:, 0:128], SM[:, ob, 2:3], OS[:, ob, :], ALU.mult, ALU.add, [k2, ks, ko], [ko])
                            stt(OJ[:, :], OS[:, ob, :], 1.0, OS[:, ob, :], ALU.mult, ALU.mult, [ko], [("OJ",)], accum=SM[:, ob, 3:4])
                            ts_("dve", SM[:, ob, 4:5], SM[:, ob, 3:4], 1.0 / 128.0, LN_EPS, ALU.mult, ALU.add, [("OJ",), ks], [ks])
                            tt_("pool", SM[:, ob, 5:6], SM[:, ob, 4:5], NHALF[:, 0:1], ALU.pow, [ks, ("NHALF",)], [ks])
                            stt(XB[:, 1 + qs, h * 128:(h + 1) * 128], OS[:, ob, :], SM[:, ob, 5:6], GSUB[:, :], ALU.mult, ALU.mult,
                                [ko, ks, ("GSUB",)], [("XB", 1 + qs)])
                    if debug and j == 0:
                        dma("sp", "dbg", dbg_at[:, :, :], XB[:, 1:5, :], [("XB", 1 + t) for t in range(4)], [("dbg", 5)])
                    bptr[0] = 7
                    for c in range(8):
                        b = 7
                        for t in range(4):
                            mm(bank(b)[:, t * 128:(t + 1) * 128], XB[:, 1 + t, c * 128:(c + 1) * 128], IDB[:, :], True, True,
                               [("XB", 1 + t), ("IDB",)], [("ps", b)], signal=(t == 3))
                        cp("act" if c % 2 else "dve", HT[:, c, HALO:TT], bank(b), [("ps", b)], [("HT", c)])
                    bptr[0] = 0
                    s0 = ws.get(("wo", 0))
                    s1 = ws.get(("wo", 1))
                    for t in range(4):
                        pb = nb2()
                        for half, slot in ((0, s0), (1, s1)):
                            b = pb + half
                            mm(bank(b), ONEB[0:1, :], PBR[0:1, 0, half * 512:(half + 1) * 512], True, False,
                               [("ONEB",), ("PBR", 0)], [("ps", b)])
                            for k in range(8):
                                mm(bank(b), HT[:, k, HALO + t * 128:HALO + (t + 1) * 128], WR[:, slot, k * 512:(k + 1) * 512], False, k == 7,
                                   [("HT", k), ("WR", slot)], [("ps", b)])
                        if t == 3:
                            ws.release(2)
                        epilogue(t, pb, X[:, 1 + t, :], ("X", 1 + t), 1, X[:, 1 + t, :], ("X", 1 + t), True)
                    if debug and j == 0:
                        dma("sp", "dbg", dbg_xq[:, :, :], X[:, 1:5, :], [("X", 1 + t) for t in range(4)], [("dbg", 6)])
                    mlp(ws, 3, "w1_1", "w2_1", 1)
                    tails = []
                    for t in range(4):
                        tl = epilogue(t, 2 * t, X[:, 1 + t, :], ("X", 1 + t), 2, None, None, False,
                                      final=(4, 5, XO[:, t % 2, :], ("XO", t % 2)))

                        def fin(t=t, tl=tl):
                            tl()
                            dma("sp", "xo%d" % (t % 2), out[j, t * 128:(t + 1) * 128, :], XO[:, t % 2, :],
                                [("XO", t % 2)], [("out", j, t)])
                        tails.append(fin)
                        if t > 0:
                            tails[t - 1]()
                    tails[3]()
                sch.wait_all("sp", ["xo0", "xo1"])
                sch.flush()
    return nc


def _t5_bucket(rel):
    n = np.maximum(rel, 0)
    nf = np.maximum(n, 1).astype(np.float32)
    large = 16 + (np.log(nf / np.float32(16)) / np.float32(math.log(128 / 16)) * np.float32(16)).astype(np.int32)
    large = np.minimum(large, 31)
    return np.where(n < 16, n, large)


def _bias_arrays(rel_bias, role):
    p = np.arange(128)[:, None]
    out = np.empty((NH, 128, 2304), np.float32)

    def fill(width, base_rel):
        j = np.arange(width)[None, :]
        rel = j - p + base_rel
        bk = _t5_bucket(rel)
        v = rel_bias[bk]
        v = np.where((rel >= 0)[:, :, None], v, np.float32(NEG))
        return np.transpose(v, (2, 0, 1))

    if role == 0:
        dA, dB, relC = 0, -512, 128
    else:
        dA, dB, relC = 0, 512, 1152
    out[:, :, 0:896] = fill(896, -384 + dA)
    out[:, :, 896:1792] = fill(896, -384 + dB)
    out[:, :, 1792:2304] = fill(512, relC)
    return out


_CACHE = {}


def _get(mode):
    if mode not in _CACHE:
        _CACHE[mode] = build(mode)
    return _CACHE[mode]


def _core_inputs(inputs, c):
    b, r = c // 2, c % 2
    x = inputs["x"]
    d = {}
    for name, shape in W_INPUTS:
        a = np.ascontiguousarray(inputs[name], dtype=np.float32).reshape(shape)
        d[name] = a
    d["cvec"] = np.ascontiguousarray(inputs["c"][b])
    d["ident"] = np.eye(128, dtype=np.float32)
    xs = np.zeros((2 * NG, TT, D), np.float32)
    hv = np.ones((128, 2 * NG), np.float32)
    for pp in range(2 * NG):
        i = pp // 2
        G = 2 * i + (r if pp % 2 == 0 else 1 - r)
        lo = G * GR - HALO
        if lo < 0:
            xs[pp, HALO:] = x[b, 0:GR]
            hv[:, pp] = 0.0
        else:
            xs[pp] = x[b, lo:lo + TT]
    d["xs"] = xs
    d["hv"] = hv
    d["biasarr"] = _bias_arrays(np.asarray(inputs["rel_bias"], np.float32), r)
    d["ch"] = np.ascontiguousarray(np.broadcast_to(np.asarray(inputs["rel_bias"], np.float32)[31][None, :], (128, NH)))
    return d


FUSED = True


def kernel(**inputs):
    inputs = {k: np.asarray(v) for k, v in inputs.items()}
    cores = list(range(8))
    per = [_core_inputs(inputs, c) for c in cores]
    if FUSED:
        nc = _get("F")
        keys = [t for t in per[0].keys()]
        res = run_bass_kernel_spmd(nc, per, core_ids=cores)
        outs = [r["out"] for r in res.results]
    else:
        ncA = _get("A")
        inA = [{k: v for k, v in p.items() if k not in ("biasarr", "ch")} for p in per]
        resA = run_bass_kernel_spmd(ncA, inA, core_ids=cores).results
        ncB = _get("B")
        inB = []
        for c in cores:
            p = {k: v for k, v in per[c].items() if k not in ("xs", "hv")}
            p["kv_all"] = resA[c]["kv_all"]
            p["x1_scr"] = resA[c]["x1_scr"]
            p["qt_scr"] = resA[c]["qt_scr"]
            inB.append(p)
        resB = run_bass_kernel_spmd(ncB, inB, core_ids=cores).results
        outs = [r["out"] for r in resB]
    y = np.empty((NB, SEQ, D), np.float32)
    for c in cores:
        b, r = c // 2, c % 2
        o = np.asarray(outs[c]).reshape(NG, GR, D)
        for i in range(NG):
            G = 2 * i + r
            y[b, G * GR:(G + 1) * GR] = o[i]
    return y
```

```python
import math
from contextlib import ExitStack

import numpy as np
import concourse.bass as bass
import concourse.mybir as mybir
from concourse.bass_utils import run_bass_kernel_spmd

F32 = mybir.dt.float32
BF16 = mybir.dt.bfloat16
AF = mybir.ActivationFunctionType
ALU = mybir.AluOpType

D = 1024
SEQ = 8192
NB = 4
DFF = 4096
GR = 512
HALO = 32
TT = GR + HALO
NG = 8
NH = 8
ALPHA = 4.0 ** 0.25
LN_EPS = 1e-5
LAMBDA_INIT = 0.8 - 0.6 * math.exp(-0.3 * 1)
CONVW = 31
NEG = -30000.0

ENGS = ("pe", "act", "dve", "pool", "sp")


class Sched:
    def __init__(self, nc, stack, strict_same=True):
        self.nc = nc
        self.stack = stack
        self.sems = {}
        self.cnt = {}
        self.prog = {e: [] for e in ENGS}
        self.seen = {e: {} for e in ENGS}
        self.last_w = {}
        self.readers = {}
        self.strict_same = strict_same
        for e in ENGS:
            self._mk(e)

    def _mk(self, name):
        self.sems[name] = self.stack.enter_context(self.nc.semaphore("s_" + name))
        self.cnt[name] = 0

    def _deps(self, reads, writes):
        d = {}
        for k in reads:
            w = self.last_w.get(k)
            if w:
                d[w[0]] = max(d.get(w[0], 0), w[1])
        for k in writes:
            w = self.last_w.get(k)
            if w:
                d[w[0]] = max(d.get(w[0], 0), w[1])
            for e2, c in self.readers.get(k, {}).items():
                d[e2] = max(d.get(e2, 0), c)
        return d

    def _wait(self, eng, d):
        for src, c in d.items():
            if src == eng and (eng == "pe" or not self.strict_same):
                continue
            if self.seen[eng].get(src, 0) >= c:
                continue
            self.seen[eng][src] = c
            if c > self.cnt[src]:
                print("SCHED WARNING: %s waits on future signal of %s (%d > %d)" % (eng, src, c, self.cnt[src]))
            unit = 1 if src in ENGS else 16
            self.prog[eng].append(("wait", src, c * unit))

    def _record(self, src, n, reads, writes):
        for k in reads:
            self.readers.setdefault(k, {})[src] = n
        for k in writes:
            self.last_w[k] = (src, n)
            self.readers[k] = {}

    def op(self, eng, fn, reads=(), writes=(), signal=True):
        self._wait(eng, self._deps(reads, writes))
        n = self.cnt[eng] + 1
        if signal:
            self.cnt[eng] = n
        self.prog[eng].append(("op", fn, eng if signal else None, 1))
        self._record(eng, n, reads, writes)

    def dma(self, issuer, lane, fn, reads=(), writes=()):
        if lane not in self.sems:
            self._mk(lane)
        d = self._deps(reads, writes)
        if self.cnt[lane] > 0:
            d[lane] = max(d.get(lane, 0), self.cnt[lane])
        self._wait(issuer, d)
        self.cnt[lane] += 1
        self.prog[issuer].append(("op", fn, lane, 16))
        self._record(lane, self.cnt[lane], reads, writes)

    def wait_all(self, eng, lanes):
        d = {l: self.cnt[l] for l in lanes if self.cnt.get(l, 0) > 0}
        self._wait(eng, d)

    def flush(self):
        nc = self.nc
        prog = self.prog
        sems = self.sems
        self.prog = {e: [] for e in ENGS}

        def run(e, lst):
            for it in lst:
                if it[0] == "wait":
                    e.wait_ge(sems[it[1]], it[2])
                else:
                    ins = it[1](e)
                    if it[2] is not None:
                        ins.then_inc(sems[it[2]], it[3])

        with nc.Block() as block:
            @block.tensor
            def _(e):
                run(e, prog["pe"])

            @block.scalar
            def _(e):
                run(e, prog["act"])

            @block.vector
            def _(e):
                run(e, prog["dve"])

            @block.gpsimd
            def _(e):
                run(e, prog["pool"])

            @block.sync
            def _(e):
                run(e, prog["sp"])


W_INPUTS = [
    ("conv_mod_w", [D, 3 * D]), ("conv_mod_b", [3 * D]),
    ("conv_pw1_w", [D, 2 * D]), ("conv_pw1_b", [2 * D]),
    ("conv_dw_w", [CONVW, D]), ("conv_dw_b", [D]),
    ("conv_norm_g", [D]), ("conv_norm_b", [D]),
    ("conv_pw2_w", [D, D]), ("conv_pw2_b", [D]),
    ("attn_mod_w", [D, 3 * D]), ("attn_mod_b", [3 * D]),
    ("attn_qkv_w", [D, 3 * D]),
    ("attn_lam_q1", [64]), ("attn_lam_k1", [64]), ("attn_lam_q2", [64]), ("attn_lam_k2", [64]),
    ("attn_subln_g", [128]),
    ("attn_out_w", [D, D]),
    ("mlp_mod_w", [2, D, 3 * D]), ("mlp_mod_b", [2, 3 * D]),
    ("mlp_w1", [2, D, DFF]), ("mlp_w2", [2, DFF, D]),
    ("post_mix_g", [2, D]), ("post_mix_b", [2, D]),
    ("post_mlp_g", [2, D]), ("post_mlp_b", [2, D]),
]


def build(mode, ng=NG, strict_same=True, debug=False):
    doA = mode in ("A", "F")
    doB = mode in ("B", "F")
    nc = bass.Bass("TRN2", target_bir_lowering=False)
    I = {}

    def din(name, shape, dt=F32):
        I[name] = nc.dram_tensor(name, list(shape), dt, kind="ExternalInput").ap()
        return I[name]

    for name, shape in W_INPUTS:
        din(name, shape)
    din("cvec", [D])
    din("ident", [128, 128])
    if doA:
        din("xs", [2 * ng, TT, D])
        din("hv", [128, 2 * NG])
    if doB:
        din("biasarr", [NH, 128, 2304])
        din("ch", [128, NH])

    inter = "Internal" if mode == "F" else None
    x1_scr = nc.dram_tensor("x1_scr", [ng, GR, D], F32,
                            kind=inter or ("ExternalOutput" if mode == "A" else "ExternalInput")).ap()
    qt_scr = nc.dram_tensor("qt_scr", [16 * 64, NG * GR], BF16,
                            kind=inter or ("ExternalOutput" if mode == "A" else "ExternalInput")).ap()
    kv_all = nc.dram_tensor("kv_all", [4096, NG * GR], BF16,
                            kind=inter or ("ExternalOutput" if mode == "A" else "ExternalInput")).ap()
    if doB:
        out = nc.dram_tensor("out", [ng, GR, D], F32, kind="ExternalOutput").ap()
    NCH = 46
    if debug and doB:
        dbg_at = nc.dram_tensor("dbg_at", [128, 4, D], BF16, kind="ExternalOutput").ap()
        dbg_xq = nc.dram_tensor("dbg_xq", [128, 4, D], F32, kind="ExternalOutput").ap()
    if debug and doA:
        dbg_ht = nc.dram_tensor("dbg_ht", [128, 8, TT], BF16, kind="ExternalOutput").ap()
        dbg_acc = nc.dram_tensor("dbg_acc", [128, 8, GR], F32, kind="ExternalOutput").ap()
        dbg_vt = nc.dram_tensor("dbg_vt", [128, 8, GR], BF16, kind="ExternalOutput").ap()
        dbg_x0 = nc.dram_tensor("dbg_x0", [128, 4, D], F32, kind="ExternalOutput").ap()
        dbg_cv = nc.dram_tensor("dbg_cv", [128, 64], F32, kind="ExternalOutput").ap()
    wscr = nc.dram_tensor("wscr", [NCH, 128, 4096], BF16, kind="Internal").ap()
    modscr = nc.dram_tensor("modscr", [4, 3 * D], F32, kind="Internal").ap()

    st = ExitStack()
    with st:
        sch = Sched(nc, st, strict_same=strict_same)

        def sb(name, shape, dt, stack=st):
            return stack.enter_context(nc.sbuf_tensor(name, list(shape), dt))

        NWR = 3
        WR = sb("WR", [128, NWR, 4096], BF16)
        X = sb("X", [128, 5, D], F32)
        XB = sb("XB", [128, 5, D], BF16)
        HT = sb("HT", [128, 8, TT], BF16)
        HID = sb("HID", [128, 32, GR], BF16)
        Z = sb("Z", [128, 2, D], F32)
        ZN = sb("ZN", [128, 2, D], F32)
        XO = sb("XO", [128, 2, D], F32)
        BC = sb("BC", [128, 6, D], F32)
        RT = sb("RT", [128, 2, GR], F32)
        CV = sb("CV", [128, 8 * 8], F32)
        IDB = sb("IDB", [128, 128], BF16)
        STt = sb("STt", [128, 2, 12], F32)
        MV = sb("MV", [128, 2, 4], F32)
        NHALF = sb("NHALF", [128, GR], F32)
        PBR = sb("PBR", [1, 2, D], BF16)
        if not doA:
            ONEB = sb("ONEB", [1, 128], BF16)
        PS = [st.enter_context(nc.psum_tensor("PS%d" % i, [128, 1024], F32)) for i in range(4)]

        def bank(b):
            return PS[b // 2][:, (b % 2) * 512:(b % 2) * 512 + 512]

        bptr = [0]
        nbmod = [8]

        def nb():
            b = bptr[0]
            bptr[0] = (b + 1) % nbmod[0]
            return b

        def nb2():
            if bptr[0] % 2:
                bptr[0] = (bptr[0] + 1) % 8
            b = bptr[0]
            bptr[0] = (b + 2) % 8
            return b

        def mm(out_ap, lhsT, rhs, start, stop, reads, writes, signal=None):
            sig = stop if signal is None else signal
            sch.op("pe", lambda e: e.matmul(out_ap, lhsT=lhsT, rhs=rhs, start=start, stop=stop),
                   reads, writes, signal=sig)

        def act(out_ap, in_ap, func, reads, writes, bias=None, scale=None):
            kw = {}
            if bias is not None:
                kw["bias"] = bias
            if scale is not None:
                kw["scale"] = scale
            sch.op("act", lambda e: e.activation(out=out_ap, in_=in_ap, func=func, **kw), reads, writes)

        def tt_(eng, out_ap, a, b, op, reads, writes):
            sch.op(eng, lambda e: e.tensor_tensor(out=out_ap, in0=a, in1=b, op=op), reads, writes)

        def ts_(eng, out_ap, a, s1, s2, op0, op1, reads, writes):
            if s2 is None:
                sch.op(eng, lambda e: e.tensor_scalar(out=out_ap, in0=a, scalar1=s1, scalar2=None, op0=op0),
                       reads, writes)
            else:
                sch.op(eng, lambda e: e.tensor_scalar(out=out_ap, in0=a, scalar1=s1, scalar2=s2, op0=op0, op1=op1),
                       reads, writes)

        def stt(out_ap, a, s, b, op0, op1, reads, writes, accum=None):
            if accum is None:
                sch.op("dve", lambda e: e.scalar_tensor_tensor(out=out_ap, in0=a, scalar=s, in1=b, op0=op0, op1=op1),
                       reads, writes)
            else:
                sch.op("dve", lambda e: e.scalar_tensor_tensor(out=out_ap, in0=a, scalar=s, in1=b, op0=op0, op1=op1,
                                                               accum_out=accum), reads, writes)

        def cp(eng, out_ap, in_ap, reads, writes):
            if eng == "act":
                sch.op("act", lambda e: e.copy(out=out_ap, in_=in_ap), reads, writes)
            else:
                sch.op(eng, lambda e: e.tensor_copy(out=out_ap, in_=in_ap), reads, writes)

        def dma(issuer, lane, out_ap, in_ap, reads, writes, nonc=False):
            if nonc:
                sch.dma(issuer, lane, lambda e: e.dma_start(out=out_ap, in_=in_ap, allow_slow_non_contiguous=True),
                        reads, writes)
            else:
                sch.dma(issuer, lane, lambda e: e.dma_start(out=out_ap, in_=in_ap), reads, writes)

        chunk_id = {}
        cvl = [0]

        cvt_jobs = []

        def cvt(out_ap, in_ap, ci):
            cvt_jobs.append((out_ap, in_ap, ci))

        def emit_cvts(nmax=None, nodep=False):
            k = 0
            while cvt_jobs and (nmax is None or k < nmax):
                (out_ap, in_ap, ci) = cvt_jobs.pop(0)
                lane = "cv%d" % (cvl[0] % 4)
                first = (k == 0) and not nodep
                cvl[0] += 1
                k += 1
                dma("pool", lane, out_ap, in_ap, [("modscr", s_, n_) for s_ in range(4) for n_ in range(6)] if first else [], [("wscr", ci)])

        def add_kmajor(name, W, ncols):
            Wv = W.rearrange("(k p) n -> p k n", p=128)
            for n in range(ncols // 512):
                ci = len(chunk_id)
                chunk_id[(name, n)] = ci
                cvt(wscr[ci].rearrange("p (k n) -> p k n", k=8), Wv[:, :, n * 512:(n + 1) * 512], ci)

        def add_w2(name, W):
            Wv = W.rearrange("(f p) n -> p f n", p=128)
            for c in range(8):
                ci = len(chunk_id)
                chunk_id[(name, c)] = ci
                cvt(wscr[ci].rearrange("p (f n) -> p f n", f=4), Wv[:, 4 * c:4 * c + 4, :], ci)

        if doA:
            Wv = I["conv_pw1_w"].rearrange("(k p) n -> p k n", p=128)
            for j in range(4):
                ci = len(chunk_id)
                chunk_id[("pw1", j)] = ci
                dst = wscr[ci].rearrange("p (k n) -> p k n", k=8)
                cvt(dst[:, :, 0:256], Wv[:, :, 256 * j:256 * j + 256], ci)
                cvt(dst[:, :, 256:512], Wv[:, :, 1024 + 256 * j:1024 + 256 * j + 256], ci)
            add_kmajor("pw2", I["conv_pw2_w"], D)
            add_kmajor("w1_0", I["mlp_w1"][0], DFF)
            add_w2("w2_0", I["mlp_w2"][0])
            add_kmajor("qkv", I["attn_qkv_w"], 3 * D)
        if doB:
            add_kmajor("wo", I["attn_out_w"], D)
            add_kmajor("w1_1", I["mlp_w1"][1], DFF)
            add_w2("w2_1", I["mlp_w2"][1])

        GATED = {"pw2": (0, "k"), "w2_0": (3, "f"), "wo": (0, "k"), "w2_1": (3, "f")}
        scaled_chunks = set()

        class WStream:
            def __init__(self, seq):
                self.seq = seq
                self.issued = 0
                self.pos = 0
                self.released = 0

            def _issue(self):
                k = self.issued
                slot = k % NWR
                ci = chunk_id[self.seq[k]]
                assert ("wscr", ci) in sch.last_w, ("weight chunk loaded before its conversion was emitted", self.seq[k])
                dma("sp", "wr%d" % slot, WR[:, slot, :], wscr[ci], [("wscr", ci)], [("WR", slot)])
                self.issued += 1

            def prefetch(self):
                while self.issued < len(self.seq) and self.issued - NWR < self.released:
                    self._issue()

            def get(self, name):
                assert self.seq[self.pos] == name, (self.seq[self.pos], name)
                self.prefetch()
                assert self.issued > self.pos
                slot = self.pos % NWR
                self.pos += 1
                if name[0] in GATED and name not in scaled_chunks:
                    scaled_chunks.add(name)
                    gslot, kind = GATED[name[0]]
                    ci = chunk_id[name]
                    if kind == "k":
                        n0 = name[1] * 512
                        for k in range(8):
                            tt_("dve", WR[:, slot, k * 512:(k + 1) * 512], WR[:, slot, k * 512:(k + 1) * 512],
                                BC[:, gslot, n0:n0 + 512], ALU.mult, [("WR", slot), ("BC", gslot)], [("WR", slot)])
                    else:
                        for f in range(4):
                            tt_("dve", WR[:, slot, f * 1024:(f + 1) * 1024], WR[:, slot, f * 1024:(f + 1) * 1024],
                                BC[:, gslot, :], ALU.mult, [("WR", slot), ("BC", gslot)], [("WR", slot)])
                    dma("sp", "wsb%d" % slot, wscr[ci], WR[:, slot, :], [("WR", slot)], [("wscr", ci)])
                return slot

            def release(self, n=1):
                self.released += n
                self.prefetch()

        if doA:
            CA = sb("CA", [128, 8 * 5 + CONVW * 8], F32)
            HVt = sb("HVt", [128, 2 * NG], F32)
            ONES = sb("ONES", [128, 128], F32)
            ONEB = sb("ONEB", [1, 128], BF16)
            dma("sp", "pl6", HVt[:, :], I["hv"], [], [("HV",)])
            sch.op("dve", lambda e: e.memset(ONES[:, :], 1.0 / D), [], [("ONES",)])
            sch.op("dve", lambda e: e.memset(ONEB[:, :], 1.0), [], [("ONEB",)])
        pst = ExitStack()
        with pst:
            MW = sb("MW", [128, 2, 8, 512], F32, pst)
            SREP = sb("SREP", [128, 8, 128], F32, pst)
            CCOL = sb("CCOL", [128, 8], F32, pst)
            SCOL = sb("SCOL", [128, 8], F32, pst)
            MB = sb("MB", [128, 2, 512], F32, pst)
            MROW = sb("MROW", [128, 2, 512], F32, pst)
            mwc = [0]
            TMPC4 = sb("TMPC", [128, 4, 48], F32, pst)
            IDF = sb("IDF", [64, 32], F32, pst)
            E0 = sb("E0", [128, 1], F32, pst)
            RW = sb("RW", [64, D], F32, pst)
            ROWS = RW[0:16, :]
            TAPR = RW[32:64, :]
            COLV = sb("COLV", [128, 8, 11], F32, pst)
            MCOL = sb("MCOL", [128, 64], F32, pst)
            nbmod[0] = 5
            dma("sp", "pl7", IDF[0:32, :], I["ident"][0:32, 0:32], [], [("IDF",)])
            dma("sp", "pl7", IDF[32:64, :], I["ident"][0:32, 0:32], [], [("IDF",)])
            dma("sp", "pl7", E0[:, :], I["ident"][:, 0:1], [], [("E0",)], nonc=True)
            rowvecs = [I["post_mix_g"][0], I["post_mix_b"][0], I["post_mlp_g"][0], I["post_mlp_b"][0],
                       I["post_mix_g"][1], I["post_mix_b"][1]]
            if doA:
                rowvecs += [I["conv_pw1_b"][0:D], I["conv_pw1_b"][D:2 * D], I["conv_dw_b"], I["conv_norm_g"], I["conv_norm_b"]]
            for vi, vec in enumerate(rowvecs):
                dma("sp", "pl8", ROWS[vi:vi + 1, :], vec.rearrange("(o n) -> o n", o=1), [], [("ROWS",)])
            nv = len(rowvecs)
            for c in range(8):
                mm(bank(6)[:, c * 11:c * 11 + nv], ROWS[0:nv, c * 128:(c + 1) * 128], IDF[0:nv, 0:nv], True, True,
                   [("ROWS",), ("IDF",)], [("ps", 6)], signal=(c == 7))
            cp("dve", COLV[:, :, :], bank(6)[:, 0:88].rearrange("p (c v) -> p c v", v=11), [("ps", 6)], [("COLV",)])
            if doA:
                dma("sp", "pl8", TAPR[0:CONVW, :], I["conv_dw_w"], [], [("TAPR",)])
                for c in range(8):
                    mm(bank(5)[:, c * CONVW:(c + 1) * CONVW], TAPR[0:CONVW, c * 128:(c + 1) * 128], IDF[32:32 + CONVW, 0:CONVW], True, True,
                       [("TAPR",), ("IDF",)], [("ps", 5)], signal=(c == 7))
                cp("dve", CA[:, 40:40 + CONVW * 8].rearrange("p (j c) -> p j c", c=8),
                   bank(5)[:, 0:CONVW * 8].rearrange("p (c j) -> p j c", j=CONVW), [("ps", 5)], [("CAt",)])
                for k5 in range(5):
                    cp("dve", CA[:, k5 * 8:(k5 + 1) * 8], COLV[:, :, 6 + k5], [("COLV",)], [("CA", k5)])

            dma("pool", "pl0", IDB[:, :], I["ident"], [], [("IDB",)])
            sch.op("dve", lambda e: e.memset(NHALF[:, :], -0.5), [], [("NHALF",)])
            dma("sp", "pl1", CCOL[:, :], I["cvec"].rearrange("(k p) -> p k", p=128), [], [("CCOL",)], nonc=True)
            act(SCOL[:, :], CCOL[:, :], AF.Sigmoid, [("CCOL",)], [("SCOL",)])
            tt_("dve", SCOL[:, :], SCOL[:, :], CCOL[:, :], ALU.mult, [("SCOL",), ("CCOL",)], [("SCOL",)])
            for k in range(8):
                cp("dve", SREP[:, k, :], SCOL[:, k:k + 1].to_broadcast([128, 128]), [("SCOL",)], [("SREP",)])
            if doA:
                emit_cvts(8, nodep=True)
            mods = []
            if doA:
                mods += [(0, I["conv_mod_w"], I["conv_mod_b"]), (1, I["mlp_mod_w"][0], I["mlp_mod_b"][0])]
            mods += [(2, I["attn_mod_w"], I["attn_mod_b"])]
            if doB:
                mods += [(3, I["mlp_mod_w"][1], I["mlp_mod_b"][1])]
            for (s, mw, mb) in mods:
                mwv = mw.rearrange("(k p) n -> p k n", p=128)
                for n in range(6):
                    q = mwc[0] % 2
                    mwc[0] += 1
                    dma("sp", "pl2%d" % q, MW[:, q, :, :], mwv[:, :, n * 512:(n + 1) * 512], [], [("MW", q)])
                    dma("sp", "pl3%d" % q, MB[:, q, :], mb[n * 512:(n + 1) * 512].partition_broadcast(128), [], [("MB", q)])
                    b = nb()
                    for k in range(8):
                        mm(bank(b), SREP[:, k, :], MW[:, q, k, :], k == 0, k == 7,
                           [("SREP",), ("MW", q)], [("ps", b)])
                    tt_("dve", MROW[:, q, :], bank(b), MB[:, q, :], ALU.add, [("ps", b), ("MB", q)], [("MROW", q)])
                    if n < 4:
                        for fc in range(4):
                            col = s * 16 + n * 4 + fc
                            mm(bank(7)[:, col:col + 1], MROW[:, q, fc * 128:(fc + 1) * 128], E0[:, 0:1], True, True,
                               [("MROW", q), ("E0",)], [("ps", 7)], signal=(fc == 3))
                    dma("act", "pl4%d" % q, modscr[s:s + 1, n * 512:(n + 1) * 512], MROW[0:1, q, :], [("MROW", q)], [("modscr", s, n)])

            if mode != "F":
                emit_cvts(None, nodep=True)
            cp("dve", MCOL[:, :], bank(7)[:, 0:64], [("ps", 7)], [("MCOL",)])
            lnv = {1: (0, 1), 2: (2, 3), 3: (4, 5)}
            for (s, _, _) in mods:
                G2 = CV[:, s * 16:s * 16 + 8]
                B2 = CV[:, s * 16 + 8:s * 16 + 16]
                TMPC = TMPC4[:, s, :]
                SH = MCOL[:, s * 16:s * 16 + 8]
                ts_("dve", TMPC[:, 8:16], MCOL[:, s * 16 + 8:s * 16 + 16], 1.0, None, ALU.add, None, [("MCOL",)], [("TMPC", s, 1)])
                if s == 0:
                    cp("dve", G2, TMPC[:, 8:16], [("TMPC", s, 1)], [("CV", s)])
                    cp("dve", B2, SH, [("MCOL",)], [("CV", s)])
                else:
                    gi, bi = lnv[s]
                    tt_("dve", G2, COLV[:, :, gi], TMPC[:, 8:16], ALU.mult, [("COLV",), ("TMPC", s, 1)], [("CV", s)])
                    tt_("dve", TMPC[:, 32:40], COLV[:, :, bi], TMPC[:, 8:16], ALU.mult, [("COLV",), ("TMPC", s, 1)], [("TMPC", s, 4)])
                    tt_("dve", B2, TMPC[:, 32:40], SH, ALU.add, [("TMPC", s, 4), ("MCOL",)], [("CV", s)])
            nbmod[0] = 8
            bptr[0] = 0
            sch.flush()

        def load_bc(slot, vec, lane="bc"):
            dma("sp", "bc%d" % slot, BC[:, slot, :], vec.partition_broadcast(128),
                [("modscr", s_, n_) for s_ in range(4) for n_ in range(6)], [("BC", slot)])

        def to_hT(s, halo):
            for c in range(8):
                b = nb()
                for t in range(4):
                    mm(bank(b)[:, t * 128:(t + 1) * 128], XB[:, 1 + t, c * 128:(c + 1) * 128], IDB[:, :], True, True,
                       [("XB", 1 + t), ("IDB",)], [("ps", b)], signal=(t == 3))
                act(HT[:, c, HALO:TT], bank(b), AF.Identity, [("ps", b), ("CV", s)], [("HT", c)],
                    bias=CV[:, s * 16 + 8 + c:s * 16 + 9 + c], scale=CV[:, s * 16 + c:s * 16 + c + 1])
            if halo:
                b = nb()
                for c in range(8):
                    mm(bank(b)[:, c * 32:(c + 1) * 32], XB[0:32, 0, c * 128:(c + 1) * 128], IDB[0:32, 0:32], True, True,
                       [("XB", 0), ("IDB",)], [("ps", b)], signal=(c == 7))
                for c in range(8):
                    act(HT[:, c, 0:HALO], bank(b)[:, c * 32:(c + 1) * 32], AF.Identity, [("ps", b), ("CV", s)], [("HT", c)],
                        bias=CV[:, s * 16 + 8 + c:s * 16 + 9 + c], scale=CV[:, s * 16 + c:s * 16 + c + 1])

        def epilogue(t, pb, res_ap, res_key, ag, st_ap, st_key, znb, final=None):
            zb = t % 2
            P = PS[pb // 2][:, :]
            kz, kn = ("Z", zb), ("ZN", zb)
            pk = [("ps", pb), ("ps", pb + 1)]
            if ag is None:
                stt(Z[:, zb, :], res_ap, ALPHA, P, ALU.mult, ALU.add, [res_key] + pk, [kz])
            else:
                tt_("dve", Z[:, zb, :], res_ap, BC[:, ag, :], ALU.mult, [res_key, ("BC", ag)], [kz])
                tt_("dve", Z[:, zb, :], Z[:, zb, :], P, ALU.add, [kz] + pk, [kz])
            for hh in range(2):
                sch.op("dve", lambda e, hh=hh: e.bn_stats(out=STt[:, zb, hh * 6:hh * 6 + 6], in_=Z[:, zb, hh * 512:(hh + 1) * 512]),
                       [kz], [("ST", zb)])
            sch.op("dve", lambda e: e.bn_aggr(out=MV[:, zb, 0:2], in_=STt[:, zb, :]), [("ST", zb)], [("MV", zb)])
            ts_("dve", MV[:, zb, 1:2], MV[:, zb, 1:2], LN_EPS, None, ALU.add, None, [("MV", zb)], [("MV", zb)])
            tt_("pool", MV[:, zb, 2:3], MV[:, zb, 1:2], NHALF[:, 0:1], ALU.pow, [("MV", zb), ("NHALF",)], [("MV", zb)])
            stt(MV[:, zb, 3:4], MV[:, zb, 0:1], -1.0, MV[:, zb, 2:3], ALU.mult, ALU.mult, [("MV", zb)], [("MV", zb)])
            if znb:
                act(XB[:, 1 + t, :], Z[:, zb, :], AF.Identity, [kz, ("MV", zb)], [("XB", 1 + t)],
                    bias=MV[:, zb, 3:4], scale=MV[:, zb, 2:3])
            if st_ap is not None:
                act(st_ap, Z[:, zb, :], AF.Identity, [kz, ("MV", zb)], [st_key],
                    bias=MV[:, zb, 3:4], scale=MV[:, zb, 2:3])
            if final is None:
                return lambda: None
            gs, bs, out_ap, out_key = final
            act(ZN[:, zb, :], Z[:, zb, :], AF.Identity, [kz, ("MV", zb)], [kn],
                bias=MV[:, zb, 3:4], scale=MV[:, zb, 2:3])

            def tail():
                tt_("pool", out_ap, ZN[:, zb, :], BC[:, gs, :], ALU.mult, [kn, ("BC", gs)], [out_key])
                tt_("dve", out_ap, out_ap, BC[:, bs, :], ALU.add, [out_key, ("BC", bs)], [out_key])
            return tail

        def mlp(ws, s, w1n, w2n, brow):
            to_hT(s, False)
            for f in range(8):
                slot = ws.get((w1n, f))
                for fl in range(4):
                    fc = 4 * f + fl
                    b = nb()
                    for k in range(8):
                        mm(bank(b), WR[:, slot, k * 512 + fl * 128:k * 512 + fl * 128 + 128], HT[:, k, HALO:TT], k == 0, k == 7,
                           [("WR", slot), ("HT", k)], [("ps", b)])
                    rb = fc % 2
                    act(RT[:, rb, :], bank(b), AF.Relu, [("ps", b)], [("RT", rb)])
                    tt_("pool", HID[:, fc, :], RT[:, rb, :], RT[:, rb, :], ALU.mult, [("RT", rb)], [("HID", fc)])
                ws.release()
            for c in range(8):
                slot = ws.get((w2n, c))
                for t in range(4):
                    for half in range(2):
                        b = 2 * t + half
                        if c == 0:
                            mm(bank(b), ONEB[0:1, :], PBR[0:1, brow, half * 512:(half + 1) * 512], True, False,
                               [("ONEB",), ("PBR", brow)], [("ps", b)])
                        for fl in range(4):
                            mm(bank(b), HID[:, 4 * c + fl, t * 128:(t + 1) * 128],
                               WR[:, slot, fl * 1024 + half * 512:fl * 1024 + half * 512 + 512],
                               False, c == 7 and fl == 3,
                               [("HID", 4 * c + fl), ("WR", slot)], [("ps", b)], signal=(fl == 3))
                ws.release()
            bptr[0] = 0

        if doA:
            ast = ExitStack()
            with ast:
                U = sb("U", [128, 2, TT], BF16, ast)
                NDG = 8
                DG = sb("DG", [128, NDG, 128], BF16, ast)
                dgc = [0]
                SG = sb("SG", [128, 2, TT], F32, ast)
                ACC = sb("ACC", [128, 8, GR], F32, ast)
                SQ = sb("SQ", [128, 2, GR], F32, ast)
                UN = sb("UN", [128, 2, GR], F32, ast)
                VT = sb("VT", [128, 8, GR], BF16, ast)
                STB = sb("STB", [128, 4, GR], F32, ast)
                QS = sb("QS", [128, 2, GR], BF16, ast)
                VS = sb("VS", [128, 1, D], BF16, ast)
                load_bc(0, modscr[0, 2 * D:3 * D])
                load_bc(1, I["post_mix_g"][0])
                load_bc(3, modscr[1, 2 * D:3 * D])
                ts_("dve", BC[:, 1, :], BC[:, 1, :], ALPHA, None, ALU.mult, None, [("BC", 1)], [("BC", 1)])
                dma("sp", "pr0", Z[0:1, 0, :], I["conv_pw2_b"].rearrange("(o n) -> o n", o=1), [], [("Z", 0)])
                dma("sp", "pr1", Z[0:1, 1, :], I["post_mix_b"][0].rearrange("(o n) -> o n", o=1), [], [("Z", 1)])
                tt_("dve", PBR[0:1, 0, :], Z[0:1, 0, :], BC[0:1, 0, :], ALU.mult, [("Z", 0), ("BC", 0)], [("PBR", 0)])
                ts_("dve", PBR[0:1, 1, :], Z[0:1, 1, :], ALPHA, None, ALU.mult, None, [("Z", 1)], [("PBR", 1)])

                seqA = []
                for pp in range(2 * ng):
                    seqA += [("pw1", j) for j in range(4)] + [("pw2", n) for n in range(2)]
                    seqA += [("w1_0", f) for f in range(8)] + [("w2_0", c) for c in range(8)]
                    seqA += [("qkv", n) for n in range(0 if pp % 2 == 0 else 2, 6)]
                ws = WStream(seqA)

                def load_x(g):
                    dma("sp", "xl0", X[0:HALO, 0, :], I["xs"][g, 0:HALO, :], [], [("X", 0)])
                    dma("sp", "xl1", X[:, 1:5, :], I["xs"][g, HALO:TT, :].rearrange("(t p) d -> p t d", p=128),
                        [], [("X", 1), ("X", 2), ("X", 3), ("X", 4)])

                load_x(0)
                ws.prefetch()
                for pp in range(2 * ng):
                    g = pp // 2
                    own = (pp % 2 == 0)
                    kvl = 0 if own else 1
                    if pp == 0:
                        emit_cvts(4, nodep=True)
                    cp("dve", XB[0:HALO, 0, :], X[0:HALO, 0, :], [("X", 0)], [("XB", 0)])
                    for t in range(4):
                        cp("dve" if t % 2 else "act", XB[:, 1 + t, :], X[:, 1 + t, :], [("X", 1 + t)], [("XB", 1 + t)])
                    to_hT(0, True)
                    if pp == 0:
                        emit_cvts(4, nodep=True)
                    if debug and pp == 0:
                        dma("sp", "dbg", dbg_ht[:, :, :], HT[:, :, :], [("HT", c) for c in range(8)], [("dbg", 0)])
                        dma("sp", "dbg", dbg_cv[:, :], CV[:, :], [("CV", 0)], [("dbg", 4)])
                    nbmod[0] = 6
                    bptr[0] = 0
                    pw1_slot = {}
                    pw1_banks = {}

                    def pw1_glu(i):
                        j, s2 = i // 2, i % 2
                        if s2 == 0:
                            pw1_slot[j] = ws.get(("pw1", j))
                        slot = pw1_slot[j]
                        ub = i % 2
                        ba, bg, bh = nb(), nb(), nb()
                        for (bk, col0) in ((ba, 128 * s2), (bg, 256 + 128 * s2)):
                            for k in range(8):
                                mm(bank(bk), WR[:, slot, k * 512 + col0:k * 512 + col0 + 128], HT[:, k, HALO:TT], k == 0, k == 7,
                                   [("WR", slot), ("HT", k)], [("ps", bk)])
                        for hi, col0 in enumerate((128 * s2, 256 + 128 * s2)):
                            for k in range(8):
                                mm(bank(bh)[:, hi * 32:hi * 32 + 32], WR[:, slot, k * 512 + col0:k * 512 + col0 + 128], HT[:, k, 0:HALO],
                                   k == 0, k == 7, [("WR", slot), ("HT", k)], [("ps", bh)], signal=(k == 7 and hi == 1))
                        if s2 == 1:
                            ws.release()
                        pw1_banks[i] = (ba, bg, bh)

                    def glu_elem(i):
                        ub = i % 2
                        ba, bg, bh = pw1_banks[i]
                        act(SG[:, ub, HALO:TT], bank(bg), AF.Sigmoid, [("ps", bg), ("CA", 0), ("CA", 1), ("CA", 2), ("CA", 3), ("CA", 4), ("CAt",)], [("SG", ub)], bias=CA[:, 8 + i:9 + i])
                        act(SG[:, ub, 0:HALO], bank(bh)[:, 32:64], AF.Sigmoid, [("ps", bh), ("CA", 0), ("CA", 1), ("CA", 2), ("CA", 3), ("CA", 4), ("CAt",)], [("SG", ub)], bias=CA[:, 8 + i:9 + i])
                        stt(U[:, ub, HALO:TT], bank(ba), CA[:, i:i + 1], SG[:, ub, HALO:TT], ALU.add, ALU.mult,
                            [("ps", ba), ("SG", ub), ("CA", 0), ("CA", 1), ("CA", 2), ("CA", 3), ("CA", 4), ("CAt",)], [("U", ub)])
                        stt(U[:, ub, 0:HALO], bank(bh)[:, 0:32], CA[:, i:i + 1], SG[:, ub, 0:HALO], ALU.add, ALU.mult,
                            [("ps", bh), ("SG", ub), ("CA", 0), ("CA", 1), ("CA", 2), ("CA", 3), ("CA", 4), ("CAt",)], [("U", ub)])
                        ts_("dve", U[:, ub, 0:HALO], U[:, ub, 0:HALO], HVt[:, pp:pp + 1], None, ALU.mult, None,
                            [("U", ub), ("HV",)], [("U", ub)])

                    def dwconv(i):
                        ub = i % 2
                        bc_ = nb()
                        for jt in range(CONVW):
                            dgs = dgc[0] % NDG
                            dgc[0] += 1
                            if jt % 2 == 0:
                                ts_("dve", DG[:, dgs, :], IDB[:, :], CA[:, 40 + jt * 8 + i:41 + jt * 8 + i], None, ALU.mult, None,
                                    [("IDB",), ("CA", 0), ("CA", 1), ("CA", 2), ("CA", 3), ("CA", 4), ("CAt",)], [("DG", dgs)])
                            else:
                                act(DG[:, dgs, :], IDB[:, :], AF.Copy, [("IDB",), ("CA", 0), ("CA", 1), ("CA", 2), ("CA", 3), ("CA", 4), ("CAt",)], [("DG", dgs)],
                                    scale=CA[:, 40 + jt * 8 + i:41 + jt * 8 + i])
                            mm(bank(bc_), DG[:, dgs, :], U[:, ub, 2 + jt:2 + jt + GR], jt == 0, jt == CONVW - 1,
                               [("DG", dgs), ("U", ub)], [("ps", bc_)], signal=True)
                        act(ACC[:, i, :], bank(bc_), AF.Identity, [("ps", bc_), ("CA", 0), ("CA", 1), ("CA", 2), ("CA", 3), ("CA", 4), ("CAt",)], [("ACC", i)], bias=CA[:, 16 + i:17 + i])
                        act(SQ[:, ub, :], ACC[:, i, :], AF.Square, [("ACC", i)], [("SQ", ub)])
                        mm(bank(6), ONES[:, :], ACC[:, i, :], i == 0, i == 7, [("ONES",), ("ACC", i)], [("ps", 6)], signal=True)
                        mm(bank(7), ONES[:, :], SQ[:, ub, :], i == 0, i == 7, [("ONES",), ("SQ", ub)], [("ps", 7)], signal=True)

                    pw1_glu(0)
                    glu_elem(0)
                    for i in range(8):
                        if i + 1 < 8:
                            pw1_glu(i + 1)
                        dwconv(i)
                        if pp == 0 and i in (1, 4):
                            emit_cvts(4, nodep=True)
                        if i + 1 < 8:
                            glu_elem(i + 1)
                    bm, bq = 6, 7
                    nbmod[0] = 8
                    bptr[0] = 0
                    cp("dve", STB[:, 0, :], bank(bm), [("ps", bm)], [("STB", 0)])
                    tt_("dve", STB[:, 3, :], STB[:, 0, :], STB[:, 0, :], ALU.mult, [("STB", 0)], [("STB", 3)])
                    tt_("dve", STB[:, 1, :], bank(bq), STB[:, 3, :], ALU.subtract, [("ps", bq), ("STB", 3)], [("STB", 1)])
                    act(STB[:, 1, :], STB[:, 1, :], AF.Ln, [("STB", 1)], [("STB", 1)], bias=LN_EPS)
                    act(STB[:, 1, :], STB[:, 1, :], AF.Exp, [("STB", 1)], [("STB", 1)], scale=-0.5)
                    stt(STB[:, 2, :], STB[:, 0, :], -1.0, STB[:, 1, :], ALU.mult, ALU.mult, [("STB", 0), ("STB", 1)], [("STB", 2)])
                    for i in range(8):
                        ub = i % 2
                        tt_("dve", UN[:, ub, :], ACC[:, i, :], STB[:, 1, :], ALU.mult, [("ACC", i), ("STB", 1)], [("UN", ub)])
                        tt_("dve", UN[:, ub, :], UN[:, ub, :], STB[:, 2, :], ALU.add, [("UN", ub), ("STB", 2)], [("UN", ub)])
                        act(VT[:, i, :], UN[:, ub, :], AF.Silu, [("UN", ub), ("CA", 0), ("CA", 1), ("CA", 2), ("CA", 3), ("CA", 4), ("CAt",)], [("VT", i)],
                            bias=CA[:, 32 + i:33 + i], scale=CA[:, 24 + i:25 + i])
                    if debug and pp == 0:
                        dma("sp", "dbg", dbg_acc[:, :, :], ACC[:, :, :], [("ACC", c) for c in range(8)], [("dbg", 1)])
                        dma("sp", "dbg", dbg_vt[:, :, :], VT[:, :, :], [("VT", c) for c in range(8)], [("dbg", 2)])
                    s0 = ws.get(("pw2", 0))
                    s1 = ws.get(("pw2", 1))
                    for t in range(4):
                        pb = nb2()
                        for half, slot in ((0, s0), (1, s1)):
                            b = pb + half
                            mm(bank(b), ONEB[0:1, :], PBR[0:1, 0, half * 512:(half + 1) * 512], True, False,
                               [("ONEB",), ("PBR", 0)], [("ps", b)])
                            for k in range(8):
                                mm(bank(b), VT[:, k, t * 128:(t + 1) * 128], WR[:, slot, k * 512:(k + 1) * 512], False, k == 7,
                                   [("VT", k), ("WR", slot)], [("ps", b)])
                        if t == 3:
                            ws.release(2)
                        epilogue(t, pb, X[:, 1 + t, :], ("X", 1 + t), None, X[:, 1 + t, :], ("X", 1 + t), True)
                    if debug and pp == 0:
                        dma("sp", "dbg", dbg_x0[:, :, :], X[:, 1:5, :], [("X", 1 + t) for t in range(4)], [("dbg", 3)])
                    if pp == 0:
                        emit_cvts(8, nodep=True)
                    mlp(ws, 1, "w1_0", "w2_0", 1)
                    for t in range(4):
                        if own:
                            epilogue(t, 2 * t, X[:, 1 + t, :], ("X", 1 + t), 1, XO[:, t % 2, :], ("XO", t % 2), True)
                            dma("sp", "xo%d" % (t % 2), x1_scr[g, t * 128:(t + 1) * 128, :], XO[:, t % 2, :],
                                [("XO", t % 2)], [("x1", g, t)])
                        else:
                            epilogue(t, 2 * t, X[:, 1 + t, :], ("X", 1 + t), 1, None, None, True)
                    if pp + 1 < 2 * ng:
                        load_x(pp + 1)
                    emit_cvts(2)
                    to_hT(2, False)
                    for n in range(0 if own else 2, 4):
                        slot = ws.get(("qkv", n))
                        for cg in range(4):
                            b = nb()
                            for k in range(8):
                                mm(bank(b), WR[:, slot, k * 512 + cg * 128:k * 512 + cg * 128 + 128], HT[:, k, HALO:TT], k == 0, k == 7,
                                   [("WR", slot), ("HT", k)], [("ps", b)])
                            qb = (n * 4 + cg) % 2
                            if n < 2:
                                act(QS[:, qb, :], bank(b), AF.Copy, [("ps", b)], [("QS", qb)], scale=0.125)
                            else:
                                cp("dve", QS[:, qb, :], bank(b), [("ps", b)], [("QS", qb)])
                            r0 = ((n % 2) * 8 + cg * 2) * 64
                            if n < 2:
                                dst = qt_scr[r0:r0 + 128, g * GR:(g + 1) * GR]
                            else:
                                dst = kv_all[kvl * 2048 + r0:kvl * 2048 + r0 + 128, g * GR:(g + 1) * GR]
                            dma("sp", "qs%d" % qb, dst, QS[:, qb, :], [("QS", qb)],
                                [("qk", n, cg, g)] if n < 2 else [("kw", kvl, n, cg, g)])
                        ws.release()
                    sv0 = ws.get(("qkv", 4))
                    sv1 = ws.get(("qkv", 5))
                    Vv = kv_all[kvl * 2048 + 1024:kvl * 2048 + 2048, :].rearrange("r (a c) -> (r a) c", a=4)
                    for t in range(4):
                        pb = nb2()
                        for half, slot in ((0, sv0), (1, sv1)):
                            b = pb + half
                            for k in range(8):
                                mm(bank(b), HT[:, k, HALO + t * 128:HALO + (t + 1) * 128], WR[:, slot, k * 512:(k + 1) * 512], k == 0, k == 7,
                                   [("HT", k), ("WR", slot)], [("ps", b)])
                        if t == 3:
                            ws.release(2)
                        vb = 0
                        cp("act" if t % 2 else "dve", VS[:, vb, :], PS[pb // 2][:, :], [("ps", pb), ("ps", pb + 1)], [("VS", vb)])
                        dma("sp", "vs%d" % vb, Vv[g * GR + t * 128:g * GR + (t + 1) * 128, :], VS[:, vb, :], [("VS", vb)], [("vw", kvl, g, t)])
                emit_cvts()
                lanesA = ["xo0", "xo1", "qs0", "qs1", "vs0", "vs1"]
                sch.wait_all("sp", lanesA)
                sch.flush()

        if doB:
            bst = ExitStack()
            with bst:
                QT = sb("QT", [128, 2, 2, GR], BF16, bst)
                KT = sb("KT", [128, 3, 2, GR], BF16, bst)
                VA = sb("VA", [128, 3, 4, 132], BF16, bst)
                PT = sb("PT", [128, 2, 1024], BF16, bst)
                BA = sb("BA", [128, 2, 2304], BF16, bst)
                CH = sb("CH", [128, NH], F32, bst)
                LM = sb("LM", [128, 4, 64], F32, bst)
                LS = sb("LS", [128, 8], F32, bst)
                GSUB = sb("GSUB", [128, 128], F32, bst)
                OS = sb("OS", [128, 2, 128], F32, bst)
                SM = sb("SM", [128, 2, 8], F32, bst)
                OJ = sb("OJ", [128, 128], F32, bst)
                OAC = sb("OAC", [128, 1032], F32, bst)

                dma("sp", "pl6", CH[:, :], I["ch"], [], [("CH",)])
                for q, nm in enumerate(("attn_lam_q1", "attn_lam_k1", "attn_lam_q2", "attn_lam_k2")):
                    dma("sp", "pl7", LM[:, q, :], I[nm].partition_broadcast(128), [], [("LM", q)])
                dma("sp", "pl7", GSUB[:, :], I["attn_subln_g"].partition_broadcast(128), [], [("GSUB",)])
                ts_("dve", GSUB[:, :], GSUB[:, :], 1.0 - LAMBDA_INIT, None, ALU.mult, None, [("GSUB",)], [("GSUB",)])
                for q in range(2):
                    stt(LM[:, 2 * q, :], LM[:, 2 * q, :], 1.0, LM[:, 2 * q + 1, :], ALU.mult, ALU.mult,
                        [("LM", 2 * q), ("LM", 2 * q + 1)], [("LM", 2 * q)], accum=LS[:, q:q + 1])
                act(LS[:, 2:4], LS[:, 0:2], AF.Exp, [("LM", 0), ("LM", 2)], [("LS",)])
                tt_("dve", LS[:, 4:5], LS[:, 2:3], LS[:, 3:4], ALU.subtract, [("LS",)], [("LS",)])
                ts_("dve", LS[:, 5:6], LS[:, 4:5], LAMBDA_INIT, -1.0, ALU.add, ALU.mult, [("LS",)], [("LS",)])
                sch.op("dve", lambda e: e.memset(VA[:, :, :, 128:129], 1.0), [], [("VA", 0), ("VA", 1), ("VA", 2)])
                sch.op("dve", lambda e: e.memset(QT[64:128, :, :, :], 0.0), [], [("QT", 0), ("QT", 1)])
                sch.op("dve", lambda e: e.memset(KT[64:128, :, :, :], 0.0), [], [("KT", 0), ("KT", 1), ("KT", 2)])
                load_bc(0, modscr[2, 2 * D:3 * D])
                load_bc(1, I["post_mlp_g"][0])
                load_bc(2, I["post_mix_g"][1])
                load_bc(3, modscr[3, 2 * D:3 * D])
                load_bc(4, I["post_mlp_g"][1])
                load_bc(5, I["post_mlp_b"][1])
                ts_("dve", BC[:, 1, :], BC[:, 1, :], ALPHA, None, ALU.mult, None, [("BC", 1)], [("BC", 1)])
                ts_("dve", BC[:, 2, :], BC[:, 2, :], ALPHA, None, ALU.mult, None, [("BC", 2)], [("BC", 2)])
                dma("sp", "pr0", Z[0:1, 0, :], I["post_mlp_b"][0].rearrange("(o n) -> o n", o=1), [], [("Z", 0)])
                dma("sp", "pr1", Z[0:1, 1, :], I["post_mix_b"][1].rearrange("(o n) -> o n", o=1), [], [("Z", 1)])
                ts_("dve", PBR[0:1, 0, :], Z[0:1, 0, :], ALPHA, None, ALU.mult, None, [("Z", 0)], [("PBR", 0)])
                ts_("dve", PBR[0:1, 1, :], Z[0:1, 1, :], ALPHA, None, ALU.mult, None, [("Z", 1)], [("PBR", 1)])
                if not doA:
                    sch.op("dve", lambda e: e.memset(ONEB[:, :], 1.0), [], [("ONEB",)])

                seqB = []
                for g in range(ng):
                    seqB += [("wo", n) for n in range(2)] + [("w1_1", f) for f in range(8)] + [("w2_1", c) for c in range(8)]
                ws = WStream(seqB)

                Vall = [kv_all[r * 2048 + 1024:r * 2048 + 2048, :].rearrange("r (a c) -> (r a) c", a=4) for r in range(2)]

                def acc_ap(m, qs):
                    a = m * 4 + qs
                    bk, sl = a // 3, a % 3
                    if bk < 2:
                        return PS[2][:, bk * 512 + sl * 129:bk * 512 + sl * 129 + 129], ("ps", 4 + bk)
                    return PS[3][:, sl * 129:sl * 129 + 129], ("ps", 6)

                kvc = [0]
                hcount = [0]
                for j in range(ng):
                    dma("sp", "xl1", X[:, 1:5, :], x1_scr[j].rearrange("(t p) d -> p t d", p=128),
                        [("x1", j, t) for t in range(4)], [("X", 1), ("X", 2), ("X", 3), ("X", 4)])
                    for h in range(NH):
                        hb = hcount[0] % 2
                        hcount[0] += 1

                        def load_head(jj, hh, hbb):
                            dma("sp", "qt%d" % hbb, QT[0:64, hbb, :, :],
                                qt_scr[2 * hh * 64:2 * hh * 64 + 128, jj * GR:(jj + 1) * GR].rearrange("(m d) t -> d m t", m=2),
                                [("qk", n, cg, jj) for n in range(2) for cg in range(4)], [("QT", hbb)])
                            dma("pool", "ba%d" % hbb, BA[:, hbb, :], I["biasarr"][hh], [], [("BA", hbb)])

                        if j == 0 and h == 0:
                            load_head(0, 0, hb)
                        if h + 1 < NH:
                            load_head(j, h + 1, 1 - hb)
                        elif j + 1 < ng:
                            load_head(j + 1, 0, 1 - hb)
                        blocks = []
                        for i in range(j + 1):
                            for r in range(2):
                                for kb in range(4):
                                    sp_off = None
                                    if i == j:
                                        sp_off = (0 if r == 0 else 896) + 384 - kb * 128
                                    elif i == j - 1 and r == 1 and kb == 3:
                                        sp_off = 1792
                                    blocks.append((r, i, kb, sp_off))
                        kvslot = {}

                        def load_kv(r, i):
                            sl = kvc[0] % 3
                            kvc[0] += 1
                            kvslot[(r, i)] = sl
                            dma("sp", "kt%d" % sl, KT[0:64, sl, :, :],
                                kv_all[r * 2048 + 2 * h * 64:r * 2048 + 2 * h * 64 + 128, i * GR:(i + 1) * GR].rearrange("(m d) t -> d m t", m=2),
                                [], [("KT", sl)])
                            dma("sp", "va%d" % sl, VA[:, sl, :, 0:128],
                                Vall[r][i * GR:(i + 1) * GR, h * 128:(h + 1) * 128].rearrange("(kb p) e -> p kb e", p=128),
                                [], [("VA", sl)], nonc=True)

                        def qk(n):
                            r, i, kb, _ = blocks[n]
                            if (r, i) not in kvslot:
                                load_kv(r, i)
                            sl = kvslot[(r, i)]
                            sbuf_ = n % 2
                            spo = blocks[n][3]
                            for m in range(2):
                                b = 2 * sbuf_ + m
                                mm(bank(b), KT[:, sl, m, kb * 128:(kb + 1) * 128], QT[:, hb, m, :], True, spo is None,
                                   [("KT", sl), ("QT", hb)], [("ps", b)], signal=(m == 1 and spo is None))
                                if spo is not None:
                                    mm(bank(b), IDB[:, :], BA[:, hb, spo:spo + 512], False, True,
                                       [("IDB",), ("BA", hb)], [("ps", b)], signal=(m == 1))

                        qk(0)
                        nblk = len(blocks)
                        for n in range(nblk):
                            r, i, kb, sp_off = blocks[n]
                            if n + 1 < nblk:
                                qk(n + 1)
                            sbuf_ = n % 2
                            sl = kvslot[(r, i)]
                            b0 = 2 * sbuf_
                            if sp_off is not None:
                                act(PT[:, sbuf_, :], PS[sbuf_][:, :], AF.Exp, [("ps", b0), ("ps", b0 + 1)], [("PT", sbuf_)])
                            else:
                                act(PT[:, sbuf_, :], PS[sbuf_][:, :], AF.Exp, [("ps", b0), ("ps", b0 + 1), ("CH",)], [("PT", sbuf_)],
                                    bias=CH[:, h:h + 1])
                            for m in range(2):
                                for qs in range(4):
                                    ap_, key_ = acc_ap(m, qs)
                                    mm(ap_, PT[:, sbuf_, m * 512 + qs * 128:m * 512 + qs * 128 + 128], VA[:, sl, kb, 0:129],
                                       n == 0 and (m * 4 + qs) % 3 == 0, n == nblk - 1, [("PT", sbuf_), ("VA", sl)], [key_],
                                       signal=(m == 1 and qs == 3))
                        cp("dve", OAC[:, 0:387], PS[2][:, 0:387], [("ps", 4)], [("OAC", 0)])
                        cp("dve", OAC[:, 387:774], PS[2][:, 512:899], [("ps", 5)], [("OAC", 1)])
                        cp("dve", OAC[:, 774:1032], PS[3][:, 0:258], [("ps", 6)], [("OAC", 2)])
                        for qs in range(4):
                            ob = qs % 2
                            i1, i2 = qs, 4 + qs
                            a1, k1 = OAC[:, i1 * 129:(i1 + 1) * 129], ("OAC", i1 // 3)
                            a2, k2 = OAC[:, i2 * 129:(i2 + 1) * 129], ("OAC", i2 // 3)
                            ko, ks = ("OS", ob), ("SM", ob)
                            sch.op("dve", lambda e, a1=a1, ob=ob: e.reciprocal(out=SM[:, ob, 0:1], in_=a1[:, 128:129]), [k1], [ks])
                            sch.op("dve", lambda e, a2=a2, ob=ob: e.reciprocal(out=SM[:, ob, 1:2], in_=a2[:, 128:129]), [k2], [ks])
                            ts_("dve", SM[:, ob, 2:3], SM[:, ob, 1:2], LS[:, 5:6], None, ALU.mult, None, [ks, ("LS",)], [ks])
                            ts_("dve", OS[:, ob, :], a1[:, 0:128], SM[:, ob, 0:1], None, ALU.mult, None, [k1, ks], [ko])
                            stt(OS[:, ob, :], a2[:, 0:128], SM[:, ob, 2:3], OS[:, ob, :], ALU.mult, ALU.add, [k2, ks, ko], [ko])
                            stt(OJ[:, :], OS[:, ob, :], 1.0, OS[:, ob, :], ALU.mult, ALU.mult, [ko], [("OJ",)], accum=SM[:, ob, 3:4])
                            ts_("dve", SM[:, ob, 4:5], SM[:, ob, 3:4], 1.0 / 128.0, LN_EPS, ALU.mult, ALU.add, [("OJ",), ks], [ks])
                            tt_("pool", SM[:, ob, 5:6], SM[:, ob, 4:5], NHALF[:, 0:1], ALU.pow, [ks, ("NHALF",)], [ks])
                            stt(XB[:, 1 + qs, h * 128:(h + 1) * 128], OS[:, ob, :], SM[:, ob, 5:6], GSUB[:, :], ALU.mult, ALU.mult,
                                [ko, ks, ("GSUB",)], [("XB", 1 + qs)])
                    if debug and j == 0:
                        dma("sp", "dbg", dbg_at[:, :, :], XB[:, 1:5, :], [("XB", 1 + t) for t in range(4)], [("dbg", 5)])
                    bptr[0] = 7
                    for c in range(8):
                        b = 7
                        for t in range(4):
                            mm(bank(b)[:, t * 128:(t + 1) * 128], XB[:, 1 + t, c * 128:(c + 1) * 128], IDB[:, :], True, True,
                               [("XB", 1 + t), ("IDB",)], [("ps", b)], signal=(t == 3))
                        cp("act" if c % 2 else "dve", HT[:, c, HALO:TT], bank(b), [("ps", b)], [("HT", c)])
                    bptr[0] = 0
                    s0 = ws.get(("wo", 0))
                    s1 = ws.get(("wo", 1))
                    for t in range(4):
                        pb = nb2()
                        for half, slot in ((0, s0), (1, s1)):
                            b = pb + half
                            mm(bank(b), ONEB[0:1, :], PBR[0:1, 0, half * 512:(half + 1) * 512], True, False,
                               [("ONEB",), ("PBR", 0)], [("ps", b)])
                            for k in range(8):
                                mm(bank(b), HT[:, k, HALO + t * 128:HALO + (t + 1) * 128], WR[:, slot, k * 512:(k + 1) * 512], False, k == 7,
                                   [("HT", k), ("WR", slot)], [("ps", b)])
                        if t == 3:
                            ws.release(2)
                        epilogue(t, pb, X[:, 1 + t, :], ("X", 1 + t), 1, X[:, 1 + t, :], ("X", 1 + t), True)
                    if debug and j == 0:
                        dma("sp", "dbg", dbg_xq[:, :, :], X[:, 1:5, :], [("X", 1 + t) for t in range(4)], [("dbg", 6)])
                    mlp(ws, 3, "w1_1", "w2_1", 1)
                    tails = []
                    for t in range(4):
                        tl = epilogue(t, 2 * t, X[:, 1 + t, :], ("X", 1 + t), 2, None, None, False,
                                      final=(4, 5, XO[:, t % 2, :], ("XO", t % 2)))

                        def fin(t=t, tl=tl):
                            tl()
                            dma("sp", "xo%d" % (t % 2), out[j, t * 128:(t + 1) * 128, :], XO[:, t % 2, :],
                                [("XO", t % 2)], [("out", j, t)])
                        tails.append(fin)
                        if t > 0:
                            tails[t - 1]()
                    tails[3]()
                sch.wait_all("sp", ["xo0", "xo1"])
                sch.flush()
    return nc


def _t5_bucket(rel):
    n = np.maximum(rel, 0)
    nf = np.maximum(n, 1).astype(np.float32)
    large = 16 + (np.log(nf / np.float32(16)) / np.float32(math.log(128 / 16)) * np.float32(16)).astype(np.int32)
    large = np.minimum(large, 31)
    return np.where(n < 16, n, large)


def _bias_arrays(rel_bias, role):
    p = np.arange(128)[:, None]
    out = np.empty((NH, 128, 2304), np.float32)

    def fill(width, base_rel):
        j = np.arange(width)[None, :]
        rel = j - p + base_rel
        bk = _t5_bucket(rel)
        v = rel_bias[bk]
        v = np.where((rel >= 0)[:, :, None], v, np.float32(NEG))
        return np.transpose(v, (2, 0, 1))

    if role == 0:
        dA, dB, relC = 0, -512, 128
    else:
        dA, dB, relC = 0, 512, 1152
    out[:, :, 0:896] = fill(896, -384 + dA)
    out[:, :, 896:1792] = fill(896, -384 + dB)
    out[:, :, 1792:2304] = fill(512, relC)
    return out


_CACHE = {}


def _get(mode):
    if mode not in _CACHE:
        _CACHE[mode] = build(mode)
    return _CACHE[mode]


def _core_inputs(inputs, c):
    b, r = c // 2, c % 2
    x = inputs["x"]
    d = {}
    for name, shape in W_INPUTS:
        a = np.ascontiguousarray(inputs[name], dtype=np.float32).reshape(shape)
        d[name] = a
    d["cvec"] = np.ascontiguousarray(inputs["c"][b])
    d["ident"] = np.eye(128, dtype=np.float32)
    xs = np.zeros((2 * NG, TT, D), np.float32)
    hv = np.ones((128, 2 * NG), np.float32)
    for pp in range(2 * NG):
        i = pp // 2
        G = 2 * i + (r if pp % 2 == 0 else 1 - r)
        lo = G * GR - HALO
        if lo < 0:
            xs[pp, HALO:] = x[b, 0:GR]
            hv[:, pp] = 0.0
        else:
            xs[pp] = x[b, lo:lo + TT]
    d["xs"] = xs
    d["hv"] = hv
    d["biasarr"] = _bias_arrays(np.asarray(inputs["rel_bias"], np.float32), r)
    d["ch"] = np.ascontiguousarray(np.broadcast_to(np.asarray(inputs["rel_bias"], np.float32)[31][None, :], (128, NH)))
    return d


FUSED = True


def kernel(**inputs):
    inputs = {k: np.asarray(v) for k, v in inputs.items()}
    cores = list(range(8))
    per = [_core_inputs(inputs, c) for c in cores]
    if FUSED:
        nc = _get("F")
        keys = [t for t in per[0].keys()]
        res = run_bass_kernel_spmd(nc, per, core_ids=cores)
        outs = [r["out"] for r in res.results]
    else:
        ncA = _get("A")
        inA = [{k: v for k, v in p.items() if k not in ("biasarr", "ch")} for p in per]
        resA = run_bass_kernel_spmd(ncA, inA, core_ids=cores).results
        ncB = _get("B")
        inB = []
        for c in cores:
            p = {k: v for k, v in per[c].items() if k not in ("xs", "hv")}
            p["kv_all"] = resA[c]["kv_all"]
            p["x1_scr"] = resA[c]["x1_scr"]
            p["qt_scr"] = resA[c]["qt_scr"]
            inB.append(p)
        resB = run_bass_kernel_spmd(ncB, inB, core_ids=cores).results
        outs = [r["out"] for r in resB]
    y = np.empty((NB, SEQ, D), np.float32)
    for c in cores:
        b, r = c // 2, c % 2
        o = np.asarray(outs[c]).reshape(NG, GR, D)
        for i in range(NG):
            G = 2 * i + r
            y[b, G * GR:(G + 1) * GR] = o[i]
    return y
```
